# Optimizing a Trainium2 kernel written in Bass

```python
import math
import jax, jax.numpy as jnp
from jax import lax
import numpy as np

D_MODEL = 1024
BATCH = 4
SEQ = 8192
DEPTH = 2

N_A = DEPTH // 2
N_B = DEPTH - N_A

A_HEADS = 8
A_QK_DIM = D_MODEL // 2 // A_HEADS
A_V_DIM = D_MODEL // A_HEADS
A_QK_W = A_HEADS * A_QK_DIM
A_V_W = A_HEADS * A_V_DIM
A_IN_WIDTH = 2 * A_QK_W + A_V_W + 2 * A_HEADS + A_V_W
A_CHUNK = 128

B_HEADS = 8
B_Q_LORA = 384
B_KV_LORA = 256
B_NOPE = 128
B_ROPE = 64
B_V = 128
B_QBLOCK = 128
ROPE_THETA = 10000.0

D_FF = 4 * D_MODEL
EPS = 1e-6

kernel_name = "yoco_mlstm_mla_sandwich_adaln"


def rmsnorm(x, g):
    xf = x.astype(jnp.float32)
    y = xf * lax.rsqrt(jnp.mean(xf * xf, axis=-1, keepdims=True) + EPS)
    return (y * g.astype(jnp.float32)).astype(x.dtype)


def modulate(h, shift, scale):
    return h * (1 + scale[:, None, :]) + shift[:, None, :]


def rope_tables(positions):
    half = B_ROPE // 2
    inv = ROPE_THETA ** (-jnp.arange(half, dtype=jnp.float32) / half)
    ang = positions.astype(jnp.float32)[..., None] * inv
    return jnp.cos(ang), jnp.sin(ang)


def apply_rope(x, cos, sin):
    xf = x.astype(jnp.float32)
    x1, x2 = jnp.split(xf, 2, axis=-1)
    out = jnp.concatenate([x1 * cos - x2 * sin, x2 * cos + x1 * sin], axis=-1)
    return out.astype(x.dtype)


def mlstm_chunkwise(q, k, v, i_pre, f_pre):
    B_, H, S, dk = q.shape
    dv = v.shape[-1]
    L = A_CHUNK
    nc = S // L
    f32 = jnp.float32
    qf = q.astype(f32)
    kf = k.astype(f32) * (dk ** -0.5)
    vf = v.astype(f32)
    lf = jax.nn.log_sigmoid(f_pre.astype(f32))
    ig = i_pre.astype(f32)

    def chunks(a):
        return jnp.moveaxis(a.reshape(B_, H, nc, L, *a.shape[3:]), 2, 0)

    qc, kc, vc, ic = chunks(qf), chunks(kf), chunks(vf), chunks(ig)
    bc = jnp.cumsum(chunks(lf), axis=-1)
    tri = jnp.tril(jnp.ones((L, L), dtype=bool))

    def step(carry, xs):
        C, n, m = carry
        qb, kb, vb, ib, bb = xs
        log_d = bb[..., :, None] - bb[..., None, :] + ib[..., None, :]
        log_d = jnp.where(tri, log_d, -jnp.inf)
        log_inter = bb + m[..., None]
        m_t = jnp.maximum(log_inter, jnp.max(log_d, axis=-1))
        w_intra = jnp.exp(log_d - m_t[..., None])
        w_inter = jnp.exp(log_inter - m_t)
        s = jnp.einsum('bhtd,bhsd->bhts', qb, kb) * w_intra
        num = (w_inter[..., None] * jnp.einsum('bhtd,bhde->bhte', qb, C)
               + jnp.einsum('bhts,bhse->bhte', s, vb))
        den = w_inter * jnp.einsum('bhtd,bhd->bht', qb, n) + jnp.sum(s, axis=-1)
        h = num / jnp.maximum(jnp.abs(den), jnp.exp(-m_t))[..., None]
        b_last = bb[..., -1]
        log_w = b_last[..., None] - bb + ib
        m_new = jnp.maximum(b_last + m, jnp.max(log_w, axis=-1))
        w = jnp.exp(log_w - m_new[..., None])
        decay = jnp.exp(b_last + m - m_new)
        C_new = decay[..., None, None] * C + jnp.einsum('bhs,bhsd,bhse->bhde', w, kb, vb)
        n_new = decay[..., None] * n + jnp.einsum('bhs,bhsd->bhd', w, kb)
        return (C_new, n_new, m_new), h

    init = (jnp.zeros((B_, H, dk, dv), f32), jnp.zeros((B_, H, dk), f32), jnp.zeros((B_, H), f32))
    _, hc = lax.scan(step, init, (qc, kc, vc, ic, bc))
    return jnp.moveaxis(hc, 0, 2).reshape(B_, H, S, dv).astype(q.dtype)


def mlstm_mixer(h, w_in, gate_b, head_g, w_out):
    B_, S, _ = h.shape
    proj = h @ w_in
    splits = [A_QK_W, 2 * A_QK_W, 2 * A_QK_W + A_V_W, 2 * A_QK_W + A_V_W + A_HEADS,
              2 * A_QK_W + A_V_W + 2 * A_HEADS]
    q, k, v, ig, fg, og = jnp.split(proj, splits, axis=-1)
    heads = lambda a, d: a.reshape(B_, S, A_HEADS, d).transpose(0, 2, 1, 3)
    i_pre = (ig + gate_b[0]).transpose(0, 2, 1)
    f_pre = (fg + gate_b[1]).transpose(0, 2, 1)
    hh = mlstm_chunkwise(heads(q, A_QK_DIM), heads(k, A_QK_DIM), heads(v, A_V_DIM), i_pre, f_pre)
    hh = rmsnorm(hh.transpose(0, 2, 1, 3), head_g).reshape(B_, S, A_V_W)
    return (hh * jax.nn.sigmoid(og)) @ w_out


def mla_shared_kv(x, shift, scale, g_in, w_a, g_latent, w_b, cos, sin):
    B_, S, _ = x.shape
    h = modulate(rmsnorm(x, g_in), shift, scale)
    kv = h @ w_a
    c_kv, k_rope = jnp.split(kv, [B_KV_LORA], axis=-1)
    c_kv = rmsnorm(c_kv, g_latent)
    kvb = (c_kv @ w_b).reshape(B_, S, B_HEADS, B_NOPE + B_V)
    k_nope, v = jnp.split(kvb, [B_NOPE], axis=-1)
    k_rope = apply_rope(k_rope, cos, sin)
    return k_nope, k_rope, v


def causal_mla_attention(q_nope, q_rope, k_nope, k_rope, v):
    B_, S, H, _ = q_nope.shape
    nb = S // B_QBLOCK
    scale = (B_NOPE + B_ROPE) ** -0.5
    qn = jnp.moveaxis(q_nope.reshape(B_, nb, B_QBLOCK, H, B_NOPE), 1, 0)
    qr = jnp.moveaxis(q_rope.reshape(B_, nb, B_QBLOCK, H, B_ROPE), 1, 0)
    kpos = jnp.arange(S, dtype=jnp.int32)
    starts = jnp.arange(nb, dtype=jnp.int32) * B_QBLOCK

    def block(args):
        qn_b, qr_b, start = args
        s = (jnp.einsum('bqhd,bkhd->bhqk', qn_b, k_nope)
             + jnp.einsum('bqhd,bkd->bhqk', qr_b, k_rope)).astype(jnp.float32) * scale
        qpos = start + jnp.arange(B_QBLOCK, dtype=jnp.int32)
        s = jnp.where(kpos[None, :] <= qpos[:, None], s, -jnp.inf)
        p = jax.nn.softmax(s, axis=-1)
        return jnp.einsum('bhqk,bkhd->bqhd', p.astype(v.dtype), v)

    o = lax.map(block, (qn, qr, starts))
    return jnp.moveaxis(o, 0, 1).reshape(B_, S, H, B_V)


def mla_mixer(h, w_q_a, g_q_latent, w_q_b, w_out, k_nope, k_rope, v, cos, sin):
    B_, S, _ = h.shape
    cq = rmsnorm(h @ w_q_a, g_q_latent)
    q = (cq @ w_q_b).reshape(B_, S, B_HEADS, B_NOPE + B_ROPE)
    q_nope, q_rope = jnp.split(q, [B_NOPE], axis=-1)
    q_rope = apply_rope(q_rope, cos[:, :, None, :], sin[:, :, None, :])
    o = causal_mla_attention(q_nope, q_rope, k_nope, k_rope, v)
    return o.reshape(B_, S, B_HEADS * B_V) @ w_out


def sqrelu_mlp(h, w1, w2):
    return jnp.square(jax.nn.relu(h @ w1)) @ w2


def setup_inputs(seed: int = 0) -> dict:
    key = jax.random.key(seed)
    ks = jax.random.split(key, 24)
    f32 = jnp.float32

    def nrm(k, shape, fan_in, s=1.0):
        return s * (fan_in ** -0.5) * jax.random.normal(k, shape, f32)

    def gain(k, shape):
        return 1.0 + 0.02 * jax.random.normal(k, shape, f32)

    x = jax.random.normal(ks[0], (BATCH, SEQ, D_MODEL), f32)
    c = jax.random.normal(ks[1], (BATCH, D_MODEL), f32)
    positions = (jax.random.randint(ks[2], (BATCH, 1), 0, 4096, dtype=jnp.int32)
                 + jnp.arange(SEQ, dtype=jnp.int32)[None, :])
    ada_w = nrm(ks[3], (DEPTH, D_MODEL, 6 * D_MODEL), D_MODEL, 0.5)
    ada_b = 0.02 * jax.random.normal(ks[4], (DEPTH, 6 * D_MODEL), f32)
    norm_g = gain(ks[5], (DEPTH, 4, D_MODEL))
    a_w_in = nrm(ks[6], (N_A, D_MODEL, A_IN_WIDTH), D_MODEL)
    kg1, kg2 = jax.random.split(ks[7])
    i_bias = 0.1 * jax.random.normal(kg1, (N_A, A_HEADS), f32)
    f_bias = jnp.linspace(3.0, 6.0, A_HEADS, dtype=f32)[None, :] + 0.1 * jax.random.normal(kg2, (N_A, A_HEADS), f32)
    a_gate_b = jnp.stack([i_bias, f_bias], axis=1)
    a_head_g = gain(ks[8], (N_A, A_HEADS, A_V_DIM))
    a_w_out = nrm(ks[9], (N_A, A_V_W, D_MODEL), A_V_W)
    kv_ada_w = nrm(ks[10], (D_MODEL, 2 * D_MODEL), D_MODEL, 0.5)
    kv_ada_b = 0.02 * jax.random.normal(ks[11], (2 * D_MODEL,), f32)
    kv_norm_g = gain(ks[12], (D_MODEL,))
    kv_w_a = nrm(ks[13], (D_MODEL, B_KV_LORA + B_ROPE), D_MODEL)
    kv_latent_g = gain(ks[14], (B_KV_LORA,))
    kv_w_b = nrm(ks[15], (B_KV_LORA, B_HEADS * (B_NOPE + B_V)), B_KV_LORA)
    b_w_q_a = nrm(ks[16], (N_B, D_MODEL, B_Q_LORA), D_MODEL)
    b_q_latent_g = gain(ks[17], (N_B, B_Q_LORA))
    b_w_q_b = nrm(ks[18], (N_B, B_Q_LORA, B_HEADS * (B_NOPE + B_ROPE)), B_Q_LORA)
    b_w_out = nrm(ks[19], (N_B, B_HEADS * B_V, D_MODEL), B_HEADS * B_V)
    mlp_w1 = nrm(ks[20], (DEPTH, D_MODEL, D_FF), D_MODEL)
    mlp_w2 = nrm(ks[21], (DEPTH, D_FF, D_MODEL), D_FF)
    return {"x": x, "c": c, "positions": positions, "ada_w": ada_w, "ada_b": ada_b,
            "norm_g": norm_g, "a_w_in": a_w_in, "a_gate_b": a_gate_b, "a_head_g": a_head_g,
            "a_w_out": a_w_out, "kv_ada_w": kv_ada_w, "kv_ada_b": kv_ada_b, "kv_norm_g": kv_norm_g,
            "kv_w_a": kv_w_a, "kv_latent_g": kv_latent_g, "kv_w_b": kv_w_b, "b_w_q_a": b_w_q_a,
            "b_q_latent_g": b_q_latent_g, "b_w_q_b": b_w_q_b, "b_w_out": b_w_out,
            "mlp_w1": mlp_w1, "mlp_w2": mlp_w2}


def reference(x, c, positions, ada_w, ada_b, norm_g, a_w_in, a_gate_b, a_head_g, a_w_out,
              kv_ada_w, kv_ada_b, kv_norm_g, kv_w_a, kv_latent_g, kv_w_b, b_w_q_a, b_q_latent_g,
              b_w_q_b, b_w_out, mlp_w1, mlp_w2):
    cond = jax.nn.silu(c)
    cos, sin = rope_tables(positions)
    shared_kv = None
    for l in range(DEPTH):
        ada = cond @ ada_w[l] + ada_b[l]
        sh1, sc1, g1, sh2, sc2, g2 = jnp.split(ada, 6, axis=-1)
        h = modulate(rmsnorm(x, norm_g[l, 0]), sh1, sc1)
        if l < N_A:
            y = mlstm_mixer(h, a_w_in[l], a_gate_b[l], a_head_g[l], a_w_out[l])
        else:
            if shared_kv is None:
                kv_shift, kv_scale = jnp.split(cond @ kv_ada_w + kv_ada_b, 2, axis=-1)
                shared_kv = mla_shared_kv(x, kv_shift, kv_scale, kv_norm_g, kv_w_a, kv_latent_g,
                                          kv_w_b, cos, sin)
            k_nope, k_rope, v = shared_kv
            j = l - N_A
            y = mla_mixer(h, b_w_q_a[j], b_q_latent_g[j], b_w_q_b[j], b_w_out[j],
                          k_nope, k_rope, v, cos, sin)
        x = x + g1[:, None, :] * rmsnorm(y, norm_g[l, 1])
        h = modulate(rmsnorm(x, norm_g[l, 2]), sh2, sc2)
        y = sqrelu_mlp(h, mlp_w1[l], mlp_w2[l])
        x = x + g2[:, None, :] * rmsnorm(y, norm_g[l, 3])
    return x
```

```python
import numpy as np
from contextlib import ExitStack
import concourse.bass as bass
import concourse.mybir as mybir
from concourse.bass_utils import run_bass_kernel_spmd

F32 = mybir.dt.float32
BF16 = mybir.dt.bfloat16
I32 = mybir.dt.int32
AF = mybir.ActivationFunctionType
ALU = mybir.AluOpType
AX = mybir.AxisListType

D = 1024
SEQ = 8192
T = 512
NT = 16
FIRST_OWN = 8
EPS = 1e-6
H = 8
SAME_ENG_SYNC = True
TWO_PI = 6.283185307179586
PI = 3.141592653589793
ATT_SCALE = 192.0 ** -0.5

ENGS = ("pe", "act", "dve", "pool", "sp")


class Buf:
    __slots__ = ("name", "w", "r", "dsem", "dcnt")

    def __init__(self, name):
        self.name = name
        self.w = None
        self.r = {}
        self.dsem = None
        self.dcnt = 0


class TT:
    def __init__(self, ap, buf):
        self.ap = ap
        self.b = buf

    def __getitem__(self, k):
        return self.ap[k]


class Prog:
    def __init__(self, nc, es):
        self.nc = nc
        self.es = es
        self.streams = {e: [] for e in ENGS}
        self.esem = {e: es.enter_context(nc.semaphore("es_" + e)) for e in ENGS}
        self.ecnt = {e: 0 for e in ENGS}
        self.waited = {e: {} for e in ENGS}
        self.semh = {}
        for e in ENGS:
            self.semh["es_" + e] = self.esem[e]
        self.nbuf = 0
        self.psb = []
        self.rot = []
        self.rot_i = 0

    def sb(self, name, shape, dt):
        t = self.es.enter_context(self.nc.sbuf_tensor("s_" + name, list(shape), dt))
        return TT(t, Buf(name))

    def view(self, ap, name):
        return TT(ap, Buf(name))

    def dsem_of(self, buf):
        if buf.dsem is None:
            nm = "ds%d" % len(self.semh)
            buf.dsem = nm
            self.semh[nm] = self.es.enter_context(self.nc.semaphore(nm))
        return buf.dsem

    def _collect(self, reads, writes, eng=None):
        deps = {}
        own = None if eng is None else "es_" + eng

        def add(tok, raw):
            if tok is None:
                return
            s, v = tok
            if s == own and not raw:
                return
            if deps.get(s, 0) < v:
                deps[s] = v

        for b in reads:
            add(b.w, True)
        for b in writes:
            add(b.w, False)
            for s, v in b.r.items():
                add((s, v), False)
        return deps

    def _emit_waits(self, eng, deps):
        own = "es_" + eng
        for s, v in deps.items():
            if s == own and (eng in ("pe", "sp") or not SAME_ENG_SYNC):
                continue
            if self.waited[eng].get(s, 0) >= v:
                continue
            self.waited[eng][s] = v
            self.streams[eng].append(("wait", s, v))

    @staticmethod
    def _bufs(lst):
        return [x.b if isinstance(x, TT) else x for x in lst]

    def op(self, eng, fn, reads=(), writes=()):
        reads = self._bufs(reads)
        writes = self._bufs(writes)
        if eng != "pe":
            writes = writes + [b for b in reads if b.name.startswith("ps") and b not in writes]
        self._emit_waits(eng, self._collect(reads, writes, eng))
        self.ecnt[eng] += 1
        tok = ("es_" + eng, self.ecnt[eng])
        self.streams[eng].append(("ins", fn, tok[0], 1))
        for b in reads:
            if b.r.get(tok[0], 0) < tok[1]:
                b.r[tok[0]] = tok[1]
        for b in writes:
            b.w = tok
            b.r = {}

    def dma(self, q, out_ap, in_ap, reads, target):
        reads = self._bufs(reads)
        tb = target.b if isinstance(target, TT) else target
        self._emit_waits(q, self._collect(reads, [tb]))
        s = self.dsem_of(tb)
        tb.dcnt += 16
        tok = (s, tb.dcnt)
        self.streams[q].append(("ins", lambda e: e.dma_start(out=out_ap, in_=in_ap), s, 16))
        for b in reads:
            if b.r.get(s, 0) < tok[1]:
                b.r[s] = tok[1]
        tb.w = tok
        tb.r = {}

    def collective(self, fn, reads, target):
        reads = self._bufs(reads)
        tb = target.b if isinstance(target, TT) else target
        self._emit_waits("pool", self._collect(reads, [tb]))
        s = self.dsem_of(tb)
        tb.dcnt += 1
        tok = (s, tb.dcnt)
        self.streams["pool"].append(("ins", fn, s, 1))
        for b in reads:
            if b.r.get(s, 0) < tok[1]:
                b.r[s] = tok[1]
        tb.w = tok
        tb.r = {}

    def carry(self, frm, to):
        m = {}
        for b in self._bufs(frm):
            if b.w is not None and m.get(b.w[0], 0) < b.w[1]:
                m[b.w[0]] = b.w[1]
            for s, v in b.r.items():
                if m.get(s, 0) < v:
                    m[s] = v
        for b in self._bufs(to):
            for s, v in m.items():
                if b.r.get(s, 0) < v:
                    b.r[s] = v
            if b.w is not None:
                pass

    def final_wait(self, eng, bufs):
        self._emit_waits(eng, self._collect(self._bufs(bufs), []))

    def init_psum(self):
        for i in range(8):
            t = self.es.enter_context(self.nc.psum_tensor("ps%d" % i, [128, 512], F32))
            self.psb.append(TT(t, Buf("ps%d" % i)))
        self.rot = list(range(8))

    def ps(self):
        i = self.rot[self.rot_i % len(self.rot)]
        self.rot_i += 1
        return self.psb[i]

    def acquire(self, k):
        got = self.rot[-k:]
        self.rot = self.rot[:-k]
        return [self.psb[i] for i in got], got

    def release(self, got):
        self.rot = self.rot + list(got)

    def replay(self, eng, handle):
        for it in self.streams[eng]:
            if it[0] == "wait":
                handle.wait_ge(self.semh[it[1]], it[2])
            else:
                ins = it[1](handle)
                ins.then_inc(self.semh[it[2]], it[3])


class _Stop(Exception):
    pass


def build_program(ntiles=NT, stage=None):
    nc = bass.Bass("TRN2", target_bir_lowering=False)
    es = ExitStack()
    with es:
        _build(nc, es, ntiles, stage)
    return nc


def _build(nc, es, ntiles, stage=None):
    P = Prog(nc, es)

    def din(name, shape, dt=F32):
        return nc.dram_tensor(name, list(shape), dt, kind="ExternalInput").ap()

    def dscr(name, shape, dt=BF16):
        return TT(nc.dram_tensor(name, list(shape), dt, kind="Internal").ap(), Buf(name))

    xs = din("xs", [SEQ, D])
    pos = din("pos", [1, SEQ], I32)
    flag_d = din("flag", [128, 1])
    cT_d = din("cT", [128, 8])
    ident_d = din("ident", [128, 128])
    mask_d = din("mask01", [128, 128])
    inv_d = din("inv_col", [64, 1])
    ada_w = din("ada_w", [2, D, 6 * D])
    ada_bA = din("ada_bA", [2, 4, 128, 8])
    ada_bG = din("ada_bG", [2, 2, 1, D])
    normA = din("normA", [2, 2, 128, 8])
    normG = din("normG", [2, 2, 1, D])
    kv_ada_w = din("kv_ada_w", [D, 2 * D])
    kv_ada_bA = din("kv_ada_bA", [2, 128, 8])
    kv_normA = din("kv_normA", [128, 8])
    a_w_in = din("a_w_in", [D, 3088])
    gate_b = din("gate_b", [2, 1, 8])
    head_g = din("head_g", [1, D])
    a_w_out = din("a_w_out", [D, D])
    kv_w_a = din("kv_w_a", [D, 320])
    kv_lat_g = din("kv_lat_g", [1, 256])
    kv_w_b = din("kv_w_b", [256, 2048])
    w_q_a = din("w_q_a", [D, 384])
    q_lat_g = din("q_lat_g", [1, 384])
    w_q_b = din("w_q_b", [384, 1536])
    b_w_out = din("b_w_out", [D, D])
    mlp_w1 = din("mlp_w1", [2, D, 4 * D])
    mlp_w2 = din("mlp_w2", [2, 4 * D, D])
    yout = nc.dram_tensor("y", [SEQ // 2, D], F32, kind="ExternalOutput").ap()
    yout_b = Buf("yout")

    wb_in = dscr("wb_in", [D, 3088])
    wb_out = dscr("wb_out", [D, D])
    wb_w1 = [dscr("wb_w1_%d" % l, [D, 4 * D]) for l in range(2)]
    wb_w2 = [dscr("wb_w2_%d" % l, [4 * D, D]) for l in range(2)]
    wb_a2 = dscr("wb_a2", [D, 384])
    wb_b = dscr("wb_b", [256, 2048])
    wb_qa = dscr("wb_qa", [D, 384])
    wb_qb2 = dscr("wb_qb2", [384, 8, 256])
    wb_bo = dscr("wb_bo", [D, D])
    kcache = dscr("kcache", [H, 128, SEQ])
    vcache = dscr("vcache", [H, 128, SEQ // 128, 128])
    xmid = dscr("xmid", [SEQ // 2, D], F32)
    lat_src = [dscr("lat_src%d" % i, [128, 1536]) for i in range(NT - FIRST_OWN)]
    lat_all = [dscr("lat_all%d" % i, [256, 1536]) for i in range(NT - FIRST_OWN)]

    ident_f = P.sb("ident_f", [128, 128], F32)
    ident_b = P.sb("ident_b", [128, 128], BF16)
    mask_f = P.sb("mask_f", [128, 128], F32)
    mask_b = P.sb("mask_b", [128, 128], BF16)
    ones_f = P.sb("ones_f", [128, 128], F32)
    ones_b = P.sb("ones_b", [128, 128], BF16)
    fones_b = P.sb("fones_b", [128, 128], BF16)
    flag = P.sb("flag", [128, 1], F32)
    cond = P.sb("cond", [128, 8], F32)
    inv_col = P.sb("inv_col", [64, 1], F32)
    vecA = P.sb("vecA", [128, 16, 8], F32)
    biasA = P.sb("biasA", [128, 10, 8], F32)
    nrmA = P.sb("nrmA", [128, 5, 8], F32)
    modA = P.sb("modA", [128, 5, 8], F32)
    modB = P.sb("modB", [128, 5, 8], F32)
    G = P.sb("G", [128, 4, D], F32)
    headg = P.sb("headg", [128, D], F32)
    latg = P.sb("latg", [128, 256], F32)
    qlatg = P.sb("qlatg", [128, 384], F32)
    gb = P.sb("gb", [128, 2, 8], F32)
    krT = P.sb("krT", [65, SEQ], BF16)
    WIF = P.sb("WIF", [128, 8, 16], BF16)
    Cf = P.sb("Cf", [128, 4, 130], F32)
    Cb = P.sb("Cb", [128, 4, 130], BF16)
    mst = P.sb("mst", [8, 1], F32)
    eps_t = P.sb("eps_t", [128, 1], F32)

    X = P.sb("X", [128, 4, D], F32)
    XN = P.sb("XN", [128, 4, D], BF16)
    HT = P.sb("HT", [128, 8, T], BF16)
    Ys = [P.sb("Y%d" % i, [128, D], F32) for i in range(2)]
    junk = P.sb("junk", [128, D], BF16)
    st4 = P.sb("st4", [128, 16], F32)
    ARENA_E = 31 * 1024
    arena = es.enter_context(nc.sbuf_tensor("arena", [128, ARENA_E], BF16))
    NWS = 4
    wslots = [P.sb("wslot%d" % i, [128, 4096], BF16) for i in range(NWS)]
    NKS = 2
    kslots = [P.sb("kslot%d" % i, [128, 4096], BF16) for i in range(NKS)]
    ANG = P.sb("ANG", [64, T], F32)
    KF = P.sb("KF", [64, T], F32)
    COS = P.sb("COS", [64, T], F32)
    SIN = P.sb("SIN", [64, T], F32)
    P.init_psum()

    state = {"ws": 0, "ks": 0, "aoff": 0}

    def aview(nelem_bf16, shape, dt, name):
        o = state["aoff"]
        assert o + nelem_bf16 <= ARENA_E, (name, o, nelem_bf16)
        state["aoff"] = o + nelem_bf16
        ap = arena[:, o:o + nelem_bf16]
        if dt == F32:
            ap = ap.bitcast(F32)
        if len(shape) == 3:
            ap = ap.rearrange("p (a b) -> p a b", a=shape[1])
        elif len(shape) == 4:
            ap = ap.rearrange("p (a b c) -> p a b c", a=shape[1], b=shape[2])
        return TT(ap, Buf(name))

    def wload(src_ap, a, bcols, src_buf):
        sl = wslots[state["ws"] % NWS]
        state["ws"] += 1
        v = sl.ap[:, 0:a * bcols].rearrange("p (a b) -> p a b", a=a)
        P.dma("sp", v, src_ap, [src_buf], sl)
        return TT(v, sl.b)

    def mm(psT, out_ap, lhsT, rhs, start, stop, reads):
        P.op("pe", lambda e: e.matmul(out_ap, lhsT=lhsT, rhs=rhs, start=start, stop=stop), reads=reads, writes=[psT])

    def trn(psT, out_ap, in_ap, idn, reads):
        P.op("pe", lambda e: e.transpose(out_ap, in_ap, idn.ap[:]), reads=list(reads) + [idn], writes=[psT])

    def act(out_ap, in_ap, func, reads, writes, scale=1.0, bias=None, accum=None):
        kw = {}
        if bias is not None:
            kw["bias"] = bias
        if accum is not None:
            kw["accum_out"] = accum
        P.op("act", lambda e: e.activation(out=out_ap, in_=in_ap, func=func, scale=scale, **kw), reads=reads, writes=writes)

    def tcopy(eng, out_ap, in_ap, reads, writes):
        P.op(eng, lambda e: e.tensor_copy(out=out_ap, in_=in_ap), reads=reads, writes=writes)

    def tt(eng, out_ap, in0, in1, op, reads, writes):
        P.op(eng, lambda e: e.tensor_tensor(out=out_ap, in0=in0, in1=in1, op=op), reads=reads, writes=writes)

    def ts(eng, out_ap, in0, s1, s2, op0, op1, reads, writes):
        if s2 is None:
            P.op(eng, lambda e: e.tensor_scalar(out=out_ap, in0=in0, scalar1=s1, scalar2=None, op0=op0), reads=reads, writes=writes)
        else:
            P.op(eng, lambda e: e.tensor_scalar(out=out_ap, in0=in0, scalar1=s1, scalar2=s2, op0=op0, op1=op1), reads=reads, writes=writes)

    def stt(eng, out_ap, in0, scalar, in1, op0, op1, reads, writes):
        P.op(eng, lambda e: e.scalar_tensor_tensor(out=out_ap, in0=in0, scalar=scalar, in1=in1, op0=op0, op1=op1), reads=reads, writes=writes)

    def memset(eng, ap, val, writes):
        P.op(eng, lambda e: e.memset(ap, val), reads=[], writes=writes)

    def rstd_from_ss(out_ap, ss_ap, n, rd, wr):
        act(out_ap, ss_ap, AF.Sqrt, rd + [eps_t], wr, scale=1.0 / n, bias=eps_t.ap[0:ss_ap.shape[0], 0:1])
        P.op("dve", lambda e: e.reciprocal(out=out_ap, in_=out_ap), reads=wr, writes=wr)

    def chk(n):
        if stage is not None and stage == n:
            raise _Stop()

    def body():
        P.dma("sp", ident_f.ap[:], ident_d[:, :], [], ident_f)
        P.dma("sp", mask_f.ap[:], mask_d[:, :], [], mask_f)
        P.dma("sp", flag.ap[:], flag_d[:, :], [], flag)
        P.dma("sp", cond.ap[:], cT_d[:, :], [], cond)
        P.dma("sp", inv_col.ap[:], inv_d[:, :], [], inv_col)
        P.dma("sp", biasA.ap[:, 0:8, :], ada_bA.rearrange("l v p k -> p (l v) k"), [], biasA)
        P.dma("sp", biasA.ap[:, 8:10, :], kv_ada_bA.rearrange("v p k -> p v k"), [], biasA)
        P.dma("sp", nrmA.ap[:, 0:4, :], normA.rearrange("l v p k -> p (l v) k"), [], nrmA)
        P.dma("sp", nrmA.ap[:, 4, :], kv_normA[:, :], [], nrmA)
        P.dma("sp", headg.ap[:], head_g[0:1, :].partition_broadcast(128), [], headg)
        P.dma("sp", latg.ap[:], kv_lat_g[0:1, :].partition_broadcast(128), [], latg)
        P.dma("sp", qlatg.ap[:], q_lat_g[0:1, :].partition_broadcast(128), [], qlatg)
        for i in range(2):
            P.dma("sp", gb.ap[:, i, :], gate_b[i, 0:1, :].partition_broadcast(128), [], gb)
        tcopy("dve", ident_b.ap[:], ident_f.ap[:], [ident_f], [ident_b])
        tcopy("dve", mask_b.ap[:], mask_f.ap[:], [mask_f], [mask_b])
        memset("dve", ones_f.ap[:], 1.0, [ones_f])
        memset("dve", ones_b.ap[:], 1.0, [ones_b])
        memset("dve", eps_t.ap[:], EPS, [eps_t])
        ts("dve", fones_b.ap[:], ones_f.ap[:], flag.ap[:, 0:1], None, ALU.mult, None, [ones_f, flag], [fones_b])
        memset("pool", krT.ap[64:65, :], 1.0, [krT])
        memset("pool", Cf.ap[:], 0.0, [Cf])
        memset("pool", mst.ap[:], 0.0, [mst])
        act(cond.ap[:], cond.ap[:], AF.Silu, [cond], [cond])

        def cast_rows(dst, src, nrows, step=256):
            for r0 in range(0, nrows, step):
                r1 = min(nrows, r0 + step)
                P.dma("pool", dst.ap[r0:r1], src[r0:r1], [], dst)

        cast_rows(wb_in, a_w_in, D)
        P.dma("pool", wb_a2.ap[:, 0:320], kv_w_a[:, :], [], wb_a2)
        P.dma("pool", wb_qb2.ap[:, :, 0:192], w_q_b.rearrange("r (h c) -> r h c", h=8), [], wb_qb2)
        cast_rows(wb_out, a_w_out, D)
        cast_rows(wb_w1[0], mlp_w1[0], D)
        cast_rows(wb_w2[0], mlp_w2[0], 4 * D)
        cast_rows(wb_b, kv_w_b, 256)
        cast_rows(wb_qa, w_q_a, D)
        cast_rows(wb_bo, b_w_out, D)
        cast_rows(wb_w1[1], mlp_w1[1], D)
        cast_rows(wb_w2[1], mlp_w2[1], 4 * D)

        state["aoff"] = 0
        cond_rep = aview(8 * 128 * 2, [128, 8, 128], F32, "cond_rep")
        aslab = [aview(8 * 512 * 2, [128, 8, 512], F32, "aslab%d" % i) for i in range(2)]
        brow = aview(512 * 2, [128, 512], F32, "brow")
        nrow = aview(512 * 2, [128, 512], F32, "nrow")
        rtmp = aview(8 * 64 * 2, [128, 8, 64], F32, "rtmp")
        rtmpb = aview(8 * 64, [128, 8, 64], BF16, "rtmpb")
        rq = aview(3 * 512 * 2, [128, 3, 8, 64], F32, "rq")
        rqb = aview(3 * 512, [128, 3, 8, 64], BF16, "rqb")
        prol_bufs = [cond_rep, brow, nrow, rtmp, rtmpb, rq, rqb] + aslab

        for k in range(8):
            tcopy("dve", cond_rep.ap[:, k, :], cond.ap[:, k:k + 1].to_broadcast([128, 128]), [cond], [cond_rep])

        P.dma("sp", rtmp.ap[:], kv_w_a[:, 256:320].rearrange("(k p) c -> p k c", p=128), [], rtmp)
        ts("dve", rtmpb.ap[:, :, 0:32], rtmp.ap[:, :, 32:64], -1.0, None, ALU.mult, None, [rtmp], [rtmpb])
        tcopy("dve", rtmpb.ap[:, :, 32:64], rtmp.ap[:, :, 0:32], [rtmp], [rtmpb])
        P.dma("act", wb_a2.ap[:, 320:384].rearrange("(k p) c -> p k c", p=128), rtmpb.ap[:], [rtmpb], wb_a2)
        for c3 in range(3):
            P.dma("sp", rq.ap[:, c3], w_q_b[c3 * 128:(c3 + 1) * 128, :].rearrange("p (h c) -> p h c", h=8)[:, :, 128:192], [], rq)
        ts("dve", rqb.ap[:, :, :, 0:32], rq.ap[:, :, :, 32:64], -1.0, None, ALU.mult, None, [rq], [rqb])
        tcopy("dve", rqb.ap[:, :, :, 32:64], rq.ap[:, :, :, 0:32], [rq], [rqb])
        for c3 in range(3):
            P.dma("act", wb_qb2.ap[c3 * 128:(c3 + 1) * 128, :, 192:256], rqb.ap[:, c3], [rqb], wb_qb2)

        chk(0)
        def ada_group(wsrc, col0, kind, idx, li, sidx):
            ab = state["abuf"]
            sl = ab["slabs"][state["as"] % len(ab["slabs"])]
            brow, nrow = ab["brow"], ab["nrow"]
            state["as"] += 1
            P.dma("sp", sl.ap[:], wsrc[:, col0:col0 + 512].rearrange("(k p) n -> p k n", p=128), [], sl)
            half = (col0 % 1024) // 512
            pt = P.ps()
            if kind == "A":
                for j in range(4):
                    for k in range(8):
                        mm(pt, pt.ap[:, j:j + 1], sl.ap[:, k, j * 128:(j + 1) * 128], cond.ap[:, k:k + 1], k == 0, k == 7, [sl, cond])
                tcopy("dve", vecA.ap[:, idx, half * 4:half * 4 + 4], pt.ap[:, 0:4], [pt], [vecA])
            else:
                for k in range(8):
                    mm(pt, pt.ap[:, :], cond_rep.ap[:, k, :], sl.ap[:, k, :], k == 0, k == 7, [sl, cond_rep])
                P.dma("sp", brow.ap[:], ada_bG[li, sidx, 0:1, half * 512:(half + 1) * 512].partition_broadcast(128), [], brow)
                P.dma("sp", nrow.ap[:], normG[li, sidx, 0:1, half * 512:(half + 1) * 512].partition_broadcast(128), [], nrow)
                tt("dve", brow.ap[:], pt.ap[:, :], brow.ap[:], ALU.add, [pt, brow], [brow])
                tt("dve", G.ap[:, idx, half * 512:(half + 1) * 512], brow.ap[:], nrow.ap[:], ALU.mult, [brow, nrow], [G])

        state["as"] = 0
        state["abuf"] = {"slabs": aslab, "brow": brow, "nrow": nrow}

        def ada_layer_groups(l):
            gl = []
            for v in range(6):
                for half in range(2):
                    col0 = v * 1024 + half * 512
                    if v in (2, 5):
                        gl.append((ada_w[l], col0, "G", l * 2 + (0 if v == 2 else 1), l, 0 if v == 2 else 1))
                    else:
                        gl.append((ada_w[l], col0, "A", l * 4 + {0: 0, 1: 1, 3: 2, 4: 3}[v], l, 0))
            return gl

        def ada_finish(i0, i1, mods):
            tt("dve", vecA.ap[:, i0:i1, :], vecA.ap[:, i0:i1, :], biasA.ap[:, i0:i1, :], ALU.add, [vecA, biasA], [vecA])
            for (mi, shi, sci) in mods:
                stt("dve", modA.ap[:, mi, :], vecA.ap[:, sci, :], 1.0, nrmA.ap[:, mi, :], ALU.add, ALU.mult, [vecA, nrmA], [modA])
                tcopy("dve", modB.ap[:, mi, :], vecA.ap[:, shi, :], [vecA], [modB])

        for g in ada_layer_groups(0):
            ada_group(*g)
        ada_finish(0, 4, ((0, 0, 1), (1, 2, 3)))
        deferred = [(kv_ada_w, v * 1024 + half * 512, "A", 8 + v, 0, 0) for v in range(2) for half in range(2)] + ada_layer_groups(1)
        if stage == 1:
            tcopy("dve", X.ap[:, :, :], G.ap[:, :, :], [G], [X])
            tcopy("dve", X.ap[:, 0, 0:40], modA.ap[:].rearrange("p a k -> p (a k)"), [modA], [X])
            tcopy("dve", X.ap[:, 0, 40:80], modB.ap[:].rearrange("p a k -> p (a k)"), [modB], [X])
        chk(1)
        P.dma("sp", WIF.ap[:], wb_in.ap[:, 2048:2064].rearrange("(k p) n -> p k n", p=128), [wb_in], WIF)

        state["aoff"] = 0
        QT = aview(4 * T, [128, 4, T], BF16, "QT")
        KT = aview(4 * T, [128, 4, T], BF16, "KT")
        KE = aview(4 * 4 * 128, [128, 4, 4, 128], BF16, "KE")
        KO = aview(4 * 4 * 128, [128, 4, 4, 128], BF16, "KO")
        VE = aview(4 * 8 * 130, [128, 4, 8, 130], BF16, "VE")
        VW = aview(4 * 8 * 130, [128, 4, 8, 130], BF16, "VW")
        OG = aview(4 * D, [128, 4, D], BF16, "OG")
        STt = aview(8 * 128, [128, 8, 128], BF16, "ST")
        HH = aview(2 * D, [128, 8, 128], F32, "HH")
        SQ = aview(2 * D, [128, 8, 128], F32, "SQ")
        GT = aview(2 * 256, [128, 256], F32, "GT")
        GH = aview(2 * 768, [128, 768], F32, "GH")
        M0 = [QT, KT, KE, KO, VE, VW, OG, STt, HH, SQ, GT, GH]
        _save = state["aoff"]
        state["aoff"] = 2048
        browL = aview(1024, [128, 512], F32, "browL")
        nrowL = aview(1024, [128, 512], F32, "nrowL")
        state["aoff"] = 16512
        aslabL = aview(8 * 512 * 2, [128, 8, 512], F32, "aslabL")
        state["aoff"] = _save
        LATE = [browL, nrowL, aslabL, cond_rep]
        m0_end = state["aoff"]
        state["aoff"] = 0
        HID = aview(32 * T, [128, 32, T], BF16, "HID")
        RT = [aview(T * 2, [128, T], F32, "RT%d" % i) for i in range(2)]
        MLPB = [HID] + RT
        state["aoff"] = 0
        CKV = aview(4 * 256 * 2, [128, 4, 256], F32, "CKV")
        CKN = aview(4 * 256, [128, 4, 256], BF16, "CKN")
        CKT = aview(2 * T, [128, 2, T], BF16, "CKT")
        KEXP = aview(8 * T, [128, 8, T], BF16, "KEXP")
        VEXP = aview(8 * T, [128, 8, 4, 128], BF16, "VEXP")
        R1 = aview(T * 2, [128, T], F32, "R1")
        R2 = aview(T * 2, [128, T], F32, "R2")
        KVB = [CKV, CKN, CKT, KEXP, VEXP, R1, R2]
        kv_end = state["aoff"]
        state["aoff"] = 0
        CQ = aview(4 * 384 * 2, [128, 4, 384], F32, "CQ")
        CQN = aview(4 * 384, [128, 4, 384], BF16, "CQN")
        CQT = aview(3 * T, [128, 3, T], BF16, "CQT")
        QNT = aview(8 * T, [128, 8, T], BF16, "QNT")
        QRT = aview(8 * T, [128, 8, T], BF16, "QRT")
        OT = aview(8 * T, [128, 8, T], BF16, "OT")
        PTs = [aview(T, [128, T], BF16, "PT%d" % i) for i in range(4)]
        RDEN = aview(T * 2, [128, T], F32, "RDEN")
        AR1 = aview(T * 2, [128, T], F32, "AR1")
        AR2 = aview(T * 2, [128, T], F32, "AR2")
        ATTB = [CQ, CQN, CQT, QNT, QRT, OT, RDEN, AR1, AR2] + PTs

        P.carry(prol_bufs, M0)
        P.carry(prol_bufs, [browL, nrowL, aslabL])

        def norm_to_HT(mi):
            memset("dve", st4.ap[:, 0:4], 0.0, [st4])
            for s in range(4):
                act(junk.ap[:], X.ap[:, s, :], AF.Square, [X], [junk, st4], accum=st4.ap[:, s:s + 1])
            rstd_from_ss(st4.ap[:, 0:4], st4.ap[:, 0:4], D, [st4], [st4])
            for s in range(4):
                if s % 2 == 0:
                    ts("dve", XN.ap[:, s, :], X.ap[:, s, :], st4.ap[:, s:s + 1], None, ALU.mult, None, [X, st4], [XN])
                else:
                    act(XN.ap[:, s, :], X.ap[:, s, :], AF.Identity, [X, st4], [XN], scale=st4.ap[:, s:s + 1])
            transpose_to_HT(XN, modA.ap[:, mi, :], modB.ap[:, mi, :], [modA, modB])

        def transpose_to_HT(src, a_ap, b_ap, extra):
            for k in range(8):
                pt = P.ps()
                pv = pt.ap[:].bitcast(BF16)
                for s in range(4):
                    trn(pt, pv[:, s * 128:(s + 1) * 128], src.ap[:, s, k * 128:(k + 1) * 128], ident_b, [src])
                if a_ap is not None:
                    if k % 2 == 0:
                        act(HT.ap[:, k, :], pv[:, 0:T], AF.Identity, [pt] + extra, [HT], scale=a_ap[:, k:k + 1], bias=b_ap[:, k:k + 1])
                    else:
                        ts("dve", HT.ap[:, k, :], pv[:, 0:T], a_ap[:, k:k + 1], b_ap[:, k:k + 1], ALU.mult, ALU.add, [pt] + extra, [HT])
                else:
                    if k % 2 == 0:
                        act(HT.ap[:, k, :], pv[:, 0:T], AF.Copy, [pt], [HT])
                    else:
                        tcopy("dve", HT.ap[:, k, :], pv[:, 0:T], [pt], [HT])

        def out_proj_load(wsrc):
            w0 = wload(wsrc.ap[:, 0:512].rearrange("(k p) n -> p k n", p=128), 8, 512, wsrc)
            w1 = wload(wsrc.ap[:, 512:1024].rearrange("(k p) n -> p k n", p=128), 8, 512, wsrc)
            return (w0, w1)

        def out_proj_sub(s, ws, lhs_of, gi, lhs_reads, light_dve=False):
            Y = Ys[s % 2]
            for half, w in enumerate(ws):
                pt = P.ps()
                for k in range(8):
                    mm(pt, pt.ap[:, :], lhs_of(k, s), w.ap[:, k, :], k == 0, k == 7, lhs_reads + [w])
                finish_half(pt, Y, s, half, light_dve)
            residual_update(Y, s, gi, light_dve)

        def out_proj_residual(lhs_of, wsrc, gi, lhs_reads):
            ws = out_proj_load(wsrc)
            for s in range(4):
                out_proj_sub(s, ws, lhs_of, gi, lhs_reads)

        def finish_half(pt, Y, s, half, light_dve=False):
            if half == 0:
                memset("dve", st4.ap[:, 8:10], 0.0, [st4])
            act(junk.ap[:, 0:512], pt.ap[:, :], AF.Square, [pt], [junk, st4], accum=st4.ap[:, 8 + half:9 + half])
            if light_dve:
                act(Y.ap[:, half * 512:(half + 1) * 512], pt.ap[:, :], AF.Copy, [pt], [Y])
            else:
                tcopy("dve", Y.ap[:, half * 512:(half + 1) * 512], pt.ap[:, :], [pt], [Y])

        def residual_update(Y, s, gi, light_dve=False):
            tt("dve", st4.ap[:, 10:11], st4.ap[:, 8:9], st4.ap[:, 9:10], ALU.add, [st4], [st4])
            rstd_from_ss(st4.ap[:, 10:11], st4.ap[:, 10:11], D, [st4], [st4])
            stt("dve", Y.ap[:], Y.ap[:], st4.ap[:, 10:11], G.ap[:, gi, :], ALU.mult, ALU.mult, [Y, st4, G], [Y])
            tt("pool" if light_dve else "dve", X.ap[:, s, :], X.ap[:, s, :], Y.ap[:], ALU.add, [X, Y], [X])

        def mlp(l, mi, gi, hook=None):
            norm_to_HT(mi)
            if hook is not None:
                hook()
            for g8 in range(8):
                w = wload(wb_w1[l].ap[:, g8 * 512:(g8 + 1) * 512].rearrange("(k p) n -> p k n", p=128), 8, 512, wb_w1[l])
                for j in range(4):
                    m = g8 * 4 + j
                    pt = P.ps()
                    for k in range(8):
                        mm(pt, pt.ap[:, :], w.ap[:, k, j * 128:(j + 1) * 128], HT.ap[:, k, :], k == 0, k == 7, [w, HT])
                    rt = RT[m % 2]
                    act(rt.ap[:], pt.ap[:, :], AF.Relu, [pt], [rt])
                    tt("pool", HID.ap[:, m, :], rt.ap[:], rt.ap[:], ALU.mult, [rt], [HID])
            memset("dve", st4.ap[:, 4:8], 0.0, [st4])
            memset("dve", st4.ap[:, 12:16], 0.0, [st4])
            for half in range(2):
                acc, got = P.acquire(4)
                for kg in range(4):
                    w = wload(wb_w2[l].ap[kg * 1024:(kg + 1) * 1024, half * 512:(half + 1) * 512].rearrange("(k p) n -> p k n", p=128), 8, 512, wb_w2[l])
                    for kk in range(8):
                        kc = kg * 8 + kk
                        for s in range(4):
                            mm(acc[s], acc[s].ap[:, :], HID.ap[:, kc, s * 128:(s + 1) * 128], w.ap[:, kk, :], kc == 0, kc == 31, [HID, w])
                for s in range(4):
                    finish_half_mlp(acc[s], s, half)
                P.release(got)
            for s in range(4):
                residual_update_mlp(s, gi)

        state["aoff"] = 32 * T + 2 * T * 2
        Y2 = aview(4 * D * 2, [128, 4, D], F32, "Y2") if state["aoff"] + 4 * D * 2 <= ARENA_E else None
        assert Y2 is not None
        MLPB.append(Y2)

        def finish_half_mlp(pt, s, half):
            act(junk.ap[:, 0:512], pt.ap[:, :], AF.Square, [pt], [junk, st4], accum=st4.ap[:, ((4 + s) if half == 0 else (12 + s)):((5 + s) if half == 0 else (13 + s))])
            tcopy("dve", Y2.ap[:, s, half * 512:(half + 1) * 512], pt.ap[:, :], [pt], [Y2])

        def residual_update_mlp(s, gi):
            tt("dve", st4.ap[:, 10:11], st4.ap[:, 4 + s:5 + s], st4.ap[:, 12 + s:13 + s], ALU.add, [st4], [st4])
            rstd_from_ss(st4.ap[:, 10:11], st4.ap[:, 10:11], D, [st4], [st4])
            stt("dve", Y2.ap[:, s, :], Y2.ap[:, s, :], st4.ap[:, 10:11], G.ap[:, gi, :], ALU.mult, ALU.mult, [Y2, st4, G], [Y2])
            tt("dve", X.ap[:, s, :], X.ap[:, s, :], Y2.ap[:, s, :], ALU.add, [X, Y2], [X])

        def mixer0(mt, state_only=False):
            memset("pool", VE.ap[:, :, :, 128:129], 1.0, [VE])
            memset("pool", KE.ap[:, :, :, 64:128], 0.0, [KE])
            memset("pool", KO.ap[:, :, :, 0:64], 0.0, [KO])
            norm_to_HT(0)
            pg = P.ps()
            for s in range(4):
                for k in range(8):
                    mm(pg, pg.ap[:, s * 16:(s + 1) * 16], HT.ap[:, k, s * 128:(s + 1) * 128], WIF.ap[:, k, :], k == 0, k == 7, [HT, WIF])
            gi_v = GT.ap[:, 0:32].rearrange("p (s h) -> p s h", s=4)
            gf_v = GT.ap[:, 32:64].rearrange("p (s h) -> p s h", s=4)
            pgv = pg.ap[:, 0:64].rearrange("p (s g) -> p s g", s=4)
            tt("dve", gi_v, pgv[:, :, 0:8], gb.ap[:, 0:1, :].to_broadcast([128, 4, 8]), ALU.add, [pg, gb], [GT])
            tt("dve", gf_v, pgv[:, :, 8:16], gb.ap[:, 1:2, :].to_broadcast([128, 4, 8]), ALU.add, [pg, gb], [GT])
            act(gf_v, gf_v, AF.Exp, [GT], [GT], scale=-1.0)
            act(gf_v, gf_v, AF.Ln, [GT, ones_f], [GT], scale=1.0, bias=ones_f.ap[:, 0:1])
            ts("dve", gf_v, gf_v, -1.0, None, ALU.mult, None, [GT], [GT])
            pb = P.ps()
            for s in range(4):
                mm(pb, pb.ap[:, s * 8:(s + 1) * 8], mask_f.ap[:], GT.ap[:, 32 + s * 8:32 + (s + 1) * 8], True, True, [mask_f, GT])
            bb_v = GT.ap[:, 64:96].rearrange("p (s h) -> p s h", s=4)
            a_v = GT.ap[:, 96:128].rearrange("p (s h) -> p s h", s=4)
            tcopy("dve", GT.ap[:, 64:96], pb.ap[:, 0:32], [pb], [GT])
            tt("dve", GT.ap[:, 96:128], GT.ap[:, 0:32], GT.ap[:, 64:96], ALU.subtract, [GT], [GT])
            pa = P.ps()
            pbt = P.ps()
            for s in range(4):
                trn(pa, pa.ap[0:8, s * 128:(s + 1) * 128], GT.ap[:, 96 + s * 8:96 + (s + 1) * 8], ident_f, [GT])
                trn(pbt, pbt.ap[0:8, s * 128:(s + 1) * 128], GT.ap[:, 64 + s * 8:64 + (s + 1) * 8], ident_f, [GT])
            P.op("dve", lambda e: e.tensor_reduce(out=GH.ap[0:8, 0:4], in_=pa.ap[0:8, 0:512].rearrange("p (s t) -> p s t", s=4), axis=AX.X, op=ALU.max), reads=[pa], writes=[GH])
            tcopy("dve", GH.ap[0:8, 4:8], pbt.ap[0:8, 0:512].rearrange("p (s t) -> p s t", s=4)[:, :, 127], [pbt], [GH])
            for c in range(4):
                tt("dve", GH.ap[0:8, 8 + c:9 + c], GH.ap[0:8, c:c + 1], mst.ap[:], ALU.max, [GH, mst], [GH])
                tt("dve", GH.ap[0:8, 12 + c:13 + c], mst.ap[:], GH.ap[0:8, 8 + c:9 + c], ALU.subtract, [GH, mst], [GH])
                tt("dve", mst.ap[:], GH.ap[0:8, 4 + c:5 + c], GH.ap[0:8, 8 + c:9 + c], ALU.add, [GH], [mst])
            act(GH.ap[0:8, 12:16], GH.ap[0:8, 12:16], AF.Exp, [GH], [GH])
            Rv = GH.ap[0:8, 32:96].rearrange("p (c x) -> p c x", c=4)
            for c in range(4):
                ts("dve", Rv[:, c, 0:8], ident_f.ap[0:8, 0:8], GH.ap[0:8, 8 + c:9 + c], None, ALU.mult, None, [ident_f, GH], [GH])
                ts("dve", Rv[:, c, 8:12], ident_f.ap[0:8, 0:8].rearrange("p (j r) -> p j r", r=2)[:, :, 0], GH.ap[0:8, 12 + c:13 + c], None, ALU.mult, None, [ident_f, GH], [GH])
                ts("dve", Rv[:, c, 12:16], ident_f.ap[0:8, 0:8].rearrange("p (j r) -> p j r", r=2)[:, :, 1], GH.ap[0:8, 12 + c:13 + c], None, ALU.mult, None, [ident_f, GH], [GH])
            pm = P.ps()
            mm(pm, pm.ap[:, 0:64], ones_f.ap[0:8, :], GH.ap[0:8, 32:96], True, True, [ones_f, GH])
            pmv = pm.ap[:, 0:64].rearrange("p (c x) -> p c x", c=4)
            w_v = GT.ap[:, 128:160].rearrange("p (s h) -> p s h", s=4)
            e_v = GT.ap[:, 160:192].rearrange("p (s h) -> p s h", s=4)
            dec_v = GT.ap[:, 192:224].rearrange("p (c x) -> p c x", c=4)
            tt("dve", w_v, a_v, pmv[:, :, 0:8], ALU.subtract, [GT, pm], [GT])
            stt("dve", e_v, bb_v, -1.0, pmv[:, :, 0:8], ALU.mult, ALU.subtract, [GT, pm], [GT])
            tcopy("dve", dec_v, pmv[:, :, 8:16], [pm], [GT])
            act(GT.ap[:, 128:192], GT.ap[:, 128:192], AF.Exp, [GT], [GT])

            if stage == 2:
                for s_ in range(4):
                    tcopy("dve", X.ap[:, s_, :], HT.ap[:, 2 * s_:2 * s_ + 2, :].rearrange("p a t -> p (a t)"), [HT], [X])
                tcopy("dve", X.ap[:, 0, 0:256], GT.ap[:], [GT], [X])
            chk(2)
            if not state_only:
                wq = wload(wb_in.ap[:, 0:512].rearrange("(k p) n -> p k n", p=128), 8, 512, wb_in)
                for m in range(4):
                    pt = P.ps()
                    for k in range(8):
                        mm(pt, pt.ap[:, :], wq.ap[:, k, m * 128:(m + 1) * 128], HT.ap[:, k, :], k == 0, k == 7, [wq, HT])
                    act(QT.ap[:, m, :], pt.ap[:, :], AF.Copy, [pt], [QT], scale=0.125)
            for hv in range(2):
                wv = wload(wb_in.ap[:, 1024 + hv * 512:1024 + (hv + 1) * 512].rearrange("(k p) n -> p k n", p=128), 8, 512, wb_in)
                for s in range(4):
                    pt = P.ps()
                    for k in range(8):
                        mm(pt, pt.ap[:, :], HT.ap[:, k, s * 128:(s + 1) * 128], wv.ap[:, k, :], k == 0, k == 7, [wv, HT])
                    ptv = pt.ap[:, :].rearrange("p (h e) -> p h e", h=4)
                    tcopy("dve", VE.ap[:, s, hv * 4:(hv + 1) * 4, 0:128], ptv, [pt], [VE])
            if not state_only:
                for ho in range(2):
                    wo = wload(wb_in.ap[:, 2064 + ho * 512:2064 + (ho + 1) * 512].rearrange("(k p) n -> p k n", p=128), 8, 512, wb_in)
                    for s in range(4):
                        pt = P.ps()
                        for k in range(8):
                            mm(pt, pt.ap[:, :], HT.ap[:, k, s * 128:(s + 1) * 128], wo.ap[:, k, :], k == 0, k == 7, [wo, HT])
                        act(OG.ap[:, s, ho * 512:(ho + 1) * 512], pt.ap[:, :], AF.Sigmoid, [pt], [OG])
            wk = wload(wb_in.ap[:, 512:1024].rearrange("(k p) n -> p k n", p=128), 8, 512, wb_in)
            if not state_only:
                for m in range(4):
                    pt = P.ps()
                    for k in range(8):
                        mm(pt, pt.ap[:, :], wk.ap[:, k, m * 128:(m + 1) * 128], HT.ap[:, k, :], k == 0, k == 7, [wk, HT])
                    tcopy("dve", KT.ap[:, m, :], pt.ap[:, :], [pt], [KT])
            for s in range(4):
                pt = P.ps()
                for k in range(8):
                    mm(pt, pt.ap[:, :], HT.ap[:, k, s * 128:(s + 1) * 128], wk.ap[:, k, :], k == 0, k == 7, [wk, HT])
                ptv = pt.ap[:, :].rearrange("p (j r d) -> p j r d", j=4, r=2)
                tcopy("dve", KE.ap[:, s, :, 0:64], ptv[:, :, 0, :], [pt], [KE])
                tcopy("dve", KO.ap[:, s, :, 64:128], ptv[:, :, 1, :], [pt], [KO])
            if not state_only:
                for s in range(4):
                    tt("pool", OG.ap[:, s, :], OG.ap[:, s, :], headg.ap[:], ALU.mult, [OG, headg], [OG])
            for s in range(4):
                tt("dve", VW.ap[:, s], VE.ap[:, s], w_v[:, s, :].unsqueeze(2).to_broadcast([128, 8, 130]), ALU.mult, [VE, GT], [VW])

            chk(3)
            wo_slabs = None if state_only else out_proj_load(wb_out)
            for s in range(4):
                sc = slice(s * 128, (s + 1) * 128)
                tt("dve", Cf.ap[0:64], Cf.ap[0:64], dec_v[0:64, s, 0:4].unsqueeze(2).to_broadcast([64, 4, 130]), ALU.mult, [Cf, GT], [Cf])
                tt("dve", Cf.ap[64:128], Cf.ap[64:128], dec_v[64:128, s, 4:8].unsqueeze(2).to_broadcast([64, 4, 130]), ALU.mult, [Cf, GT], [Cf])
                if not state_only:
                    act(Cb.ap[:], Cf.ap[:], AF.Copy, [Cf], [Cb])
                for r in (range(2) if not state_only else ()):
                    pS = P.ps()
                    for j in range(4):
                        mm(pS, pS.ap[:, j * 128:(j + 1) * 128], KT.ap[64 * r:64 * r + 64, j, sc], QT.ap[64 * r:64 * r + 64, j, sc], True, True, [KT, QT])
                    tt("dve", STt.ap[:, :, :].rearrange("p (j r) t -> p j r t", r=2)[:, :, r, :],
                       pS.ap[:, :].rearrange("p (j t) -> p j t", j=4), mask_b.ap[:, None, :].to_broadcast([128, 4, 128]), ALU.mult, [pS, mask_b], [STt])
                for (h0, nh) in (((0, 3), (3, 3), (6, 2)) if not state_only else ()):
                    pN = P.ps()
                    for hh in range(nh):
                        h = h0 + hh
                        j, r = h // 2, h % 2
                        mm(pN, pN.ap[:, hh * 129:(hh + 1) * 129], QT.ap[64 * r:64 * r + 64, j, sc], Cb.ap[64 * r:64 * r + 64, j, 0:129], True, False, [QT, Cb])
                        mm(pN, pN.ap[:, hh * 129:(hh + 1) * 129], STt.ap[:, h, :], VW.ap[:, s, h, 0:129], False, True, [STt, VW])
                    pNv = pN.ap[:, 0:nh * 129].rearrange("p (h e) -> p h e", h=nh)
                    dd = GT.ap[:, 224:224 + nh]
                    act(dd, pNv[:, :, 128], AF.Abs, [pN], [GT])
                    tt("dve", dd, dd, e_v[:, s, h0:h0 + nh], ALU.max, [GT], [GT])
                    P.op("dve", lambda e, dd=dd: e.reciprocal(out=dd, in_=dd), reads=[GT], writes=[GT])
                    tt("dve", HH.ap[:, h0:h0 + nh, :], pNv[:, :, 0:128], dd.unsqueeze(2).to_broadcast([128, nh, 128]), ALU.mult, [pN, GT], [HH])
                for jb in range(2):
                    pC = P.ps()
                    for jj in range(2):
                        j = jb * 2 + jj
                        mm(pC, pC.ap[:, jj * 129:(jj + 1) * 129], KE.ap[:, s, j, :], VW.ap[:, s, 2 * j, 0:129], True, False, [KE, VW])
                        mm(pC, pC.ap[:, jj * 129:(jj + 1) * 129], KO.ap[:, s, j, :], VW.ap[:, s, 2 * j + 1, 0:129], False, True, [KO, VW])
                    tt("dve", Cf.ap[:, jb * 2:jb * 2 + 2, 0:129], Cf.ap[:, jb * 2:jb * 2 + 2, 0:129], pC.ap[:, 0:258].rearrange("p (j e) -> p j e", j=2), ALU.add, [Cf, pC], [Cf])
                if state_only:
                    continue
                tt("pool", SQ.ap[:], HH.ap[:], HH.ap[:], ALU.mult, [HH], [SQ])
                P.op("dve", lambda e: e.tensor_reduce(out=GT.ap[:, 232:240], in_=SQ.ap[:], axis=AX.X, op=ALU.add), reads=[SQ], writes=[GT])
                rstd_from_ss(GT.ap[:, 232:240], GT.ap[:, 232:240], 128, [GT], [GT])
                tt("dve", HH.ap[:], HH.ap[:], GT.ap[:, 232:240].unsqueeze(2).to_broadcast([128, 8, 128]), ALU.mult, [HH, GT], [HH])
                tt("dve", XN.ap[:, s, :].rearrange("p (h e) -> p h e", h=8), HH.ap[:], OG.ap[:, s, :].rearrange("p (h e) -> p h e", h=8), ALU.mult, [HH, OG], [XN])
                ptT = P.ps()
                pvT = ptT.ap[:].bitcast(BF16)
                for k in range(8):
                    trn(ptT, pvT[:, k * 128:(k + 1) * 128], XN.ap[:, s, k * 128:(k + 1) * 128], ident_b, [XN])
                act(HT.ap[:, :, s * 128:(s + 1) * 128], pvT[:, 0:1024].rearrange("p (k t) -> p k t", k=8), AF.Copy, [ptT], [HT])
                out_proj_sub(s, wo_slabs, lambda k, s_: HT.ap[:, k, s_ * 128:(s_ + 1) * 128], 0, [HT], light_dve=True)
            if state_only:
                return
            chk(5)

        def rope_tables(mt):
            t0 = mt * T
            ji = junk.ap[0:64, :].bitcast(I32)
            P.dma("sp", ji, pos[0:1, t0:t0 + T].partition_broadcast(64), [], junk)
            tcopy("dve", ANG.ap[:], ji, [junk], [ANG])
            ts("dve", ANG.ap[:], ANG.ap[:], inv_col.ap[:, 0:1], None, ALU.mult, None, [ANG, inv_col], [ANG])
            for (dst, shift) in ((SIN, 0.0), (COS, PI / 2)):
                ts("dve", dst.ap[:], ANG.ap[:], shift, None, ALU.add, None, [ANG], [dst])
                ts("dve", KF.ap[:], dst.ap[:], 1.0 / TWO_PI, None, ALU.mult, None, [dst], [KF])
                tcopy("dve", ji, KF.ap[:], [KF], [junk])
                tcopy("dve", KF.ap[:], ji, [junk], [KF])
                stt("dve", dst.ap[:], KF.ap[:], -6.28125, dst.ap[:], ALU.mult, ALU.add, [KF, dst], [dst])
                stt("dve", dst.ap[:], KF.ap[:], -0.0019353071795864769, dst.ap[:], ALU.mult, ALU.add, [KF, dst], [dst])
                ts("dve", KF.ap[:], dst.ap[:], PI, None, ALU.is_gt, None, [dst], [KF])
                stt("dve", dst.ap[:], KF.ap[:], -TWO_PI, dst.ap[:], ALU.mult, ALU.add, [KF, dst], [dst])
                ts("dve", KF.ap[:], dst.ap[:], -PI, None, ALU.is_lt, None, [dst], [KF])
                stt("dve", dst.ap[:], KF.ap[:], TWO_PI, dst.ap[:], ALU.mult, ALU.add, [KF, dst], [dst])
                act(dst.ap[:], dst.ap[:], AF.Sin, [dst], [dst])

        def kv_phase(mt):
            t0 = mt * T
            norm_to_HT(4)
            wa = wload(wb_a2.ap[:, :].rearrange("(k p) n -> p k n", p=128), 8, 384, wb_a2)
            memset("dve", st4.ap[:, 0:4], 0.0, [st4])
            for s in range(4):
                pt = P.ps()
                for k in range(8):
                    mm(pt, pt.ap[:, 0:256], HT.ap[:, k, s * 128:(s + 1) * 128], wa.ap[:, k, 0:256], k == 0, k == 7, [HT, wa])
                act(junk.ap[:, 0:256], pt.ap[:, 0:256], AF.Square, [pt], [junk, st4], accum=st4.ap[:, s:s + 1])
                tcopy("dve", CKV.ap[:, s, :], pt.ap[:, 0:256], [pt], [CKV])
            rstd_from_ss(st4.ap[:, 0:4], st4.ap[:, 0:4], 256, [st4], [st4])
            for s in range(4):
                stt("dve", CKN.ap[:, s, :], CKV.ap[:, s, :], st4.ap[:, s:s + 1], latg.ap[:], ALU.mult, ALU.mult, [CKV, st4, latg], [CKN])
            for c in range(2):
                pt = P.ps()
                pv = pt.ap[:].bitcast(BF16)
                for s in range(4):
                    trn(pt, pv[:, s * 128:(s + 1) * 128], CKN.ap[:, s, c * 128:(c + 1) * 128], ident_b, [CKN])
                tcopy("dve", CKT.ap[:, c, :], pv[:, 0:T], [pt], [CKT])
            pA = P.ps()
            pB = P.ps()
            for k in range(8):
                mm(pA, pA.ap[0:64, :], wa.ap[:, k, 256:320], HT.ap[:, k, :], k == 0, k == 7, [HT, wa])
            for k in range(8):
                mm(pB, pB.ap[0:64, :], wa.ap[:, k, 320:384], HT.ap[:, k, :], k == 0, k == 7, [HT, wa])
            tt("dve", R1.ap[0:64], pA.ap[0:64, :], COS.ap[0:64], ALU.mult, [pA, COS], [R1])
            tt("dve", R2.ap[0:64], pB.ap[0:64, :], SIN.ap[0:64], ALU.mult, [pB, SIN], [R2])
            tt("pool", krT.ap[0:64, t0:t0 + T], R1.ap[0:64], R2.ap[0:64], ALU.add, [R1, R2], [krT])
            ti = mt - FIRST_OWN
            P.dma("act", lat_src[ti].ap[:, 0:1024].rearrange("p (c t) -> p c t", c=2), CKT.ap[:], [CKT], lat_src[ti])
            P.dma("act", lat_src[ti].ap[0:64, 1024:1536], krT.ap[0:64, t0:t0 + T], [krT], lat_src[ti])
            P.collective(lambda e, ti=ti: e.collective_compute("AllGather", ALU.bypass, replica_groups=[[0, 1], [2, 3], [4, 5], [6, 7]],
                                                               ins=[lat_src[ti].ap[:, :]], outs=[lat_all[ti].ap[:, :]]), [lat_src[ti]], lat_all[ti])
            expand_kv(mt)

        def expand_kv(mt):
            t0 = mt * T
            wbk = wload(wb_b.ap[:, :].rearrange("(c p) n -> p c n", p=128), 2, 2048, wb_b)
            wbv = wbk.ap[:, :, :].rearrange("p c (h x) -> p c h x", h=8)
            for h in range(8):
                pt = P.ps()
                for c in range(2):
                    mm(pt, pt.ap[:, :], wbv[:, c, h, 0:128], CKT.ap[:, c, :], c == 0, c == 1, [wbk, CKT])
                if h % 2 == 0:
                    act(KEXP.ap[:, h, :], pt.ap[:, :], AF.Copy, [pt], [KEXP])
                else:
                    tcopy("dve", KEXP.ap[:, h, :], pt.ap[:, :], [pt], [KEXP])
            for s in range(4):
                for hv in range(2):
                    pt = P.ps()
                    for hh in range(4):
                        for c in range(2):
                            h = hv * 4 + hh
                            mm(pt, pt.ap[:, hh * 128:(hh + 1) * 128], CKT.ap[:, c, s * 128:(s + 1) * 128], wbv[:, c, h, 128:256], c == 0, c == 1, [wbk, CKT])
                    ptv = pt.ap[:, :].rearrange("p (h e) -> p h e", h=4)
                    if hv == 0:
                        act(VEXP.ap[:, hv * 4:(hv + 1) * 4, s, :], ptv, AF.Copy, [pt], [VEXP])
                    else:
                        tcopy("dve", VEXP.ap[:, hv * 4:(hv + 1) * 4, s, :], ptv, [pt], [VEXP])
            for h in range(8):
                P.dma("act", kcache.ap[h, :, t0:t0 + T], KEXP.ap[:, h, :], [KEXP], kcache)
                P.dma("act", vcache.ap[h, :, mt * 4:(mt + 1) * 4, :], VEXP.ap[:, h, :, :], [VEXP], vcache)

        def mixer1(mt):
            t0 = mt * T
            norm_to_HT(2)
            wqa = wload(wb_qa.ap[:, :].rearrange("(k p) n -> p k n", p=128), 8, 384, wb_qa)
            memset("dve", st4.ap[:, 0:4], 0.0, [st4])
            for s in range(4):
                pt = P.ps()
                for k in range(8):
                    mm(pt, pt.ap[:, 0:384], HT.ap[:, k, s * 128:(s + 1) * 128], wqa.ap[:, k, :], k == 0, k == 7, [HT, wqa])
                act(junk.ap[:, 0:384], pt.ap[:, 0:384], AF.Square, [pt], [junk, st4], accum=st4.ap[:, s:s + 1])
                tcopy("dve", CQ.ap[:, s, :], pt.ap[:, 0:384], [pt], [CQ])
            rstd_from_ss(st4.ap[:, 0:4], st4.ap[:, 0:4], 384, [st4], [st4])
            for s in range(4):
                stt("dve", CQN.ap[:, s, :], CQ.ap[:, s, :], st4.ap[:, s:s + 1], qlatg.ap[:], ALU.mult, ALU.mult, [CQ, st4, qlatg], [CQN])
            for c in range(3):
                pt = P.ps()
                pv = pt.ap[:].bitcast(BF16)
                for s in range(4):
                    trn(pt, pv[:, s * 128:(s + 1) * 128], CQN.ap[:, s, c * 128:(c + 1) * 128], ident_b, [CQN])
                tcopy("dve", CQT.ap[:, c, :], pv[:, 0:T], [pt], [CQT])
            memset("pool", QRT.ap[64:65, :, :], 0.0, [QRT])
            for hg in range(2):
                wqb = wload(wb_qb2.ap[:, hg * 4:(hg + 1) * 4, :].rearrange("(c p) h x -> p c (h x)", p=128), 3, 1024, wb_qb2)
                wv = wqb.ap[:, :, :].rearrange("p c (h x) -> p c h x", h=4)
                for hh in range(4):
                    h = hg * 4 + hh
                    pt = P.ps()
                    for c in range(3):
                        mm(pt, pt.ap[:, :], wv[:, c, hh, 0:128], CQT.ap[:, c, :], c == 0, c == 2, [wqb, CQT])
                    act(QNT.ap[:, h, :], pt.ap[:, :], AF.Copy, [pt], [QNT])
                    pA = P.ps()
                    pB = P.ps()
                    for c in range(3):
                        mm(pA, pA.ap[0:64, :], wv[:, c, hh, 128:192], CQT.ap[:, c, :], c == 0, c == 2, [wqb, CQT])
                    for c in range(3):
                        mm(pB, pB.ap[0:64, :], wv[:, c, hh, 192:256], CQT.ap[:, c, :], c == 0, c == 2, [wqb, CQT])
                    tt("dve", AR1.ap[0:64], pA.ap[0:64, :], COS.ap[0:64], ALU.mult, [pA, COS], [AR1])
                    tt("dve", AR2.ap[0:64], pB.ap[0:64, :], SIN.ap[0:64], ALU.mult, [pB, SIN], [AR2])
                    tt("pool", QRT.ap[0:64, h, :], AR1.ap[0:64], AR2.ap[0:64], ALU.add, [AR1, AR2], [QRT])
            nkb = 4 * (mt + 1)
            npiece = (nkb + 15) // 16
            accs, got = P.acquire(4)
            LOOK = 2
            pieces = [(h, pc) for h in range(8) for pc in range(npiece)]
            loaded = {}

            def load_piece(idx):
                if idx >= len(pieces) or idx in loaded:
                    return
                h, pc = pieces[idx]
                kb0 = pc * 16
                nb = min(16, nkb - kb0)
                sl = kslots[state["ks"] % NKS]
                state["ks"] += 1
                kv_k = sl.ap[:, 0:nb * 128]
                kv_v = sl.ap[:, 2048:2048 + nb * 128].rearrange("p (b e) -> p b e", b=nb)
                P.dma("sp", kv_k, kcache.ap[h, :, kb0 * 128:(kb0 + nb) * 128], [kcache], sl)
                P.dma("sp", kv_v, vcache.ap[h, :, kb0:kb0 + nb, :], [vcache], sl)
                loaded[idx] = (sl, kv_k, kv_v)

            blocks = []
            for idx, (h, pc) in enumerate(pieces):
                kb0 = pc * 16
                for bi in range(min(16, nkb - kb0)):
                    blocks.append((h, idx, bi, kb0 + bi))
            nblk = len(blocks)
            pend = {}

            def emit_S(i):
                h, idx, bi, kb = blocks[i]
                if bi == 0:
                    load_piece(idx)
                if bi == LOOK:
                    load_piece(idx + 1)
                sl, kv_k, kv_v = loaded[idx]
                pS = P.ps()
                mm(pS, pS.ap[:, :], kv_k[:, bi * 128:(bi + 1) * 128], QNT.ap[:, h, :], True, False, [sl, QNT])
                mm(pS, pS.ap[:, :], krT.ap[0:65, kb * 128:(kb + 1) * 128], QRT.ap[0:65, h, :], False, True, [krT, QRT])
                PT = PTs[i % 4]
                act(PT.ap[:], pS.ap[:, :], AF.Exp, [pS], [PT], scale=ATT_SCALE)
                jd = kb - 4 * mt
                if jd >= 0:
                    if jd > 0:
                        memset("pool", PT.ap[:, 0:jd * 128], 0.0, [PT])
                    tt("pool", PT.ap[:, jd * 128:(jd + 1) * 128], PT.ap[:, jd * 128:(jd + 1) * 128], mask_b.ap[:], ALU.mult, [PT, mask_b], [PT])
                pend[i] = PT

            def emit_PV(i):
                h, idx, bi, kb = blocks[i]
                sl, kv_k, kv_v = loaded[idx]
                pO, pD = accs[2 * (h % 2)], accs[2 * (h % 2) + 1]
                PT = pend.pop(i)
                first = kb == 0
                last = kb == nkb - 1
                mm(pO, pO.ap[:, :], kv_v[:, bi, :], PT.ap[:], first, last, [sl, PT])
                onesT = fones_b if kb < 4 * FIRST_OWN else ones_b
                mm(pD, pD.ap[:, :], onesT.ap[:], PT.ap[:], first, last, [onesT, PT])
                if last:
                    P.op("dve", lambda e, pD=pD: e.reciprocal(out=RDEN.ap[:], in_=pD.ap[:, :]), reads=[pD], writes=[RDEN])
                    tt("dve", OT.ap[:, h, :], pO.ap[:, :], RDEN.ap[:], ALU.mult, [pO, RDEN], [OT])

            for i in range(nblk + LOOK):
                if i < nblk:
                    emit_S(i)
                if i - LOOK >= 0:
                    emit_PV(i - LOOK)
            P.release(got)
            out_proj_residual(lambda k, s: OT.ap[:, k, s * 128:(s + 1) * 128], wb_bo, 2, [OT])

        def load_x(src_ap, reads):
            P.dma("sp", X.ap[:], src_ap.rearrange("(s p) d -> p s d", p=128), reads, X)

        state["abuf"] = {"slabs": [aslabL], "brow": browL, "nrow": nrowL}
        for mt in range(FIRST_OWN):
            load_x(xs[mt * T:(mt + 1) * T, :], [])
            if deferred:
                ada_group(*deferred.pop(0))
            mixer0(mt, state_only=True)
            if deferred:
                ada_group(*deferred.pop(0))
        while deferred:
            ada_group(*deferred.pop(0))
        ada_finish(4, 10, ((2, 4, 5), (3, 6, 7), (4, 8, 9)))
        P.carry(LATE, M0)
        ts("dve", Cf.ap[:], Cf.ap[:], flag.ap[:, 0:1], None, ALU.mult, None, [Cf, flag], [Cf])
        ts("dve", mst.ap[:], mst.ap[:], flag.ap[0:8, 0:1], None, ALU.mult, None, [mst, flag], [mst])

        prev = M0
        for mt in range(FIRST_OWN, NT):
            load_x(xs[mt * T:(mt + 1) * T, :], [])
            if prev is not M0:
                P.carry(prev, M0)
            mixer0(mt)
            P.carry(M0, MLPB)
            mlp(0, 1, 1, hook=lambda: rope_tables(mt))
            o0 = (mt - FIRST_OWN) * T
            P.dma("act", xmid.ap[o0:o0 + T, :].rearrange("(s p) d -> p s d", p=128), X.ap[:], [X], xmid)
            P.carry(MLPB, KVB)
            kv_phase(mt)
            prev = KVB

        for mt in range(FIRST_OWN):
            P.dma("sp", krT.ap[0:64, mt * T:(mt + 1) * T], lat_all[mt].ap[0:64, 1024:1536], [lat_all[mt]], krT)
            P.dma("sp", CKT.ap[:], lat_all[mt].ap[0:128, 0:1024].rearrange("p (c t) -> p c t", c=2), [lat_all[mt]], CKT)
            ts("dve", CKT.ap[:], CKT.ap[:], flag.ap[:, 0:1], None, ALU.mult, None, [CKT, flag], [CKT])
            expand_kv(mt)

        rope_tables(FIRST_OWN)
        prev = KVB
        for mt in range(FIRST_OWN, NT):
            o0 = (mt - FIRST_OWN) * T
            load_x(xmid.ap[o0:o0 + T, :], [xmid])
            P.carry(prev, ATTB)
            mixer1(mt)
            P.carry(ATTB, MLPB)
            mlp(1, 3, 3, hook=(lambda: rope_tables(mt + 1)) if mt + 1 < NT else None)
            prev = MLPB
            P.dma("act", yout[o0:o0 + T, :].rearrange("(s p) d -> p s d", p=128), X.ap[:], [X], yout_b)
    try:
        body()
    except _Stop:
        P.dma("pool", yout[0:T, :].rearrange("(s p) d -> p s d", p=128), X.ap[:], [X], yout_b)
    P.final_wait("pool", [yout_b])
    allb = [Buf("fin")]
    block = es.enter_context(nc.Block())

    @block.tensor
    def _(e):
        P.replay("pe", e)

    @block.scalar
    def _(e):
        P.replay("act", e)

    @block.vector
    def _(e):
        P.replay("dve", e)

    @block.gpsimd
    def _(e):
        P.replay("pool", e)

    @block.sync
    def _(e):
        P.replay("sp", e)


_CACHE = {}


def _prep_inputs(x, c, positions, ada_w, ada_b, norm_g, a_w_in, a_gate_b, a_head_g, a_w_out,
                 kv_ada_w, kv_ada_b, kv_norm_g, kv_w_a, kv_latent_g, kv_w_b, b_w_q_a, b_q_latent_g,
                 b_w_q_b, b_w_out, mlp_w1, mlp_w2):
    f = np.float32

    def pk(v):
        return np.ascontiguousarray(np.asarray(v, f).reshape(8, 128).T)

    ident = np.eye(128, dtype=f)
    mask01 = np.triu(np.ones((128, 128), f))
    half = 32
    inv = (10000.0 ** (-np.arange(half, dtype=f) / half)).astype(f)
    inv_col = np.concatenate([inv, inv]).reshape(64, 1).astype(f)
    ada_b = np.asarray(ada_b, f)
    norm_g = np.asarray(norm_g, f)
    ada_bA = np.stack([np.stack([pk(ada_b[l, v * 1024:(v + 1) * 1024]) for v in (0, 1, 3, 4)]) for l in range(2)])
    ada_bG = np.stack([np.stack([ada_b[l, v * 1024:(v + 1) * 1024].reshape(1, 1024) for v in (2, 5)]) for l in range(2)])
    normA = np.stack([np.stack([pk(norm_g[l, v]) for v in (0, 2)]) for l in range(2)])
    normG = np.stack([np.stack([norm_g[l, v].reshape(1, 1024) for v in (1, 3)]) for l in range(2)])
    kv_ada_b = np.asarray(kv_ada_b, f)
    shared = {
        "ident": ident, "mask01": mask01, "inv_col": inv_col,
        "ada_w": np.ascontiguousarray(ada_w, f), "ada_bA": np.ascontiguousarray(ada_bA), "ada_bG": np.ascontiguousarray(ada_bG),
        "normA": np.ascontiguousarray(normA), "normG": np.ascontiguousarray(normG),
        "kv_ada_w": np.ascontiguousarray(kv_ada_w, f),
        "kv_ada_bA": np.ascontiguousarray(np.stack([pk(kv_ada_b[0:1024]), pk(kv_ada_b[1024:2048])])),
        "kv_normA": pk(kv_norm_g),
        "a_w_in": np.ascontiguousarray(a_w_in[0], f),
        "gate_b": np.ascontiguousarray(np.asarray(a_gate_b[0], f).reshape(2, 1, 8)),
        "head_g": np.ascontiguousarray(np.asarray(a_head_g[0], f).reshape(1, 1024)),
        "a_w_out": np.ascontiguousarray(a_w_out[0], f),
        "kv_w_a": np.ascontiguousarray(kv_w_a, f),
        "kv_lat_g": np.ascontiguousarray(np.asarray(kv_latent_g, f).reshape(1, 256)),
        "kv_w_b": np.ascontiguousarray(kv_w_b, f),
        "w_q_a": np.ascontiguousarray(b_w_q_a[0], f),
        "q_lat_g": np.ascontiguousarray(np.asarray(b_q_latent_g[0], f).reshape(1, 384)),
        "w_q_b": np.ascontiguousarray(b_w_q_b[0], f),
        "b_w_out": np.ascontiguousarray(b_w_out[0], f),
        "mlp_w1": np.ascontiguousarray(mlp_w1, f),
        "mlp_w2": np.ascontiguousarray(mlp_w2, f),
    }
    x = np.asarray(x, f)
    positions = np.asarray(positions, np.int32)
    in_maps = []
    for core in range(8):
        b, hf = core // 2, core % 2
        if hf == 1:
            xs = x[b]
            ps = positions[b]
        else:
            xs = np.concatenate([np.zeros((SEQ // 2, D), f), x[b, :SEQ // 2]], axis=0)
            ps = np.concatenate([np.zeros((SEQ // 2,), np.int32), positions[b, :SEQ // 2]])
        m = dict(shared)
        m["xs"] = np.ascontiguousarray(xs)
        m["pos"] = np.ascontiguousarray(ps.reshape(1, SEQ))
        m["flag"] = np.full((128, 1), float(hf), f)
        m["cT"] = pk(np.asarray(c, f)[b])
        in_maps.append(m)
    return in_maps


def kernel(**inputs):
    in_maps = _prep_inputs(**inputs)
    if "nc" not in _CACHE:
        _CACHE["nc"] = build_program(NT)
    nc = _CACHE["nc"]
    res = run_bass_kernel_spmd(nc, in_maps, core_ids=list(range(8)))
    out = np.zeros((4, SEQ, D), np.float32)
    for core in range(8):
        b, hf = core // 2, core % 2
        out[b, hf * (SEQ // 2):(hf + 1) * (SEQ // 2)] = res.results[core]["y"]
    return out
```

```python
import numpy as np
from contextlib import ExitStack
import concourse.bass as bass
import concourse.mybir as mybir
from concourse.bass_utils import run_bass_kernel_spmd

F32 = mybir.dt.float32
BF16 = mybir.dt.bfloat16
I32 = mybir.dt.int32
AF = mybir.ActivationFunctionType
ALU = mybir.AluOpType
AX = mybir.AxisListType

D = 1024
SEQ = 8192
T = 512
NT = 16
FIRST_OWN = 8
EPS = 1e-6
H = 8
SAME_ENG_SYNC = True
TWO_PI = 6.283185307179586
PI = 3.141592653589793
ATT_SCALE = 192.0 ** -0.5

ENGS = ("pe", "act", "dve", "pool", "sp")


class Buf:
    __slots__ = ("name", "w", "r", "dsem", "dcnt")

    def __init__(self, name):
        self.name = name
        self.w = None
        self.r = {}
        self.dsem = None
        self.dcnt = 0


class TT:
    def __init__(self, ap, buf):
        self.ap = ap
        self.b = buf

    def __getitem__(self, k):
        return self.ap[k]


class Prog:
    def __init__(self, nc, es):
        self.nc = nc
        self.es = es
        self.streams = {e: [] for e in ENGS}
        self.esem = {e: es.enter_context(nc.semaphore("es_" + e)) for e in ENGS}
        self.ecnt = {e: 0 for e in ENGS}
        self.waited = {e: {} for e in ENGS}
        self.semh = {}
        for e in ENGS:
            self.semh["es_" + e] = self.esem[e]
        self.nbuf = 0
        self.psb = []
        self.rot = []
        self.rot_i = 0

    def sb(self, name, shape, dt):
        t = self.es.enter_context(self.nc.sbuf_tensor("s_" + name, list(shape), dt))
        return TT(t, Buf(name))

    def view(self, ap, name):
        return TT(ap, Buf(name))

    def dsem_of(self, buf):
        if buf.dsem is None:
            nm = "ds%d" % len(self.semh)
            buf.dsem = nm
            self.semh[nm] = self.es.enter_context(self.nc.semaphore(nm))
        return buf.dsem

    def _collect(self, reads, writes, eng=None):
        deps = {}
        own = None if eng is None else "es_" + eng

        def add(tok, raw):
            if tok is None:
                return
            s, v = tok
            if s == own and not raw:
                return
            if deps.get(s, 0) < v:
                deps[s] = v

        for b in reads:
            add(b.w, True)
        for b in writes:
            add(b.w, False)
            for s, v in b.r.items():
                add((s, v), False)
        return deps

    def _emit_waits(self, eng, deps):
        own = "es_" + eng
        for s, v in deps.items():
            if s == own and (eng in ("pe", "sp") or not SAME_ENG_SYNC):
                continue
            if self.waited[eng].get(s, 0) >= v:
                continue
            self.waited[eng][s] = v
            self.streams[eng].append(("wait", s, v))

    @staticmethod
    def _bufs(lst):
        return [x.b if isinstance(x, TT) else x for x in lst]

    def op(self, eng, fn, reads=(), writes=()):
        reads = self._bufs(reads)
        writes = self._bufs(writes)
        if eng != "pe":
            writes = writes + [b for b in reads if b.name.startswith("ps") and b not in writes]
        self._emit_waits(eng, self._collect(reads, writes, eng))
        self.ecnt[eng] += 1
        tok = ("es_" + eng, self.ecnt[eng])
        self.streams[eng].append(("ins", fn, tok[0], 1))
        for b in reads:
            if b.r.get(tok[0], 0) < tok[1]:
                b.r[tok[0]] = tok[1]
        for b in writes:
            b.w = tok
            b.r = {}

    def dma(self, q, out_ap, in_ap, reads, target):
        reads = self._bufs(reads)
        tb = target.b if isinstance(target, TT) else target
        self._emit_waits(q, self._collect(reads, [tb]))
        s = self.dsem_of(tb)
        tb.dcnt += 16
        tok = (s, tb.dcnt)
        self.streams[q].append(("ins", lambda e: e.dma_start(out=out_ap, in_=in_ap), s, 16))
        for b in reads:
            if b.r.get(s, 0) < tok[1]:
                b.r[s] = tok[1]
        tb.w = tok
        tb.r = {}

    def collective(self, fn, reads, target):
        reads = self._bufs(reads)
        tb = target.b if isinstance(target, TT) else target
        self._emit_waits("pool", self._collect(reads, [tb]))
        s = self.dsem_of(tb)
        tb.dcnt += 1
        tok = (s, tb.dcnt)
        self.streams["pool"].append(("ins", fn, s, 1))
        for b in reads:
            if b.r.get(s, 0) < tok[1]:
                b.r[s] = tok[1]
        tb.w = tok
        tb.r = {}

    def carry(self, frm, to):
        m = {}
        for b in self._bufs(frm):
            if b.w is not None and m.get(b.w[0], 0) < b.w[1]:
                m[b.w[0]] = b.w[1]
            for s, v in b.r.items():
                if m.get(s, 0) < v:
                    m[s] = v
        for b in self._bufs(to):
            for s, v in m.items():
                if b.r.get(s, 0) < v:
                    b.r[s] = v
            if b.w is not None:
                pass

    def final_wait(self, eng, bufs):
        self._emit_waits(eng, self._collect(self._bufs(bufs), []))

    def init_psum(self):
        for i in range(8):
            t = self.es.enter_context(self.nc.psum_tensor("ps%d" % i, [128, 512], F32))
            self.psb.append(TT(t, Buf("ps%d" % i)))
        self.rot = list(range(8))

    def ps(self):
        i = self.rot[self.rot_i % len(self.rot)]
        self.rot_i += 1
        return self.psb[i]

    def acquire(self, k):
        got = self.rot[-k:]
        self.rot = self.rot[:-k]
        return [self.psb[i] for i in got], got

    def release(self, got):
        self.rot = self.rot + list(got)

    def replay(self, eng, handle):
        for it in self.streams[eng]:
            if it[0] == "wait":
                handle.wait_ge(self.semh[it[1]], it[2])
            else:
                ins = it[1](handle)
                ins.then_inc(self.semh[it[2]], it[3])


class _Stop(Exception):
    pass


def build_program(ntiles=NT, stage=None):
    nc = bass.Bass("TRN2", target_bir_lowering=False)
    es = ExitStack()
    with es:
        _build(nc, es, ntiles, stage)
    return nc


def _build(nc, es, ntiles, stage=None):
    P = Prog(nc, es)

    def din(name, shape, dt=F32):
        return nc.dram_tensor(name, list(shape), dt, kind="ExternalInput").ap()

    def dscr(name, shape, dt=BF16):
        return TT(nc.dram_tensor(name, list(shape), dt, kind="Internal").ap(), Buf(name))

    xs = din("xs", [SEQ, D])
    pos = din("pos", [1, SEQ], I32)
    flag_d = din("flag", [128, 1])
    cT_d = din("cT", [128, 8])
    ident_d = din("ident", [128, 128])
    mask_d = din("mask01", [128, 128])
    inv_d = din("inv_col", [64, 1])
    ada_w = din("ada_w", [2, D, 6 * D])
    ada_bA = din("ada_bA", [2, 4, 128, 8])
    ada_bG = din("ada_bG", [2, 2, 1, D])
    normA = din("normA", [2, 2, 128, 8])
    normG = din("normG", [2, 2, 1, D])
    kv_ada_w = din("kv_ada_w", [D, 2 * D])
    kv_ada_bA = din("kv_ada_bA", [2, 128, 8])
    kv_normA = din("kv_normA", [128, 8])
    a_w_in = din("a_w_in", [D, 3088])
    gate_b = din("gate_b", [2, 1, 8])
    head_g = din("head_g", [1, D])
    a_w_out = din("a_w_out", [D, D])
    kv_w_a = din("kv_w_a", [D, 320])
    kv_lat_g = din("kv_lat_g", [1, 256])
    kv_w_b = din("kv_w_b", [256, 2048])
    w_q_a = din("w_q_a", [D, 384])
    q_lat_g = din("q_lat_g", [1, 384])
    w_q_b = din("w_q_b", [384, 1536])
    b_w_out = din("b_w_out", [D, D])
    mlp_w1 = din("mlp_w1", [2, D, 4 * D])
    mlp_w2 = din("mlp_w2", [2, 4 * D, D])
    yout = nc.dram_tensor("y", [SEQ // 2, D], F32, kind="ExternalOutput").ap()
    yout_b = Buf("yout")

    wb_in = dscr("wb_in", [D, 3088])
    wb_out = dscr("wb_out", [D, D])
    wb_w1 = [dscr("wb_w1_%d" % l, [D, 4 * D]) for l in range(2)]
    wb_w2 = [dscr("wb_w2_%d" % l, [4 * D, D]) for l in range(2)]
    wb_a2 = dscr("wb_a2", [D, 384])
    wb_b = dscr("wb_b", [256, 2048])
    wb_qa = dscr("wb_qa", [D, 384])
    wb_qb2 = dscr("wb_qb2", [384, 8, 256])
    wb_bo = dscr("wb_bo", [D, D])
    kcache = dscr("kcache", [H, 128, SEQ])
    vcache = dscr("vcache", [H, 128, SEQ // 128, 128])
    xmid = dscr("xmid", [SEQ // 2, D], F32)
    lat_src = [dscr("lat_src%d" % i, [128, 1536]) for i in range(NT - FIRST_OWN)]
    lat_all = [dscr("lat_all%d" % i, [256, 1536]) for i in range(NT - FIRST_OWN)]

    ident_f = P.sb("ident_f", [128, 128], F32)
    ident_b = P.sb("ident_b", [128, 128], BF16)
    mask_f = P.sb("mask_f", [128, 128], F32)
    mask_b = P.sb("mask_b", [128, 128], BF16)
    ones_f = P.sb("ones_f", [128, 128], F32)
    ones_b = P.sb("ones_b", [128, 128], BF16)
    fones_b = P.sb("fones_b", [128, 128], BF16)
    flag = P.sb("flag", [128, 1], F32)
    cond = P.sb("cond", [128, 8], F32)
    inv_col = P.sb("inv_col", [64, 1], F32)
    vecA = P.sb("vecA", [128, 16, 8], F32)
    biasA = P.sb("biasA", [128, 10, 8], F32)
    nrmA = P.sb("nrmA", [128, 5, 8], F32)
    modA = P.sb("modA", [128, 5, 8], F32)
    modB = P.sb("modB", [128, 5, 8], F32)
    G = P.sb("G", [128, 4, D], F32)
    headg = P.sb("headg", [128, D], F32)
    latg = P.sb("latg", [128, 256], F32)
    qlatg = P.sb("qlatg", [128, 384], F32)
    gb = P.sb("gb", [128, 2, 8], F32)
    krT = P.sb("krT", [65, SEQ], BF16)
    WIF = P.sb("WIF", [128, 8, 16], BF16)
    Cf = P.sb("Cf", [128, 4, 130], F32)
    Cb = P.sb("Cb", [128, 4, 130], BF16)
    mst = P.sb("mst", [8, 1], F32)
    eps_t = P.sb("eps_t", [128, 1], F32)

    X = P.sb("X", [128, 4, D], F32)
    XN = P.sb("XN", [128, 4, D], BF16)
    HT = P.sb("HT", [128, 8, T], BF16)
    Ys = [P.sb("Y%d" % i, [128, D], F32) for i in range(2)]
    junk = P.sb("junk", [128, D], BF16)
    st4 = P.sb("st4", [128, 16], F32)
    ARENA_E = 31 * 1024
    arena = es.enter_context(nc.sbuf_tensor("arena", [128, ARENA_E], BF16))
    NWS = 4
    wslots = [P.sb("wslot%d" % i, [128, 4096], BF16) for i in range(NWS)]
    NKS = 2
    kslots = [P.sb("kslot%d" % i, [128, 4096], BF16) for i in range(NKS)]
    ANG = P.sb("ANG", [64, T], F32)
    KF = P.sb("KF", [64, T], F32)
    COS = P.sb("COS", [64, T], F32)
    SIN = P.sb("SIN", [64, T], F32)
    P.init_psum()

    state = {"ws": 0, "ks": 0, "aoff": 0}

    def aview(nelem_bf16, shape, dt, name):
        o = state["aoff"]
        assert o + nelem_bf16 <= ARENA_E, (name, o, nelem_bf16)
        state["aoff"] = o + nelem_bf16
        ap = arena[:, o:o + nelem_bf16]
        if dt == F32:
            ap = ap.bitcast(F32)
        if len(shape) == 3:
            ap = ap.rearrange("p (a b) -> p a b", a=shape[1])
        elif len(shape) == 4:
            ap = ap.rearrange("p (a b c) -> p a b c", a=shape[1], b=shape[2])
        return TT(ap, Buf(name))

    def wload(src_ap, a, bcols, src_buf):
        sl = wslots[state["ws"] % NWS]
        state["ws"] += 1
        v = sl.ap[:, 0:a * bcols].rearrange("p (a b) -> p a b", a=a)
        P.dma("sp", v, src_ap, [src_buf], sl)
        return TT(v, sl.b)

    def mm(psT, out_ap, lhsT, rhs, start, stop, reads):
        P.op("pe", lambda e: e.matmul(out_ap, lhsT=lhsT, rhs=rhs, start=start, stop=stop), reads=reads, writes=[psT])

    def trn(psT, out_ap, in_ap, idn, reads):
        P.op("pe", lambda e: e.transpose(out_ap, in_ap, idn.ap[:]), reads=list(reads) + [idn], writes=[psT])

    def act(out_ap, in_ap, func, reads, writes, scale=1.0, bias=None, accum=None):
        kw = {}
        if bias is not None:
            kw["bias"] = bias
        if accum is not None:
            kw["accum_out"] = accum
        P.op("act", lambda e: e.activation(out=out_ap, in_=in_ap, func=func, scale=scale, **kw), reads=reads, writes=writes)

    def tcopy(eng, out_ap, in_ap, reads, writes):
        P.op(eng, lambda e: e.tensor_copy(out=out_ap, in_=in_ap), reads=reads, writes=writes)

    def tt(eng, out_ap, in0, in1, op, reads, writes):
        P.op(eng, lambda e: e.tensor_tensor(out=out_ap, in0=in0, in1=in1, op=op), reads=reads, writes=writes)

    def ts(eng, out_ap, in0, s1, s2, op0, op1, reads, writes):
        if s2 is None:
            P.op(eng, lambda e: e.tensor_scalar(out=out_ap, in0=in0, scalar1=s1, scalar2=None, op0=op0), reads=reads, writes=writes)
        else:
            P.op(eng, lambda e: e.tensor_scalar(out=out_ap, in0=in0, scalar1=s1, scalar2=s2, op0=op0, op1=op1), reads=reads, writes=writes)

    def stt(eng, out_ap, in0, scalar, in1, op0, op1, reads, writes):
        P.op(eng, lambda e: e.scalar_tensor_tensor(out=out_ap, in0=in0, scalar=scalar, in1=in1, op0=op0, op1=op1), reads=reads, writes=writes)

    def memset(eng, ap, val, writes):
        P.op(eng, lambda e: e.memset(ap, val), reads=[], writes=writes)

    def rstd_from_ss(out_ap, ss_ap, n, rd, wr):
        act(out_ap, ss_ap, AF.Sqrt, rd + [eps_t], wr, scale=1.0 / n, bias=eps_t.ap[0:ss_ap.shape[0], 0:1])
        P.op("dve", lambda e: e.reciprocal(out=out_ap, in_=out_ap), reads=wr, writes=wr)

    def chk(n):
        if stage is not None and stage == n:
            raise _Stop()

    def body():
        P.dma("sp", ident_f.ap[:], ident_d[:, :], [], ident_f)
        P.dma("sp", mask_f.ap[:], mask_d[:, :], [], mask_f)
        P.dma("sp", flag.ap[:], flag_d[:, :], [], flag)
        P.dma("sp", cond.ap[:], cT_d[:, :], [], cond)
        P.dma("sp", inv_col.ap[:], inv_d[:, :], [], inv_col)
        P.dma("sp", biasA.ap[:, 0:8, :], ada_bA.rearrange("l v p k -> p (l v) k"), [], biasA)
        P.dma("sp", biasA.ap[:, 8:10, :], kv_ada_bA.rearrange("v p k -> p v k"), [], biasA)
        P.dma("sp", nrmA.ap[:, 0:4, :], normA.rearrange("l v p k -> p (l v) k"), [], nrmA)
        P.dma("sp", nrmA.ap[:, 4, :], kv_normA[:, :], [], nrmA)
        P.dma("sp", headg.ap[:], head_g[0:1, :].partition_broadcast(128), [], headg)
        P.dma("sp", latg.ap[:], kv_lat_g[0:1, :].partition_broadcast(128), [], latg)
        P.dma("sp", qlatg.ap[:], q_lat_g[0:1, :].partition_broadcast(128), [], qlatg)
        for i in range(2):
            P.dma("sp", gb.ap[:, i, :], gate_b[i, 0:1, :].partition_broadcast(128), [], gb)
        tcopy("dve", ident_b.ap[:], ident_f.ap[:], [ident_f], [ident_b])
        tcopy("dve", mask_b.ap[:], mask_f.ap[:], [mask_f], [mask_b])
        memset("dve", ones_f.ap[:], 1.0, [ones_f])
        memset("dve", ones_b.ap[:], 1.0, [ones_b])
        memset("dve", eps_t.ap[:], EPS, [eps_t])
        ts("dve", fones_b.ap[:], ones_f.ap[:], flag.ap[:, 0:1], None, ALU.mult, None, [ones_f, flag], [fones_b])
        memset("pool", krT.ap[64:65, :], 1.0, [krT])
        memset("pool", Cf.ap[:], 0.0, [Cf])
        memset("pool", mst.ap[:], 0.0, [mst])
        act(cond.ap[:], cond.ap[:], AF.Silu, [cond], [cond])

        def cast_rows(dst, src, nrows, step=256):
            for r0 in range(0, nrows, step):
                r1 = min(nrows, r0 + step)
                P.dma("pool", dst.ap[r0:r1], src[r0:r1], [], dst)

        cast_rows(wb_in, a_w_in, D)
        P.dma("pool", wb_a2.ap[:, 0:320], kv_w_a[:, :], [], wb_a2)
        P.dma("pool", wb_qb2.ap[:, :, 0:192], w_q_b.rearrange("r (h c) -> r h c", h=8), [], wb_qb2)
        cast_rows(wb_out, a_w_out, D)
        cast_rows(wb_w1[0], mlp_w1[0], D)
        cast_rows(wb_w2[0], mlp_w2[0], 4 * D)
        cast_rows(wb_b, kv_w_b, 256)
        cast_rows(wb_qa, w_q_a, D)
        cast_rows(wb_bo, b_w_out, D)
        cast_rows(wb_w1[1], mlp_w1[1], D)
        cast_rows(wb_w2[1], mlp_w2[1], 4 * D)

        state["aoff"] = 0
        cond_rep = aview(8 * 128 * 2, [128, 8, 128], F32, "cond_rep")
        aslab = [aview(8 * 512 * 2, [128, 8, 512], F32, "aslab%d" % i) for i in range(2)]
        brow = aview(512 * 2, [128, 512], F32, "brow")
        nrow = aview(512 * 2, [128, 512], F32, "nrow")
        rtmp = aview(8 * 64 * 2, [128, 8, 64], F32, "rtmp")
        rtmpb = aview(8 * 64, [128, 8, 64], BF16, "rtmpb")
        rq = aview(3 * 512 * 2, [128, 3, 8, 64], F32, "rq")
        rqb = aview(3 * 512, [128, 3, 8, 64], BF16, "rqb")
        prol_bufs = [cond_rep, brow, nrow, rtmp, rtmpb, rq, rqb] + aslab

        for k in range(8):
            tcopy("dve", cond_rep.ap[:, k, :], cond.ap[:, k:k + 1].to_broadcast([128, 128]), [cond], [cond_rep])

        P.dma("sp", rtmp.ap[:], kv_w_a[:, 256:320].rearrange("(k p) c -> p k c", p=128), [], rtmp)
        ts("dve", rtmpb.ap[:, :, 0:32], rtmp.ap[:, :, 32:64], -1.0, None, ALU.mult, None, [rtmp], [rtmpb])
        tcopy("dve", rtmpb.ap[:, :, 32:64], rtmp.ap[:, :, 0:32], [rtmp], [rtmpb])
        P.dma("act", wb_a2.ap[:, 320:384].rearrange("(k p) c -> p k c", p=128), rtmpb.ap[:], [rtmpb], wb_a2)
        for c3 in range(3):
            P.dma("sp", rq.ap[:, c3], w_q_b[c3 * 128:(c3 + 1) * 128, :].rearrange("p (h c) -> p h c", h=8)[:, :, 128:192], [], rq)
        ts("dve", rqb.ap[:, :, :, 0:32], rq.ap[:, :, :, 32:64], -1.0, None, ALU.mult, None, [rq], [rqb])
        tcopy("dve", rqb.ap[:, :, :, 32:64], rq.ap[:, :, :, 0:32], [rq], [rqb])
        for c3 in range(3):
            P.dma("act", wb_qb2.ap[c3 * 128:(c3 + 1) * 128, :, 192:256], rqb.ap[:, c3], [rqb], wb_qb2)

        chk(0)
        def ada_group(wsrc, col0, kind, idx, li, sidx):
            ab = state["abuf"]
            sl = ab["slabs"][state["as"] % len(ab["slabs"])]
            brow, nrow = ab["brow"], ab["nrow"]
            state["as"] += 1
            P.dma("sp", sl.ap[:], wsrc[:, col0:col0 + 512].rearrange("(k p) n -> p k n", p=128), [], sl)
            half = (col0 % 1024) // 512
            pt = P.ps()
            if kind == "A":
                for j in range(4):
                    for k in range(8):
                        mm(pt, pt.ap[:, j:j + 1], sl.ap[:, k, j * 128:(j + 1) * 128], cond.ap[:, k:k + 1], k == 0, k == 7, [sl, cond])
                tcopy("dve", vecA.ap[:, idx, half * 4:half * 4 + 4], pt.ap[:, 0:4], [pt], [vecA])
            else:
                for k in range(8):
                    mm(pt, pt.ap[:, :], cond_rep.ap[:, k, :], sl.ap[:, k, :], k == 0, k == 7, [sl, cond_rep])
                P.dma("sp", brow.ap[:], ada_bG[li, sidx, 0:1, half * 512:(half + 1) * 512].partition_broadcast(128), [], brow)
                P.dma("sp", nrow.ap[:], normG[li, sidx, 0:1, half * 512:(half + 1) * 512].partition_broadcast(128), [], nrow)
                tt("dve", brow.ap[:], pt.ap[:, :], brow.ap[:], ALU.add, [pt, brow], [brow])
                tt("dve", G.ap[:, idx, half * 512:(half + 1) * 512], brow.ap[:], nrow.ap[:], ALU.mult, [brow, nrow], [G])

        state["as"] = 0
        state["abuf"] = {"slabs": aslab, "brow": brow, "nrow": nrow}

        def ada_layer_groups(l):
            gl = []
            for v in range(6):
                for half in range(2):
                    col0 = v * 1024 + half * 512
                    if v in (2, 5):
                        gl.append((ada_w[l], col0, "G", l * 2 + (0 if v == 2 else 1), l, 0 if v == 2 else 1))
                    else:
                        gl.append((ada_w[l], col0, "A", l * 4 + {0: 0, 1: 1, 3: 2, 4: 3}[v], l, 0))
            return gl

        def ada_finish(i0, i1, mods):
            tt("dve", vecA.ap[:, i0:i1, :], vecA.ap[:, i0:i1, :], biasA.ap[:, i0:i1, :], ALU.add, [vecA, biasA], [vecA])
            for (mi, shi, sci) in mods:
                stt("dve", modA.ap[:, mi, :], vecA.ap[:, sci, :], 1.0, nrmA.ap[:, mi, :], ALU.add, ALU.mult, [vecA, nrmA], [modA])
                tcopy("dve", modB.ap[:, mi, :], vecA.ap[:, shi, :], [vecA], [modB])

        for g in ada_layer_groups(0):
            ada_group(*g)
        ada_finish(0, 4, ((0, 0, 1), (1, 2, 3)))
        deferred = [(kv_ada_w, v * 1024 + half * 512, "A", 8 + v, 0, 0) for v in range(2) for half in range(2)] + ada_layer_groups(1)
        if stage == 1:
            tcopy("dve", X.ap[:, :, :], G.ap[:, :, :], [G], [X])
            tcopy("dve", X.ap[:, 0, 0:40], modA.ap[:].rearrange("p a k -> p (a k)"), [modA], [X])
            tcopy("dve", X.ap[:, 0, 40:80], modB.ap[:].rearrange("p a k -> p (a k)"), [modB], [X])
        chk(1)
        P.dma("sp", WIF.ap[:], wb_in.ap[:, 2048:2064].rearrange("(k p) n -> p k n", p=128), [wb_in], WIF)

        state["aoff"] = 0
        QT = aview(4 * T, [128, 4, T], BF16, "QT")
        KT = aview(4 * T, [128, 4, T], BF16, "KT")
        KE = aview(4 * 4 * 128, [128, 4, 4, 128], BF16, "KE")
        KO = aview(4 * 4 * 128, [128, 4, 4, 128], BF16, "KO")
        VE = aview(4 * 8 * 130, [128, 4, 8, 130], BF16, "VE")
        VW = aview(4 * 8 * 130, [128, 4, 8, 130], BF16, "VW")
        OG = aview(4 * D, [128, 4, D], BF16, "OG")
        STt = aview(8 * 128, [128, 8, 128], BF16, "ST")
        HH = aview(2 * D, [128, 8, 128], F32, "HH")
        SQ = aview(2 * D, [128, 8, 128], F32, "SQ")
        GT = aview(2 * 256, [128, 256], F32, "GT")
        GH = aview(2 * 768, [128, 768], F32, "GH")
        M0 = [QT, KT, KE, KO, VE, VW, OG, STt, HH, SQ, GT, GH]
        _save = state["aoff"]
        state["aoff"] = 2048
        browL = aview(1024, [128, 512], F32, "browL")
        nrowL = aview(1024, [128, 512], F32, "nrowL")
        state["aoff"] = 16512
        aslabL = aview(8 * 512 * 2, [128, 8, 512], F32, "aslabL")
        state["aoff"] = _save
        LATE = [browL, nrowL, aslabL, cond_rep]
        m0_end = state["aoff"]
        state["aoff"] = 0
        HID = aview(32 * T, [128, 32, T], BF16, "HID")
        RT = [aview(T * 2, [128, T], F32, "RT%d" % i) for i in range(2)]
        MLPB = [HID] + RT
        state["aoff"] = 0
        CKV = aview(4 * 256 * 2, [128, 4, 256], F32, "CKV")
        CKN = aview(4 * 256, [128, 4, 256], BF16, "CKN")
        CKT = aview(2 * T, [128, 2, T], BF16, "CKT")
        KEXP = aview(8 * T, [128, 8, T], BF16, "KEXP")
        VEXP = aview(8 * T, [128, 8, 4, 128], BF16, "VEXP")
        R1 = aview(T * 2, [128, T], F32, "R1")
        R2 = aview(T * 2, [128, T], F32, "R2")
        KVB = [CKV, CKN, CKT, KEXP, VEXP, R1, R2]
        kv_end = state["aoff"]
        state["aoff"] = 0
        CQ = aview(4 * 384 * 2, [128, 4, 384], F32, "CQ")
        CQN = aview(4 * 384, [128, 4, 384], BF16, "CQN")
        CQT = aview(3 * T, [128, 3, T], BF16, "CQT")
        QNT = aview(8 * T, [128, 8, T], BF16, "QNT")
        QRT = aview(8 * T, [128, 8, T], BF16, "QRT")
        OT = aview(8 * T, [128, 8, T], BF16, "OT")
        PTs = [aview(T, [128, T], BF16, "PT%d" % i) for i in range(4)]
        RDEN = aview(T * 2, [128, T], F32, "RDEN")
        AR1 = aview(T * 2, [128, T], F32, "AR1")
        AR2 = aview(T * 2, [128, T], F32, "AR2")
        ATTB = [CQ, CQN, CQT, QNT, QRT, OT, RDEN, AR1, AR2] + PTs

        P.carry(prol_bufs, M0)
        P.carry(prol_bufs, [browL, nrowL, aslabL])

        def norm_to_HT(mi):
            memset("dve", st4.ap[:, 0:4], 0.0, [st4])
            for s in range(4):
                act(junk.ap[:], X.ap[:, s, :], AF.Square, [X], [junk, st4], accum=st4.ap[:, s:s + 1])
            rstd_from_ss(st4.ap[:, 0:4], st4.ap[:, 0:4], D, [st4], [st4])
            for s in range(4):
                if s % 2 == 0:
                    ts("dve", XN.ap[:, s, :], X.ap[:, s, :], st4.ap[:, s:s + 1], None, ALU.mult, None, [X, st4], [XN])
                else:
                    act(XN.ap[:, s, :], X.ap[:, s, :], AF.Identity, [X, st4], [XN], scale=st4.ap[:, s:s + 1])
            transpose_to_HT(XN, modA.ap[:, mi, :], modB.ap[:, mi, :], [modA, modB])

        def transpose_to_HT(src, a_ap, b_ap, extra):
            for k in range(8):
                pt = P.ps()
                pv = pt.ap[:].bitcast(BF16)
                for s in range(4):
                    trn(pt, pv[:, s * 128:(s + 1) * 128], src.ap[:, s, k * 128:(k + 1) * 128], ident_b, [src])
                if a_ap is not None:
                    if k % 2 == 0:
                        act(HT.ap[:, k, :], pv[:, 0:T], AF.Identity, [pt] + extra, [HT], scale=a_ap[:, k:k + 1], bias=b_ap[:, k:k + 1])
                    else:
                        ts("dve", HT.ap[:, k, :], pv[:, 0:T], a_ap[:, k:k + 1], b_ap[:, k:k + 1], ALU.mult, ALU.add, [pt] + extra, [HT])
                else:
                    if k % 2 == 0:
                        act(HT.ap[:, k, :], pv[:, 0:T], AF.Copy, [pt], [HT])
                    else:
                        tcopy("dve", HT.ap[:, k, :], pv[:, 0:T], [pt], [HT])

        def out_proj_load(wsrc):
            w0 = wload(wsrc.ap[:, 0:512].rearrange("(k p) n -> p k n", p=128), 8, 512, wsrc)
            w1 = wload(wsrc.ap[:, 512:1024].rearrange("(k p) n -> p k n", p=128), 8, 512, wsrc)
            return (w0, w1)

        def out_proj_sub(s, ws, lhs_of, gi, lhs_reads, light_dve=False):
            Y = Ys[s % 2]
            for half, w in enumerate(ws):
                pt = P.ps()
                for k in range(8):
                    mm(pt, pt.ap[:, :], lhs_of(k, s), w.ap[:, k, :], k == 0, k == 7, lhs_reads + [w])
                finish_half(pt, Y, s, half, light_dve)
            residual_update(Y, s, gi, light_dve)

        def out_proj_residual(lhs_of, wsrc, gi, lhs_reads):
            ws = out_proj_load(wsrc)
            for s in range(4):
                out_proj_sub(s, ws, lhs_of, gi, lhs_reads)

        def finish_half(pt, Y, s, half, light_dve=False):
            if half == 0:
                memset("dve", st4.ap[:, 8:10], 0.0, [st4])
            act(junk.ap[:, 0:512], pt.ap[:, :], AF.Square, [pt], [junk, st4], accum=st4.ap[:, 8 + half:9 + half])
            if light_dve:
                act(Y.ap[:, half * 512:(half + 1) * 512], pt.ap[:, :], AF.Copy, [pt], [Y])
            else:
                tcopy("dve", Y.ap[:, half * 512:(half + 1) * 512], pt.ap[:, :], [pt], [Y])

        def residual_update(Y, s, gi, light_dve=False):
            tt("dve", st4.ap[:, 10:11], st4.ap[:, 8:9], st4.ap[:, 9:10], ALU.add, [st4], [st4])
            rstd_from_ss(st4.ap[:, 10:11], st4.ap[:, 10:11], D, [st4], [st4])
            stt("dve", Y.ap[:], Y.ap[:], st4.ap[:, 10:11], G.ap[:, gi, :], ALU.mult, ALU.mult, [Y, st4, G], [Y])
            tt("pool" if light_dve else "dve", X.ap[:, s, :], X.ap[:, s, :], Y.ap[:], ALU.add, [X, Y], [X])

        def mlp(l, mi, gi, hook=None):
            norm_to_HT(mi)
            if hook is not None:
                hook()
            for g8 in range(8):
                w = wload(wb_w1[l].ap[:, g8 * 512:(g8 + 1) * 512].rearrange("(k p) n -> p k n", p=128), 8, 512, wb_w1[l])
                for j in range(4):
                    m = g8 * 4 + j
                    pt = P.ps()
                    for k in range(8):
                        mm(pt, pt.ap[:, :], w.ap[:, k, j * 128:(j + 1) * 128], HT.ap[:, k, :], k == 0, k == 7, [w, HT])
                    rt = RT[m % 2]
                    act(rt.ap[:], pt.ap[:, :], AF.Relu, [pt], [rt])
                    tt("pool", HID.ap[:, m, :], rt.ap[:], rt.ap[:], ALU.mult, [rt], [HID])
            memset("dve", st4.ap[:, 4:8], 0.0, [st4])
            memset("dve", st4.ap[:, 12:16], 0.0, [st4])
            for half in range(2):
                acc, got = P.acquire(4)
                for kg in range(4):
                    w = wload(wb_w2[l].ap[kg * 1024:(kg + 1) * 1024, half * 512:(half + 1) * 512].rearrange("(k p) n -> p k n", p=128), 8, 512, wb_w2[l])
                    for kk in range(8):
                        kc = kg * 8 + kk
                        for s in range(4):
                            mm(acc[s], acc[s].ap[:, :], HID.ap[:, kc, s * 128:(s + 1) * 128], w.ap[:, kk, :], kc == 0, kc == 31, [HID, w])
                for s in range(4):
                    finish_half_mlp(acc[s], s, half)
                P.release(got)
            for s in range(4):
                residual_update_mlp(s, gi)

        state["aoff"] = 32 * T + 2 * T * 2
        Y2 = aview(4 * D * 2, [128, 4, D], F32, "Y2") if state["aoff"] + 4 * D * 2 <= ARENA_E else None
        assert Y2 is not None
        MLPB.append(Y2)

        def finish_half_mlp(pt, s, half):
            act(junk.ap[:, 0:512], pt.ap[:, :], AF.Square, [pt], [junk, st4], accum=st4.ap[:, ((4 + s) if half == 0 else (12 + s)):((5 + s) if half == 0 else (13 + s))])
            tcopy("dve", Y2.ap[:, s, half * 512:(half + 1) * 512], pt.ap[:, :], [pt], [Y2])

        def residual_update_mlp(s, gi):
            tt("dve", st4.ap[:, 10:11], st4.ap[:, 4 + s:5 + s], st4.ap[:, 12 + s:13 + s], ALU.add, [st4], [st4])
            rstd_from_ss(st4.ap[:, 10:11], st4.ap[:, 10:11], D, [st4], [st4])
            stt("dve", Y2.ap[:, s, :], Y2.ap[:, s, :], st4.ap[:, 10:11], G.ap[:, gi, :], ALU.mult, ALU.mult, [Y2, st4, G], [Y2])
            tt("dve", X.ap[:, s, :], X.ap[:, s, :], Y2.ap[:, s, :], ALU.add, [X, Y2], [X])

        def mixer0(mt, state_only=False):
            memset("pool", VE.ap[:, :, :, 128:129], 1.0, [VE])
            memset("pool", KE.ap[:, :, :, 64:128], 0.0, [KE])
            memset("pool", KO.ap[:, :, :, 0:64], 0.0, [KO])
            norm_to_HT(0)
            pg = P.ps()
            for s in range(4):
                for k in range(8):
                    mm(pg, pg.ap[:, s * 16:(s + 1) * 16], HT.ap[:, k, s * 128:(s + 1) * 128], WIF.ap[:, k, :], k == 0, k == 7, [HT, WIF])
            gi_v = GT.ap[:, 0:32].rearrange("p (s h) -> p s h", s=4)
            gf_v = GT.ap[:, 32:64].rearrange("p (s h) -> p s h", s=4)
            pgv = pg.ap[:, 0:64].rearrange("p (s g) -> p s g", s=4)
            tt("dve", gi_v, pgv[:, :, 0:8], gb.ap[:, 0:1, :].to_broadcast([128, 4, 8]), ALU.add, [pg, gb], [GT])
            tt("dve", gf_v, pgv[:, :, 8:16], gb.ap[:, 1:2, :].to_broadcast([128, 4, 8]), ALU.add, [pg, gb], [GT])
            act(gf_v, gf_v, AF.Exp, [GT], [GT], scale=-1.0)
            act(gf_v, gf_v, AF.Ln, [GT, ones_f], [GT], scale=1.0, bias=ones_f.ap[:, 0:1])
            ts("dve", gf_v, gf_v, -1.0, None, ALU.mult, None, [GT], [GT])
            pb = P.ps()
            for s in range(4):
                mm(pb, pb.ap[:, s * 8:(s + 1) * 8], mask_f.ap[:], GT.ap[:, 32 + s * 8:32 + (s + 1) * 8], True, True, [mask_f, GT])
            bb_v = GT.ap[:, 64:96].rearrange("p (s h) -> p s h", s=4)
            a_v = GT.ap[:, 96:128].rearrange("p (s h) -> p s h", s=4)
            tcopy("dve", GT.ap[:, 64:96], pb.ap[:, 0:32], [pb], [GT])
            tt("dve", GT.ap[:, 96:128], GT.ap[:, 0:32], GT.ap[:, 64:96], ALU.subtract, [GT], [GT])
            pa = P.ps()
            pbt = P.ps()
            for s in range(4):
                trn(pa, pa.ap[0:8, s * 128:(s + 1) * 128], GT.ap[:, 96 + s * 8:96 + (s + 1) * 8], ident_f, [GT])
                trn(pbt, pbt.ap[0:8, s * 128:(s + 1) * 128], GT.ap[:, 64 + s * 8:64 + (s + 1) * 8], ident_f, [GT])
            P.op("dve", lambda e: e.tensor_reduce(out=GH.ap[0:8, 0:4], in_=pa.ap[0:8, 0:512].rearrange("p (s t) -> p s t", s=4), axis=AX.X, op=ALU.max), reads=[pa], writes=[GH])
            tcopy("dve", GH.ap[0:8, 4:8], pbt.ap[0:8, 0:512].rearrange("p (s t) -> p s t", s=4)[:, :, 127], [pbt], [GH])
            for c in range(4):
                tt("dve", GH.ap[0:8, 8 + c:9 + c], GH.ap[0:8, c:c + 1], mst.ap[:], ALU.max, [GH, mst], [GH])
                tt("dve", GH.ap[0:8, 12 + c:13 + c], mst.ap[:], GH.ap[0:8, 8 + c:9 + c], ALU.subtract, [GH, mst], [GH])
                tt("dve", mst.ap[:], GH.ap[0:8, 4 + c:5 + c], GH.ap[0:8, 8 + c:9 + c], ALU.add, [GH], [mst])
            act(GH.ap[0:8, 12:16], GH.ap[0:8, 12:16], AF.Exp, [GH], [GH])
            Rv = GH.ap[0:8, 32:96].rearrange("p (c x) -> p c x", c=4)
            for c in range(4):
                ts("dve", Rv[:, c, 0:8], ident_f.ap[0:8, 0:8], GH.ap[0:8, 8 + c:9 + c], None, ALU.mult, None, [ident_f, GH], [GH])
                ts("dve", Rv[:, c, 8:12], ident_f.ap[0:8, 0:8].rearrange("p (j r) -> p j r", r=2)[:, :, 0], GH.ap[0:8, 12 + c:13 + c], None, ALU.mult, None, [ident_f, GH], [GH])
                ts("dve", Rv[:, c, 12:16], ident_f.ap[0:8, 0:8].rearrange("p (j r) -> p j r", r=2)[:, :, 1], GH.ap[0:8, 12 + c:13 + c], None, ALU.mult, None, [ident_f, GH], [GH])
            pm = P.ps()
            mm(pm, pm.ap[:, 0:64], ones_f.ap[0:8, :], GH.ap[0:8, 32:96], True, True, [ones_f, GH])
            pmv = pm.ap[:, 0:64].rearrange("p (c x) -> p c x", c=4)
            w_v = GT.ap[:, 128:160].rearrange("p (s h) -> p s h", s=4)
            e_v = GT.ap[:, 160:192].rearrange("p (s h) -> p s h", s=4)
            dec_v = GT.ap[:, 192:224].rearrange("p (c x) -> p c x", c=4)
            tt("dve", w_v, a_v, pmv[:, :, 0:8], ALU.subtract, [GT, pm], [GT])
            stt("dve", e_v, bb_v, -1.0, pmv[:, :, 0:8], ALU.mult, ALU.subtract, [GT, pm], [GT])
            tcopy("dve", dec_v, pmv[:, :, 8:16], [pm], [GT])
            act(GT.ap[:, 128:192], GT.ap[:, 128:192], AF.Exp, [GT], [GT])

            if stage == 2:
                for s_ in range(4):
                    tcopy("dve", X.ap[:, s_, :], HT.ap[:, 2 * s_:2 * s_ + 2, :].rearrange("p a t -> p (a t)"), [HT], [X])
                tcopy("dve", X.ap[:, 0, 0:256], GT.ap[:], [GT], [X])
            chk(2)
            if not state_only:
                wq = wload(wb_in.ap[:, 0:512].rearrange("(k p) n -> p k n", p=128), 8, 512, wb_in)
                for m in range(4):
                    pt = P.ps()
                    for k in range(8):
                        mm(pt, pt.ap[:, :], wq.ap[:, k, m * 128:(m + 1) * 128], HT.ap[:, k, :], k == 0, k == 7, [wq, HT])
                    act(QT.ap[:, m, :], pt.ap[:, :], AF.Copy, [pt], [QT], scale=0.125)
            for hv in range(2):
                wv = wload(wb_in.ap[:, 1024 + hv * 512:1024 + (hv + 1) * 512].rearrange("(k p) n -> p k n", p=128), 8, 512, wb_in)
                for s in range(4):
                    pt = P.ps()
                    for k in range(8):
                        mm(pt, pt.ap[:, :], HT.ap[:, k, s * 128:(s + 1) * 128], wv.ap[:, k, :], k == 0, k == 7, [wv, HT])
                    ptv = pt.ap[:, :].rearrange("p (h e) -> p h e", h=4)
                    tcopy("dve", VE.ap[:, s, hv * 4:(hv + 1) * 4, 0:128], ptv, [pt], [VE])
            if not state_only:
                for ho in range(2):
                    wo = wload(wb_in.ap[:, 2064 + ho * 512:2064 + (ho + 1) * 512].rearrange("(k p) n -> p k n", p=128), 8, 512, wb_in)
                    for s in range(4):
                        pt = P.ps()
                        for k in range(8):
                            mm(pt, pt.ap[:, :], HT.ap[:, k, s * 128:(s + 1) * 128], wo.ap[:, k, :], k == 0, k == 7, [wo, HT])
                        act(OG.ap[:, s, ho * 512:(ho + 1) * 512], pt.ap[:, :], AF.Sigmoid, [pt], [OG])
            wk = wload(wb_in.ap[:, 512:1024].rearrange("(k p) n -> p k n", p=128), 8, 512, wb_in)
            if not state_only:
                for m in range(4):
                    pt = P.ps()
                    for k in range(8):
                        mm(pt, pt.ap[:, :], wk.ap[:, k, m * 128:(m + 1) * 128], HT.ap[:, k, :], k == 0, k == 7, [wk, HT])
                    tcopy("dve", KT.ap[:, m, :], pt.ap[:, :], [pt], [KT])
            for s in range(4):
                pt = P.ps()
                for k in range(8):
                    mm(pt, pt.ap[:, :], HT.ap[:, k, s * 128:(s + 1) * 128], wk.ap[:, k, :], k == 0, k == 7, [wk, HT])
                ptv = pt.ap[:, :].rearrange("p (j r d) -> p j r d", j=4, r=2)
                tcopy("dve", KE.ap[:, s, :, 0:64], ptv[:, :, 0, :], [pt], [KE])
                tcopy("dve", KO.ap[:, s, :, 64:128], ptv[:, :, 1, :], [pt], [KO])
            if not state_only:
                for s in range(4):
                    tt("pool", OG.ap[:, s, :], OG.ap[:, s, :], headg.ap[:], ALU.mult, [OG, headg], [OG])
            for s in range(4):
                tt("dve", VW.ap[:, s], VE.ap[:, s], w_v[:, s, :].unsqueeze(2).to_broadcast([128, 8, 130]), ALU.mult, [VE, GT], [VW])

            chk(3)
            wo_slabs = None if state_only else out_proj_load(wb_out)

            def emit_outproj(s_):
                ptT = P.ps()
                pvT = ptT.ap[:].bitcast(BF16)
                for k in range(8):
                    trn(ptT, pvT[:, k * 128:(k + 1) * 128], XN.ap[:, s_, k * 128:(k + 1) * 128], ident_b, [XN])
                act(HT.ap[:, :, s_ * 128:(s_ + 1) * 128], pvT[:, 0:1024].rearrange("p (k t) -> p k t", k=8), AF.Copy, [ptT], [HT])
                out_proj_sub(s_, wo_slabs, lambda k, s2: HT.ap[:, k, s2 * 128:(s2 + 1) * 128], 0, [HT], light_dve=True)
            for s in range(4):
                sc = slice(s * 128, (s + 1) * 128)
                tt("dve", Cf.ap[0:64], Cf.ap[0:64], dec_v[0:64, s, 0:4].unsqueeze(2).to_broadcast([64, 4, 130]), ALU.mult, [Cf, GT], [Cf])
                tt("dve", Cf.ap[64:128], Cf.ap[64:128], dec_v[64:128, s, 4:8].unsqueeze(2).to_broadcast([64, 4, 130]), ALU.mult, [Cf, GT], [Cf])
                if not state_only:
                    act(Cb.ap[:], Cf.ap[:], AF.Copy, [Cf], [Cb])
                for r in (range(2) if not state_only else ()):
                    pS = P.ps()
                    for j in range(4):
                        mm(pS, pS.ap[:, j * 128:(j + 1) * 128], KT.ap[64 * r:64 * r + 64, j, sc], QT.ap[64 * r:64 * r + 64, j, sc], True, True, [KT, QT])
                    tt("dve", STt.ap[:, :, :].rearrange("p (j r) t -> p j r t", r=2)[:, :, r, :],
                       pS.ap[:, :].rearrange("p (j t) -> p j t", j=4), mask_b.ap[:, None, :].to_broadcast([128, 4, 128]), ALU.mult, [pS, mask_b], [STt])
                for (h0, nh) in (((0, 3), (3, 3), (6, 2)) if not state_only else ()):
                    pN = P.ps()
                    for hh in range(nh):
                        h = h0 + hh
                        j, r = h // 2, h % 2
                        mm(pN, pN.ap[:, hh * 129:(hh + 1) * 129], QT.ap[64 * r:64 * r + 64, j, sc], Cb.ap[64 * r:64 * r + 64, j, 0:129], True, False, [QT, Cb])
                        mm(pN, pN.ap[:, hh * 129:(hh + 1) * 129], STt.ap[:, h, :], VW.ap[:, s, h, 0:129], False, True, [STt, VW])
                    pNv = pN.ap[:, 0:nh * 129].rearrange("p (h e) -> p h e", h=nh)
                    dd = GT.ap[:, 224:224 + nh]
                    act(dd, pNv[:, :, 128], AF.Abs, [pN], [GT])
                    tt("dve", dd, dd, e_v[:, s, h0:h0 + nh], ALU.max, [GT], [GT])
                    P.op("dve", lambda e, dd=dd: e.reciprocal(out=dd, in_=dd), reads=[GT], writes=[GT])
                    tt("dve", HH.ap[:, h0:h0 + nh, :], pNv[:, :, 0:128], dd.unsqueeze(2).to_broadcast([128, nh, 128]), ALU.mult, [pN, GT], [HH])
                for jb in range(2):
                    pC = P.ps()
                    for jj in range(2):
                        j = jb * 2 + jj
                        mm(pC, pC.ap[:, jj * 129:(jj + 1) * 129], KE.ap[:, s, j, :], VW.ap[:, s, 2 * j, 0:129], True, False, [KE, VW])
                        mm(pC, pC.ap[:, jj * 129:(jj + 1) * 129], KO.ap[:, s, j, :], VW.ap[:, s, 2 * j + 1, 0:129], False, True, [KO, VW])
                    tt("dve", Cf.ap[:, jb * 2:jb * 2 + 2, 0:129], Cf.ap[:, jb * 2:jb * 2 + 2, 0:129], pC.ap[:, 0:258].rearrange("p (j e) -> p j e", j=2), ALU.add, [Cf, pC], [Cf])
                if state_only:
                    continue
                tt("pool", SQ.ap[:], HH.ap[:], HH.ap[:], ALU.mult, [HH], [SQ])
                P.op("dve", lambda e: e.tensor_reduce(out=GT.ap[:, 232:240], in_=SQ.ap[:], axis=AX.X, op=ALU.add), reads=[SQ], writes=[GT])
                rstd_from_ss(GT.ap[:, 232:240], GT.ap[:, 232:240], 128, [GT], [GT])
                tt("dve", HH.ap[:], HH.ap[:], GT.ap[:, 232:240].unsqueeze(2).to_broadcast([128, 8, 128]), ALU.mult, [HH, GT], [HH])
                tt("dve", XN.ap[:, s, :].rearrange("p (h e) -> p h e", h=8), HH.ap[:], OG.ap[:, s, :].rearrange("p (h e) -> p h e", h=8), ALU.mult, [HH, OG], [XN])
                if s >= 1:
                    emit_outproj(s - 1)
            if state_only:
                return
            emit_outproj(3)
            chk(5)

        def rope_tables(mt):
            t0 = mt * T
            ji = junk.ap[0:64, :].bitcast(I32)
            P.dma("sp", ji, pos[0:1, t0:t0 + T].partition_broadcast(64), [], junk)
            tcopy("dve", ANG.ap[:], ji, [junk], [ANG])
            ts("dve", ANG.ap[:], ANG.ap[:], inv_col.ap[:, 0:1], None, ALU.mult, None, [ANG, inv_col], [ANG])
            for (dst, shift) in ((SIN, 0.0), (COS, PI / 2)):
                ts("dve", dst.ap[:], ANG.ap[:], shift, None, ALU.add, None, [ANG], [dst])
                ts("dve", KF.ap[:], dst.ap[:], 1.0 / TWO_PI, None, ALU.mult, None, [dst], [KF])
                tcopy("dve", ji, KF.ap[:], [KF], [junk])
                tcopy("dve", KF.ap[:], ji, [junk], [KF])
                stt("dve", dst.ap[:], KF.ap[:], -6.28125, dst.ap[:], ALU.mult, ALU.add, [KF, dst], [dst])
                stt("dve", dst.ap[:], KF.ap[:], -0.0019353071795864769, dst.ap[:], ALU.mult, ALU.add, [KF, dst], [dst])
                ts("dve", KF.ap[:], dst.ap[:], PI, None, ALU.is_gt, None, [dst], [KF])
                stt("dve", dst.ap[:], KF.ap[:], -TWO_PI, dst.ap[:], ALU.mult, ALU.add, [KF, dst], [dst])
                ts("dve", KF.ap[:], dst.ap[:], -PI, None, ALU.is_lt, None, [dst], [KF])
                stt("dve", dst.ap[:], KF.ap[:], TWO_PI, dst.ap[:], ALU.mult, ALU.add, [KF, dst], [dst])
                act(dst.ap[:], dst.ap[:], AF.Sin, [dst], [dst])

        def kv_phase(mt):
            t0 = mt * T
            norm_to_HT(4)
            wa = wload(wb_a2.ap[:, :].rearrange("(k p) n -> p k n", p=128), 8, 384, wb_a2)
            memset("dve", st4.ap[:, 0:4], 0.0, [st4])
            for s in range(4):
                pt = P.ps()
                for k in range(8):
                    mm(pt, pt.ap[:, 0:256], HT.ap[:, k, s * 128:(s + 1) * 128], wa.ap[:, k, 0:256], k == 0, k == 7, [HT, wa])
                act(junk.ap[:, 0:256], pt.ap[:, 0:256], AF.Square, [pt], [junk, st4], accum=st4.ap[:, s:s + 1])
                tcopy("dve", CKV.ap[:, s, :], pt.ap[:, 0:256], [pt], [CKV])
            rstd_from_ss(st4.ap[:, 0:4], st4.ap[:, 0:4], 256, [st4], [st4])
            for s in range(4):
                stt("dve", CKN.ap[:, s, :], CKV.ap[:, s, :], st4.ap[:, s:s + 1], latg.ap[:], ALU.mult, ALU.mult, [CKV, st4, latg], [CKN])
            for c in range(2):
                pt = P.ps()
                pv = pt.ap[:].bitcast(BF16)
                for s in range(4):
                    trn(pt, pv[:, s * 128:(s + 1) * 128], CKN.ap[:, s, c * 128:(c + 1) * 128], ident_b, [CKN])
                tcopy("dve", CKT.ap[:, c, :], pv[:, 0:T], [pt], [CKT])
            pA = P.ps()
            pB = P.ps()
            for k in range(8):
                mm(pA, pA.ap[0:64, :], wa.ap[:, k, 256:320], HT.ap[:, k, :], k == 0, k == 7, [HT, wa])
            for k in range(8):
                mm(pB, pB.ap[0:64, :], wa.ap[:, k, 320:384], HT.ap[:, k, :], k == 0, k == 7, [HT, wa])
            tt("dve", R1.ap[0:64], pA.ap[0:64, :], COS.ap[0:64], ALU.mult, [pA, COS], [R1])
            tt("dve", R2.ap[0:64], pB.ap[0:64, :], SIN.ap[0:64], ALU.mult, [pB, SIN], [R2])
            tt("pool", krT.ap[0:64, t0:t0 + T], R1.ap[0:64], R2.ap[0:64], ALU.add, [R1, R2], [krT])
            ti = mt - FIRST_OWN
            P.dma("act", lat_src[ti].ap[:, 0:1024].rearrange("p (c t) -> p c t", c=2), CKT.ap[:], [CKT], lat_src[ti])
            P.dma("act", lat_src[ti].ap[0:64, 1024:1536], krT.ap[0:64, t0:t0 + T], [krT], lat_src[ti])
            P.collective(lambda e, ti=ti: e.collective_compute("AllGather", ALU.bypass, replica_groups=[[0, 1], [2, 3], [4, 5], [6, 7]],
                                                               ins=[lat_src[ti].ap[:, :]], outs=[lat_all[ti].ap[:, :]]), [lat_src[ti]], lat_all[ti])
            expand_kv(mt)

        def expand_kv(mt):
            t0 = mt * T
            wbk = wload(wb_b.ap[:, :].rearrange("(c p) n -> p c n", p=128), 2, 2048, wb_b)
            wbv = wbk.ap[:, :, :].rearrange("p c (h x) -> p c h x", h=8)
            for h in range(8):
                pt = P.ps()
                for c in range(2):
                    mm(pt, pt.ap[:, :], wbv[:, c, h, 0:128], CKT.ap[:, c, :], c == 0, c == 1, [wbk, CKT])
                if h % 2 == 0:
                    act(KEXP.ap[:, h, :], pt.ap[:, :], AF.Copy, [pt], [KEXP])
                else:
                    tcopy("dve", KEXP.ap[:, h, :], pt.ap[:, :], [pt], [KEXP])
            for s in range(4):
                for hv in range(2):
                    pt = P.ps()
                    for hh in range(4):
                        for c in range(2):
                            h = hv * 4 + hh
                            mm(pt, pt.ap[:, hh * 128:(hh + 1) * 128], CKT.ap[:, c, s * 128:(s + 1) * 128], wbv[:, c, h, 128:256], c == 0, c == 1, [wbk, CKT])
                    ptv = pt.ap[:, :].rearrange("p (h e) -> p h e", h=4)
                    if hv == 0:
                        act(VEXP.ap[:, hv * 4:(hv + 1) * 4, s, :], ptv, AF.Copy, [pt], [VEXP])
                    else:
                        tcopy("dve", VEXP.ap[:, hv * 4:(hv + 1) * 4, s, :], ptv, [pt], [VEXP])
            for h in range(8):
                P.dma("act", kcache.ap[h, :, t0:t0 + T], KEXP.ap[:, h, :], [KEXP], kcache)
                P.dma("act", vcache.ap[h, :, mt * 4:(mt + 1) * 4, :], VEXP.ap[:, h, :, :], [VEXP], vcache)

        def mixer1(mt):
            t0 = mt * T
            norm_to_HT(2)
            wqa = wload(wb_qa.ap[:, :].rearrange("(k p) n -> p k n", p=128), 8, 384, wb_qa)
            memset("dve", st4.ap[:, 0:4], 0.0, [st4])
            for s in range(4):
                pt = P.ps()
                for k in range(8):
                    mm(pt, pt.ap[:, 0:384], HT.ap[:, k, s * 128:(s + 1) * 128], wqa.ap[:, k, :], k == 0, k == 7, [HT, wqa])
                act(junk.ap[:, 0:384], pt.ap[:, 0:384], AF.Square, [pt], [junk, st4], accum=st4.ap[:, s:s + 1])
                tcopy("dve", CQ.ap[:, s, :], pt.ap[:, 0:384], [pt], [CQ])
            rstd_from_ss(st4.ap[:, 0:4], st4.ap[:, 0:4], 384, [st4], [st4])
            for s in range(4):
                stt("dve", CQN.ap[:, s, :], CQ.ap[:, s, :], st4.ap[:, s:s + 1], qlatg.ap[:], ALU.mult, ALU.mult, [CQ, st4, qlatg], [CQN])
            for c in range(3):
                pt = P.ps()
                pv = pt.ap[:].bitcast(BF16)
                for s in range(4):
                    trn(pt, pv[:, s * 128:(s + 1) * 128], CQN.ap[:, s, c * 128:(c + 1) * 128], ident_b, [CQN])
                tcopy("dve", CQT.ap[:, c, :], pv[:, 0:T], [pt], [CQT])
            memset("pool", QRT.ap[64:65, :, :], 0.0, [QRT])
            for hg in range(2):
                wqb = wload(wb_qb2.ap[:, hg * 4:(hg + 1) * 4, :].rearrange("(c p) h x -> p c (h x)", p=128), 3, 1024, wb_qb2)
                wv = wqb.ap[:, :, :].rearrange("p c (h x) -> p c h x", h=4)
                for hh in range(4):
                    h = hg * 4 + hh
                    pt = P.ps()
                    for c in range(3):
                        mm(pt, pt.ap[:, :], wv[:, c, hh, 0:128], CQT.ap[:, c, :], c == 0, c == 2, [wqb, CQT])
                    act(QNT.ap[:, h, :], pt.ap[:, :], AF.Copy, [pt], [QNT])
                    pA = P.ps()
                    pB = P.ps()
                    for c in range(3):
                        mm(pA, pA.ap[0:64, :], wv[:, c, hh, 128:192], CQT.ap[:, c, :], c == 0, c == 2, [wqb, CQT])
                    for c in range(3):
                        mm(pB, pB.ap[0:64, :], wv[:, c, hh, 192:256], CQT.ap[:, c, :], c == 0, c == 2, [wqb, CQT])
                    tt("dve", AR1.ap[0:64], pA.ap[0:64, :], COS.ap[0:64], ALU.mult, [pA, COS], [AR1])
                    tt("dve", AR2.ap[0:64], pB.ap[0:64, :], SIN.ap[0:64], ALU.mult, [pB, SIN], [AR2])
                    tt("pool", QRT.ap[0:64, h, :], AR1.ap[0:64], AR2.ap[0:64], ALU.add, [AR1, AR2], [QRT])
            nkb = 4 * (mt + 1)
            npiece = (nkb + 15) // 16
            accs, got = P.acquire(4)
            LOOK = 2
            pieces = [(h, pc) for h in range(8) for pc in range(npiece)]
            loaded = {}

            def load_piece(idx):
                if idx >= len(pieces) or idx in loaded:
                    return
                h, pc = pieces[idx]
                kb0 = pc * 16
                nb = min(16, nkb - kb0)
                sl = kslots[state["ks"] % NKS]
                state["ks"] += 1
                kv_k = sl.ap[:, 0:nb * 128]
                kv_v = sl.ap[:, 2048:2048 + nb * 128].rearrange("p (b e) -> p b e", b=nb)
                P.dma("sp", kv_k, kcache.ap[h, :, kb0 * 128:(kb0 + nb) * 128], [kcache], sl)
                P.dma("sp", kv_v, vcache.ap[h, :, kb0:kb0 + nb, :], [vcache], sl)
                loaded[idx] = (sl, kv_k, kv_v)

            blocks = []
            for idx, (h, pc) in enumerate(pieces):
                kb0 = pc * 16
                for bi in range(min(16, nkb - kb0)):
                    blocks.append((h, idx, bi, kb0 + bi))
            nblk = len(blocks)
            pend = {}

            def emit_S(i):
                h, idx, bi, kb = blocks[i]
                if bi == 0:
                    load_piece(idx)
                if bi == LOOK:
                    load_piece(idx + 1)
                sl, kv_k, kv_v = loaded[idx]
                pS = P.ps()
                mm(pS, pS.ap[:, :], kv_k[:, bi * 128:(bi + 1) * 128], QNT.ap[:, h, :], True, False, [sl, QNT])
                mm(pS, pS.ap[:, :], krT.ap[0:65, kb * 128:(kb + 1) * 128], QRT.ap[0:65, h, :], False, True, [krT, QRT])
                PT = PTs[i % 4]
                act(PT.ap[:], pS.ap[:, :], AF.Exp, [pS], [PT], scale=ATT_SCALE)
                jd = kb - 4 * mt
                if jd >= 0:
                    if jd > 0:
                        memset("pool", PT.ap[:, 0:jd * 128], 0.0, [PT])
                    tt("pool", PT.ap[:, jd * 128:(jd + 1) * 128], PT.ap[:, jd * 128:(jd + 1) * 128], mask_b.ap[:], ALU.mult, [PT, mask_b], [PT])
                pend[i] = PT

            def emit_PV(i):
                h, idx, bi, kb = blocks[i]
                sl, kv_k, kv_v = loaded[idx]
                pO, pD = accs[2 * (h % 2)], accs[2 * (h % 2) + 1]
                PT = pend.pop(i)
                first = kb == 0
                last = kb == nkb - 1
                mm(pO, pO.ap[:, :], kv_v[:, bi, :], PT.ap[:], first, last, [sl, PT])
                onesT = fones_b if kb < 4 * FIRST_OWN else ones_b
                mm(pD, pD.ap[:, :], onesT.ap[:], PT.ap[:], first, last, [onesT, PT])
                if last:
                    P.op("dve", lambda e, pD=pD: e.reciprocal(out=RDEN.ap[:], in_=pD.ap[:, :]), reads=[pD], writes=[RDEN])
                    tt("dve", OT.ap[:, h, :], pO.ap[:, :], RDEN.ap[:], ALU.mult, [pO, RDEN], [OT])

            for i in range(nblk + LOOK):
                if i < nblk:
                    emit_S(i)
                if i - LOOK >= 0:
                    emit_PV(i - LOOK)
            P.release(got)
            out_proj_residual(lambda k, s: OT.ap[:, k, s * 128:(s + 1) * 128], wb_bo, 2, [OT])

        def load_x(src_ap, reads):
            P.dma("sp", X.ap[:], src_ap.rearrange("(s p) d -> p s d", p=128), reads, X)

        state["abuf"] = {"slabs": [aslabL], "brow": browL, "nrow": nrowL}
        for mt in range(FIRST_OWN):
            load_x(xs[mt * T:(mt + 1) * T, :], [])
            if deferred:
                ada_group(*deferred.pop(0))
            mixer0(mt, state_only=True)
            if deferred:
                ada_group(*deferred.pop(0))
        while deferred:
            ada_group(*deferred.pop(0))
        ada_finish(4, 10, ((2, 4, 5), (3, 6, 7), (4, 8, 9)))
        P.carry(LATE, M0)
        ts("dve", Cf.ap[:], Cf.ap[:], flag.ap[:, 0:1], None, ALU.mult, None, [Cf, flag], [Cf])
        ts("dve", mst.ap[:], mst.ap[:], flag.ap[0:8, 0:1], None, ALU.mult, None, [mst, flag], [mst])

        prev = M0
        for mt in range(FIRST_OWN, NT):
            load_x(xs[mt * T:(mt + 1) * T, :], [])
            if prev is not M0:
                P.carry(prev, M0)
            mixer0(mt)
            P.carry(M0, MLPB)
            mlp(0, 1, 1, hook=lambda: rope_tables(mt))
            o0 = (mt - FIRST_OWN) * T
            P.dma("act", xmid.ap[o0:o0 + T, :].rearrange("(s p) d -> p s d", p=128), X.ap[:], [X], xmid)
            P.carry(MLPB, KVB)
            kv_phase(mt)
            prev = KVB

        for mt in range(FIRST_OWN):
            P.dma("sp", krT.ap[0:64, mt * T:(mt + 1) * T], lat_all[mt].ap[0:64, 1024:1536], [lat_all[mt]], krT)
            P.dma("sp", CKT.ap[:], lat_all[mt].ap[0:128, 0:1024].rearrange("p (c t) -> p c t", c=2), [lat_all[mt]], CKT)
            ts("dve", CKT.ap[:], CKT.ap[:], flag.ap[:, 0:1], None, ALU.mult, None, [CKT, flag], [CKT])
            expand_kv(mt)

        rope_tables(FIRST_OWN)
        prev = KVB
        for mt in range(FIRST_OWN, NT):
            o0 = (mt - FIRST_OWN) * T
            load_x(xmid.ap[o0:o0 + T, :], [xmid])
            P.carry(prev, ATTB)
            mixer1(mt)
            P.carry(ATTB, MLPB)
            mlp(1, 3, 3, hook=(lambda: rope_tables(mt + 1)) if mt + 1 < NT else None)
            prev = MLPB
            P.dma("act", yout[o0:o0 + T, :].rearrange("(s p) d -> p s d", p=128), X.ap[:], [X], yout_b)
    try:
        body()
    except _Stop:
        P.dma("pool", yout[0:T, :].rearrange("(s p) d -> p s d", p=128), X.ap[:], [X], yout_b)
    P.final_wait("pool", [yout_b])
    allb = [Buf("fin")]
    block = es.enter_context(nc.Block())

    @block.tensor
    def _(e):
        P.replay("pe", e)

    @block.scalar
    def _(e):
        P.replay("act", e)

    @block.vector
    def _(e):
        P.replay("dve", e)

    @block.gpsimd
    def _(e):
        P.replay("pool", e)

    @block.sync
    def _(e):
        P.replay("sp", e)


_CACHE = {}


def _prep_inputs(x, c, positions, ada_w, ada_b, norm_g, a_w_in, a_gate_b, a_head_g, a_w_out,
                 kv_ada_w, kv_ada_b, kv_norm_g, kv_w_a, kv_latent_g, kv_w_b, b_w_q_a, b_q_latent_g,
                 b_w_q_b, b_w_out, mlp_w1, mlp_w2):
    f = np.float32

    def pk(v):
        return np.ascontiguousarray(np.asarray(v, f).reshape(8, 128).T)

    ident = np.eye(128, dtype=f)
    mask01 = np.triu(np.ones((128, 128), f))
    half = 32
    inv = (10000.0 ** (-np.arange(half, dtype=f) / half)).astype(f)
    inv_col = np.concatenate([inv, inv]).reshape(64, 1).astype(f)
    ada_b = np.asarray(ada_b, f)
    norm_g = np.asarray(norm_g, f)
    ada_bA = np.stack([np.stack([pk(ada_b[l, v * 1024:(v + 1) * 1024]) for v in (0, 1, 3, 4)]) for l in range(2)])
    ada_bG = np.stack([np.stack([ada_b[l, v * 1024:(v + 1) * 1024].reshape(1, 1024) for v in (2, 5)]) for l in range(2)])
    normA = np.stack([np.stack([pk(norm_g[l, v]) for v in (0, 2)]) for l in range(2)])
    normG = np.stack([np.stack([norm_g[l, v].reshape(1, 1024) for v in (1, 3)]) for l in range(2)])
    kv_ada_b = np.asarray(kv_ada_b, f)
    shared = {
        "ident": ident, "mask01": mask01, "inv_col": inv_col,
        "ada_w": np.ascontiguousarray(ada_w, f), "ada_bA": np.ascontiguousarray(ada_bA), "ada_bG": np.ascontiguousarray(ada_bG),
        "normA": np.ascontiguousarray(normA), "normG": np.ascontiguousarray(normG),
        "kv_ada_w": np.ascontiguousarray(kv_ada_w, f),
        "kv_ada_bA": np.ascontiguousarray(np.stack([pk(kv_ada_b[0:1024]), pk(kv_ada_b[1024:2048])])),
        "kv_normA": pk(kv_norm_g),
        "a_w_in": np.ascontiguousarray(a_w_in[0], f),
        "gate_b": np.ascontiguousarray(np.asarray(a_gate_b[0], f).reshape(2, 1, 8)),
        "head_g": np.ascontiguousarray(np.asarray(a_head_g[0], f).reshape(1, 1024)),
        "a_w_out": np.ascontiguousarray(a_w_out[0], f),
        "kv_w_a": np.ascontiguousarray(kv_w_a, f),
        "kv_lat_g": np.ascontiguousarray(np.asarray(kv_latent_g, f).reshape(1, 256)),
        "kv_w_b": np.ascontiguousarray(kv_w_b, f),
        "w_q_a": np.ascontiguousarray(b_w_q_a[0], f),
        "q_lat_g": np.ascontiguousarray(np.asarray(b_q_latent_g[0], f).reshape(1, 384)),
        "w_q_b": np.ascontiguousarray(b_w_q_b[0], f),
        "b_w_out": np.ascontiguousarray(b_w_out[0], f),
        "mlp_w1": np.ascontiguousarray(mlp_w1, f),
        "mlp_w2": np.ascontiguousarray(mlp_w2, f),
    }
    x = np.asarray(x, f)
    positions = np.asarray(positions, np.int32)
    in_maps = []
    for core in range(8):
        b, hf = core // 2, core % 2
        if hf == 1:
            xs = x[b]
            ps = positions[b]
        else:
            xs = np.concatenate([np.zeros((SEQ // 2, D), f), x[b, :SEQ // 2]], axis=0)
            ps = np.concatenate([np.zeros((SEQ // 2,), np.int32), positions[b, :SEQ // 2]])
        m = dict(shared)
        m["xs"] = np.ascontiguousarray(xs)
        m["pos"] = np.ascontiguousarray(ps.reshape(1, SEQ))
        m["flag"] = np.full((128, 1), float(hf), f)
        m["cT"] = pk(np.asarray(c, f)[b])
        in_maps.append(m)
    return in_maps


def kernel(**inputs):
    in_maps = _prep_inputs(**inputs)
    if "nc" not in _CACHE:
        _CACHE["nc"] = build_program(NT)
    nc = _CACHE["nc"]
    res = run_bass_kernel_spmd(nc, in_maps, core_ids=list(range(8)))
    out = np.zeros((4, SEQ, D), np.float32)
    for core in range(8):
        b, hf = core // 2, core % 2
        out[b, hf * (SEQ // 2):(hf + 1) * (SEQ // 2)] = res.results[core]["y"]
    return out
```

```python
import numpy as np
from contextlib import ExitStack
import concourse.bass as bass
import concourse.mybir as mybir
from concourse.bass_utils import run_bass_kernel_spmd

F32 = mybir.dt.float32
BF16 = mybir.dt.bfloat16
I32 = mybir.dt.int32
AF = mybir.ActivationFunctionType
ALU = mybir.AluOpType
AX = mybir.AxisListType

D = 1024
SEQ = 8192
T = 512
NT = 16
FIRST_OWN = 8
EPS = 1e-6
H = 8
SAME_ENG_SYNC = True
TWO_PI = 6.283185307179586
PI = 3.141592653589793
ATT_SCALE = 192.0 ** -0.5

ENGS = ("pe", "act", "dve", "pool", "sp")


class Buf:
    __slots__ = ("name", "w", "r", "dsem", "dcnt")

    def __init__(self, name):
        self.name = name
        self.w = None
        self.r = {}
        self.dsem = None
        self.dcnt = 0


class TT:
    def __init__(self, ap, buf):
        self.ap = ap
        self.b = buf

    def __getitem__(self, k):
        return self.ap[k]


class Prog:
    def __init__(self, nc, es):
        self.nc = nc
        self.es = es
        self.streams = {e: [] for e in ENGS}
        self.esem = {e: es.enter_context(nc.semaphore("es_" + e)) for e in ENGS}
        self.ecnt = {e: 0 for e in ENGS}
        self.waited = {e: {} for e in ENGS}
        self.semh = {}
        for e in ENGS:
            self.semh["es_" + e] = self.esem[e]
        self.nbuf = 0
        self.psb = []
        self.rot = []
        self.rot_i = 0

    def sb(self, name, shape, dt):
        t = self.es.enter_context(self.nc.sbuf_tensor("s_" + name, list(shape), dt))
        return TT(t, Buf(name))

    def view(self, ap, name):
        return TT(ap, Buf(name))

    def dsem_of(self, buf):
        if buf.dsem is None:
            nm = "ds%d" % len(self.semh)
            buf.dsem = nm
            self.semh[nm] = self.es.enter_context(self.nc.semaphore(nm))
        return buf.dsem

    def _collect(self, reads, writes, eng=None):
        deps = {}
        own = None if eng is None else "es_" + eng

        def add(tok, raw):
            if tok is None:
                return
            s, v = tok
            if s == own and not raw:
                return
            if deps.get(s, 0) < v:
                deps[s] = v

        for b in reads:
            add(b.w, True)
        for b in writes:
            add(b.w, False)
            for s, v in b.r.items():
                add((s, v), False)
        return deps

    def _emit_waits(self, eng, deps):
        own = "es_" + eng
        for s, v in deps.items():
            if s == own and (eng in ("pe", "sp") or not SAME_ENG_SYNC):
                continue
            if self.waited[eng].get(s, 0) >= v:
                continue
            self.waited[eng][s] = v
            self.streams[eng].append(("wait", s, v))

    @staticmethod
    def _bufs(lst):
        return [x.b if isinstance(x, TT) else x for x in lst]

    def op(self, eng, fn, reads=(), writes=()):
        reads = self._bufs(reads)
        writes = self._bufs(writes)
        if eng != "pe":
            writes = writes + [b for b in reads if b.name.startswith("ps") and b not in writes]
        self._emit_waits(eng, self._collect(reads, writes, eng))
        self.ecnt[eng] += 1
        tok = ("es_" + eng, self.ecnt[eng])
        self.streams[eng].append(("ins", fn, tok[0], 1))
        for b in reads:
            if b.r.get(tok[0], 0) < tok[1]:
                b.r[tok[0]] = tok[1]
        for b in writes:
            b.w = tok
            b.r = {}

    def dma(self, q, out_ap, in_ap, reads, target):
        reads = self._bufs(reads)
        tb = target.b if isinstance(target, TT) else target
        self._emit_waits(q, self._collect(reads, [tb]))
        s = self.dsem_of(tb)
        tb.dcnt += 16
        tok = (s, tb.dcnt)
        self.streams[q].append(("ins", lambda e: e.dma_start(out=out_ap, in_=in_ap), s, 16))
        for b in reads:
            if b.r.get(s, 0) < tok[1]:
                b.r[s] = tok[1]
        tb.w = tok
        tb.r = {}

    def collective(self, fn, reads, target):
        reads = self._bufs(reads)
        tb = target.b if isinstance(target, TT) else target
        self._emit_waits("pool", self._collect(reads, [tb]))
        s = self.dsem_of(tb)
        tb.dcnt += 1
        tok = (s, tb.dcnt)
        self.streams["pool"].append(("ins", fn, s, 1))
        for b in reads:
            if b.r.get(s, 0) < tok[1]:
                b.r[s] = tok[1]
        tb.w = tok
        tb.r = {}

    def carry(self, frm, to):
        m = {}
        for b in self._bufs(frm):
            if b.w is not None and m.get(b.w[0], 0) < b.w[1]:
                m[b.w[0]] = b.w[1]
            for s, v in b.r.items():
                if m.get(s, 0) < v:
                    m[s] = v
        for b in self._bufs(to):
            for s, v in m.items():
                if b.r.get(s, 0) < v:
                    b.r[s] = v
            if b.w is not None:
                pass

    def final_wait(self, eng, bufs):
        self._emit_waits(eng, self._collect(self._bufs(bufs), []))

    def init_psum(self):
        for i in range(8):
            t = self.es.enter_context(self.nc.psum_tensor("ps%d" % i, [128, 512], F32))
            self.psb.append(TT(t, Buf("ps%d" % i)))
        self.rot = list(range(8))

    def ps(self):
        i = self.rot[self.rot_i % len(self.rot)]
        self.rot_i += 1
        return self.psb[i]

    def acquire(self, k):
        got = self.rot[-k:]
        self.rot = self.rot[:-k]
        return [self.psb[i] for i in got], got

    def release(self, got):
        self.rot = self.rot + list(got)

    def replay(self, eng, handle):
        for it in self.streams[eng]:
            if it[0] == "wait":
                handle.wait_ge(self.semh[it[1]], it[2])
            else:
                ins = it[1](handle)
                ins.then_inc(self.semh[it[2]], it[3])


class _Stop(Exception):
    pass


def build_program(ntiles=NT, stage=None):
    nc = bass.Bass("TRN2", target_bir_lowering=False)
    es = ExitStack()
    with es:
        _build(nc, es, ntiles, stage)
    return nc


def _build(nc, es, ntiles, stage=None):
    P = Prog(nc, es)

    def din(name, shape, dt=F32):
        return nc.dram_tensor(name, list(shape), dt, kind="ExternalInput").ap()

    def dscr(name, shape, dt=BF16):
        return TT(nc.dram_tensor(name, list(shape), dt, kind="Internal").ap(), Buf(name))

    xs = din("xs", [SEQ, D])
    pos = din("pos", [1, SEQ], I32)
    flag_d = din("flag", [128, 1])
    cT_d = din("cT", [128, 8])
    ident_d = din("ident", [128, 128])
    mask_d = din("mask01", [128, 128])
    inv_d = din("inv_col", [64, 1])
    ada_w = din("ada_w", [2, D, 6 * D])
    ada_bA = din("ada_bA", [2, 4, 128, 8])
    ada_bG = din("ada_bG", [2, 2, 1, D])
    normA = din("normA", [2, 2, 128, 8])
    normG = din("normG", [2, 2, 1, D])
    kv_ada_w = din("kv_ada_w", [D, 2 * D])
    kv_ada_bA = din("kv_ada_bA", [2, 128, 8])
    kv_normA = din("kv_normA", [128, 8])
    a_w_in = din("a_w_in", [D, 3088])
    gate_b = din("gate_b", [2, 1, 8])
    head_g = din("head_g", [1, D])
    a_w_out = din("a_w_out", [D, D])
    kv_w_a = din("kv_w_a", [D, 320])
    kv_lat_g = din("kv_lat_g", [1, 256])
    kv_w_b = din("kv_w_b", [256, 2048])
    w_q_a = din("w_q_a", [D, 384])
    q_lat_g = din("q_lat_g", [1, 384])
    w_q_b = din("w_q_b", [384, 1536])
    b_w_out = din("b_w_out", [D, D])
    mlp_w1 = din("mlp_w1", [2, D, 4 * D])
    mlp_w2 = din("mlp_w2", [2, 4 * D, D])
    yout = nc.dram_tensor("y", [SEQ // 2, D], F32, kind="ExternalOutput").ap()
    yout_b = Buf("yout")

    wb_in = dscr("wb_in", [D, 3088])
    wb_out = dscr("wb_out", [D, D])
    wb_w1 = [dscr("wb_w1_%d" % l, [D, 4 * D]) for l in range(2)]
    wb_w2 = [dscr("wb_w2_%d" % l, [4 * D, D]) for l in range(2)]
    wb_a2 = dscr("wb_a2", [D, 384])
    wb_b = dscr("wb_b", [256, 2048])
    wb_qa = dscr("wb_qa", [D, 384])
    wb_qb2 = dscr("wb_qb2", [384, 8, 256])
    wb_bo = dscr("wb_bo", [D, D])
    kcache = dscr("kcache", [H, 128, SEQ])
    vcache = dscr("vcache", [H, 128, SEQ // 128, 128])
    xmid = dscr("xmid", [SEQ // 2, D], F32)
    lat_src = [dscr("lat_src%d" % i, [128, 1536]) for i in range(NT - FIRST_OWN)]
    lat_all = [dscr("lat_all%d" % i, [256, 1536]) for i in range(NT - FIRST_OWN)]

    ident_f = P.sb("ident_f", [128, 128], F32)
    ident_b = P.sb("ident_b", [128, 128], BF16)
    mask_f = P.sb("mask_f", [128, 128], F32)
    mask_b = P.sb("mask_b", [128, 128], BF16)
    ones_f = P.sb("ones_f", [128, 128], F32)
    ones_b = P.sb("ones_b", [128, 128], BF16)
    fones_b = P.sb("fones_b", [128, 128], BF16)
    flag = P.sb("flag", [128, 1], F32)
    cond = P.sb("cond", [128, 8], F32)
    inv_col = P.sb("inv_col", [64, 1], F32)
    vecA = P.sb("vecA", [128, 16, 8], F32)
    biasA = P.sb("biasA", [128, 10, 8], F32)
    nrmA = P.sb("nrmA", [128, 5, 8], F32)
    modA = P.sb("modA", [128, 5, 8], F32)
    modB = P.sb("modB", [128, 5, 8], F32)
    G = P.sb("G", [128, 4, D], F32)
    headg = P.sb("headg", [128, D], F32)
    latg = P.sb("latg", [128, 256], F32)
    qlatg = P.sb("qlatg", [128, 384], F32)
    gb = P.sb("gb", [128, 2, 8], F32)
    krT = P.sb("krT", [65, SEQ], BF16)
    WIF = P.sb("WIF", [128, 8, 16], BF16)
    Cf = P.sb("Cf", [128, 4, 130], F32)
    Cb = P.sb("Cb", [128, 4, 130], BF16)
    mst = P.sb("mst", [8, 1], F32)
    eps_t = P.sb("eps_t", [128, 1], F32)

    X = P.sb("X", [128, 4, D], F32)
    XN = P.sb("XN", [128, 4, D], BF16)
    HT = P.sb("HT", [128, 8, T], BF16)
    Ys = [P.sb("Y%d" % i, [128, D], F32) for i in range(2)]
    junk = P.sb("junk", [128, D], BF16)
    st4 = P.sb("st4", [128, 16], F32)
    ARENA_E = 31 * 1024
    arena = es.enter_context(nc.sbuf_tensor("arena", [128, ARENA_E], BF16))
    NWS = 4
    wslots = [P.sb("wslot%d" % i, [128, 4096], BF16) for i in range(NWS)]
    NKS = 2
    kslots = [P.sb("kslot%d" % i, [128, 4096], BF16) for i in range(NKS)]
    ANG = P.sb("ANG", [64, T], F32)
    KF = P.sb("KF", [64, T], F32)
    COS = P.sb("COS", [64, T], F32)
    SIN = P.sb("SIN", [64, T], F32)
    P.init_psum()

    state = {"ws": 0, "ks": 0, "aoff": 0}

    def aview(nelem_bf16, shape, dt, name):
        o = state["aoff"]
        assert o + nelem_bf16 <= ARENA_E, (name, o, nelem_bf16)
        state["aoff"] = o + nelem_bf16
        ap = arena[:, o:o + nelem_bf16]
        if dt == F32:
            ap = ap.bitcast(F32)
        if len(shape) == 3:
            ap = ap.rearrange("p (a b) -> p a b", a=shape[1])
        elif len(shape) == 4:
            ap = ap.rearrange("p (a b c) -> p a b c", a=shape[1], b=shape[2])
        return TT(ap, Buf(name))

    def wload(src_ap, a, bcols, src_buf):
        sl = wslots[state["ws"] % NWS]
        state["ws"] += 1
        v = sl.ap[:, 0:a * bcols].rearrange("p (a b) -> p a b", a=a)
        P.dma("sp", v, src_ap, [src_buf], sl)
        return TT(v, sl.b)

    def mm(psT, out_ap, lhsT, rhs, start, stop, reads):
        P.op("pe", lambda e: e.matmul(out_ap, lhsT=lhsT, rhs=rhs, start=start, stop=stop), reads=reads, writes=[psT])

    def trn(psT, out_ap, in_ap, idn, reads):
        P.op("pe", lambda e: e.transpose(out_ap, in_ap, idn.ap[:]), reads=list(reads) + [idn], writes=[psT])

    def act(out_ap, in_ap, func, reads, writes, scale=1.0, bias=None, accum=None):
        kw = {}
        if bias is not None:
            kw["bias"] = bias
        if accum is not None:
            kw["accum_out"] = accum
        P.op("act", lambda e: e.activation(out=out_ap, in_=in_ap, func=func, scale=scale, **kw), reads=reads, writes=writes)

    def tcopy(eng, out_ap, in_ap, reads, writes):
        P.op(eng, lambda e: e.tensor_copy(out=out_ap, in_=in_ap), reads=reads, writes=writes)

    def tt(eng, out_ap, in0, in1, op, reads, writes):
        P.op(eng, lambda e: e.tensor_tensor(out=out_ap, in0=in0, in1=in1, op=op), reads=reads, writes=writes)

    def ts(eng, out_ap, in0, s1, s2, op0, op1, reads, writes):
        if s2 is None:
            P.op(eng, lambda e: e.tensor_scalar(out=out_ap, in0=in0, scalar1=s1, scalar2=None, op0=op0), reads=reads, writes=writes)
        else:
            P.op(eng, lambda e: e.tensor_scalar(out=out_ap, in0=in0, scalar1=s1, scalar2=s2, op0=op0, op1=op1), reads=reads, writes=writes)

    def stt(eng, out_ap, in0, scalar, in1, op0, op1, reads, writes):
        P.op(eng, lambda e: e.scalar_tensor_tensor(out=out_ap, in0=in0, scalar=scalar, in1=in1, op0=op0, op1=op1), reads=reads, writes=writes)

    def memset(eng, ap, val, writes):
        P.op(eng, lambda e: e.memset(ap, val), reads=[], writes=writes)

    def rstd_from_ss(out_ap, ss_ap, n, rd, wr):
        act(out_ap, ss_ap, AF.Sqrt, rd + [eps_t], wr, scale=1.0 / n, bias=eps_t.ap[0:ss_ap.shape[0], 0:1])
        P.op("dve", lambda e: e.reciprocal(out=out_ap, in_=out_ap), reads=wr, writes=wr)

    def chk(n):
        if stage is not None and stage == n:
            raise _Stop()

    def body():
        P.dma("sp", ident_f.ap[:], ident_d[:, :], [], ident_f)
        P.dma("sp", mask_f.ap[:], mask_d[:, :], [], mask_f)
        P.dma("sp", flag.ap[:], flag_d[:, :], [], flag)
        P.dma("sp", cond.ap[:], cT_d[:, :], [], cond)
        P.dma("sp", inv_col.ap[:], inv_d[:, :], [], inv_col)
        P.dma("sp", biasA.ap[:, 0:8, :], ada_bA.rearrange("l v p k -> p (l v) k"), [], biasA)
        P.dma("sp", biasA.ap[:, 8:10, :], kv_ada_bA.rearrange("v p k -> p v k"), [], biasA)
        P.dma("sp", nrmA.ap[:, 0:4, :], normA.rearrange("l v p k -> p (l v) k"), [], nrmA)
        P.dma("sp", nrmA.ap[:, 4, :], kv_normA[:, :], [], nrmA)
        P.dma("sp", headg.ap[:], head_g[0:1, :].partition_broadcast(128), [], headg)
        P.dma("sp", latg.ap[:], kv_lat_g[0:1, :].partition_broadcast(128), [], latg)
        P.dma("sp", qlatg.ap[:], q_lat_g[0:1, :].partition_broadcast(128), [], qlatg)
        for i in range(2):
            P.dma("sp", gb.ap[:, i, :], gate_b[i, 0:1, :].partition_broadcast(128), [], gb)
        tcopy("dve", ident_b.ap[:], ident_f.ap[:], [ident_f], [ident_b])
        tcopy("dve", mask_b.ap[:], mask_f.ap[:], [mask_f], [mask_b])
        memset("dve", ones_f.ap[:], 1.0, [ones_f])
        memset("dve", ones_b.ap[:], 1.0, [ones_b])
        memset("dve", eps_t.ap[:], EPS, [eps_t])
        ts("dve", fones_b.ap[:], ones_f.ap[:], flag.ap[:, 0:1], None, ALU.mult, None, [ones_f, flag], [fones_b])
        memset("pool", krT.ap[64:65, :], 1.0, [krT])
        memset("pool", Cf.ap[:], 0.0, [Cf])
        memset("pool", mst.ap[:], 0.0, [mst])
        act(cond.ap[:], cond.ap[:], AF.Silu, [cond], [cond])

        def cast_rows(dst, src, nrows, step=256):
            for r0 in range(0, nrows, step):
                r1 = min(nrows, r0 + step)
                P.dma("pool", dst.ap[r0:r1], src[r0:r1], [], dst)

        cast_rows(wb_in, a_w_in, D)
        P.dma("pool", wb_a2.ap[:, 0:320], kv_w_a[:, :], [], wb_a2)
        P.dma("pool", wb_qb2.ap[:, :, 0:192], w_q_b.rearrange("r (h c) -> r h c", h=8), [], wb_qb2)
        cast_rows(wb_out, a_w_out, D)
        cast_rows(wb_w1[0], mlp_w1[0], D)
        cast_rows(wb_w2[0], mlp_w2[0], 4 * D)
        cast_rows(wb_b, kv_w_b, 256)
        cast_rows(wb_qa, w_q_a, D)
        cast_rows(wb_bo, b_w_out, D)
        cast_rows(wb_w1[1], mlp_w1[1], D)
        cast_rows(wb_w2[1], mlp_w2[1], 4 * D)

        state["aoff"] = 0
        cond_rep = aview(8 * 128 * 2, [128, 8, 128], F32, "cond_rep")
        aslab = [aview(8 * 512 * 2, [128, 8, 512], F32, "aslab%d" % i) for i in range(2)]
        brow = aview(512 * 2, [128, 512], F32, "brow")
        nrow = aview(512 * 2, [128, 512], F32, "nrow")
        rtmp = aview(8 * 64 * 2, [128, 8, 64], F32, "rtmp")
        rtmpb = aview(8 * 64, [128, 8, 64], BF16, "rtmpb")
        rq = aview(3 * 512 * 2, [128, 3, 8, 64], F32, "rq")
        rqb = aview(3 * 512, [128, 3, 8, 64], BF16, "rqb")
        prol_bufs = [cond_rep, brow, nrow, rtmp, rtmpb, rq, rqb] + aslab

        for k in range(8):
            tcopy("dve", cond_rep.ap[:, k, :], cond.ap[:, k:k + 1].to_broadcast([128, 128]), [cond], [cond_rep])

        P.dma("sp", rtmp.ap[:], kv_w_a[:, 256:320].rearrange("(k p) c -> p k c", p=128), [], rtmp)
        ts("dve", rtmpb.ap[:, :, 0:32], rtmp.ap[:, :, 32:64], -1.0, None, ALU.mult, None, [rtmp], [rtmpb])
        tcopy("dve", rtmpb.ap[:, :, 32:64], rtmp.ap[:, :, 0:32], [rtmp], [rtmpb])
        P.dma("act", wb_a2.ap[:, 320:384].rearrange("(k p) c -> p k c", p=128), rtmpb.ap[:], [rtmpb], wb_a2)
        for c3 in range(3):
            P.dma("sp", rq.ap[:, c3], w_q_b[c3 * 128:(c3 + 1) * 128, :].rearrange("p (h c) -> p h c", h=8)[:, :, 128:192], [], rq)
        ts("dve", rqb.ap[:, :, :, 0:32], rq.ap[:, :, :, 32:64], -1.0, None, ALU.mult, None, [rq], [rqb])
        tcopy("dve", rqb.ap[:, :, :, 32:64], rq.ap[:, :, :, 0:32], [rq], [rqb])
        for c3 in range(3):
            P.dma("act", wb_qb2.ap[c3 * 128:(c3 + 1) * 128, :, 192:256], rqb.ap[:, c3], [rqb], wb_qb2)

        chk(0)
        def ada_group(wsrc, col0, kind, idx, li, sidx):
            ab = state["abuf"]
            sl = ab["slabs"][state["as"] % len(ab["slabs"])]
            brow, nrow = ab["brow"], ab["nrow"]
            state["as"] += 1
            P.dma("sp", sl.ap[:], wsrc[:, col0:col0 + 512].rearrange("(k p) n -> p k n", p=128), [], sl)
            half = (col0 % 1024) // 512
            pt = P.ps()
            if kind == "A":
                for j in range(4):
                    for k in range(8):
                        mm(pt, pt.ap[:, j:j + 1], sl.ap[:, k, j * 128:(j + 1) * 128], cond.ap[:, k:k + 1], k == 0, k == 7, [sl, cond])
                tcopy("dve", vecA.ap[:, idx, half * 4:half * 4 + 4], pt.ap[:, 0:4], [pt], [vecA])
            else:
                for k in range(8):
                    mm(pt, pt.ap[:, :], cond_rep.ap[:, k, :], sl.ap[:, k, :], k == 0, k == 7, [sl, cond_rep])
                P.dma("sp", brow.ap[:], ada_bG[li, sidx, 0:1, half * 512:(half + 1) * 512].partition_broadcast(128), [], brow)
                P.dma("sp", nrow.ap[:], normG[li, sidx, 0:1, half * 512:(half + 1) * 512].partition_broadcast(128), [], nrow)
                tt("dve", brow.ap[:], pt.ap[:, :], brow.ap[:], ALU.add, [pt, brow], [brow])
                tt("dve", G.ap[:, idx, half * 512:(half + 1) * 512], brow.ap[:], nrow.ap[:], ALU.mult, [brow, nrow], [G])

        state["as"] = 0
        state["abuf"] = {"slabs": aslab, "brow": brow, "nrow": nrow}

        def ada_layer_groups(l):
            gl = []
            for v in range(6):
                for half in range(2):
                    col0 = v * 1024 + half * 512
                    if v in (2, 5):
                        gl.append((ada_w[l], col0, "G", l * 2 + (0 if v == 2 else 1), l, 0 if v == 2 else 1))
                    else:
                        gl.append((ada_w[l], col0, "A", l * 4 + {0: 0, 1: 1, 3: 2, 4: 3}[v], l, 0))
            return gl

        def ada_finish(i0, i1, mods):
            tt("dve", vecA.ap[:, i0:i1, :], vecA.ap[:, i0:i1, :], biasA.ap[:, i0:i1, :], ALU.add, [vecA, biasA], [vecA])
            for (mi, shi, sci) in mods:
                stt("dve", modA.ap[:, mi, :], vecA.ap[:, sci, :], 1.0, nrmA.ap[:, mi, :], ALU.add, ALU.mult, [vecA, nrmA], [modA])
                tcopy("dve", modB.ap[:, mi, :], vecA.ap[:, shi, :], [vecA], [modB])

        for g in ada_layer_groups(0):
            ada_group(*g)
        ada_finish(0, 4, ((0, 0, 1), (1, 2, 3)))
        deferred = [(kv_ada_w, v * 1024 + half * 512, "A", 8 + v, 0, 0) for v in range(2) for half in range(2)] + ada_layer_groups(1)
        if stage == 1:
            tcopy("dve", X.ap[:, :, :], G.ap[:, :, :], [G], [X])
            tcopy("dve", X.ap[:, 0, 0:40], modA.ap[:].rearrange("p a k -> p (a k)"), [modA], [X])
            tcopy("dve", X.ap[:, 0, 40:80], modB.ap[:].rearrange("p a k -> p (a k)"), [modB], [X])
        chk(1)
        P.dma("sp", WIF.ap[:], wb_in.ap[:, 2048:2064].rearrange("(k p) n -> p k n", p=128), [wb_in], WIF)

        state["aoff"] = 0
        QT = aview(4 * T, [128, 4, T], BF16, "QT")
        KT = aview(4 * T, [128, 4, T], BF16, "KT")
        KE = aview(4 * 4 * 128, [128, 4, 4, 128], BF16, "KE")
        KO = aview(4 * 4 * 128, [128, 4, 4, 128], BF16, "KO")
        VE = aview(4 * 8 * 130, [128, 4, 8, 130], BF16, "VE")
        VW = aview(4 * 8 * 130, [128, 4, 8, 130], BF16, "VW")
        OG = aview(4 * D, [128, 4, D], BF16, "OG")
        STt = aview(8 * 128, [128, 8, 128], BF16, "ST")
        HH = aview(2 * D, [128, 8, 128], F32, "HH")
        SQ = aview(2 * D, [128, 8, 128], F32, "SQ")
        GT = aview(2 * 256, [128, 256], F32, "GT")
        GH = aview(2 * 768, [128, 768], F32, "GH")
        M0 = [QT, KT, KE, KO, VE, VW, OG, STt, HH, SQ, GT, GH]
        _save = state["aoff"]
        state["aoff"] = 2048
        browL = aview(1024, [128, 512], F32, "browL")
        nrowL = aview(1024, [128, 512], F32, "nrowL")
        state["aoff"] = 16512
        aslabL = aview(8 * 512 * 2, [128, 8, 512], F32, "aslabL")
        state["aoff"] = _save
        LATE = [browL, nrowL, aslabL, cond_rep]
        m0_end = state["aoff"]
        state["aoff"] = 0
        HID = aview(32 * T, [128, 32, T], BF16, "HID")
        RT = [aview(T * 2, [128, T], F32, "RT%d" % i) for i in range(2)]
        MLPB = [HID] + RT
        state["aoff"] = 0
        CKV = aview(4 * 256 * 2, [128, 4, 256], F32, "CKV")
        CKN = aview(4 * 256, [128, 4, 256], BF16, "CKN")
        CKT = aview(2 * T, [128, 2, T], BF16, "CKT")
        KEXP = aview(8 * T, [128, 8, T], BF16, "KEXP")
        VEXP = aview(8 * T, [128, 8, 4, 128], BF16, "VEXP")
        R1 = aview(T * 2, [128, T], F32, "R1")
        R2 = aview(T * 2, [128, T], F32, "R2")
        KVB = [CKV, CKN, CKT, KEXP, VEXP, R1, R2]
        kv_end = state["aoff"]
        state["aoff"] = 0
        CQ = aview(4 * 384 * 2, [128, 4, 384], F32, "CQ")
        CQN = aview(4 * 384, [128, 4, 384], BF16, "CQN")
        CQT = aview(3 * T, [128, 3, T], BF16, "CQT")
        QNT = aview(8 * T, [128, 8, T], BF16, "QNT")
        QRT = aview(8 * T, [128, 8, T], BF16, "QRT")
        OT = aview(8 * T, [128, 8, T], BF16, "OT")
        PTs = [aview(T, [128, T], BF16, "PT%d" % i) for i in range(4)]
        RDEN = aview(T * 2, [128, T], F32, "RDEN")
        AR1 = aview(T * 2, [128, T], F32, "AR1")
        AR2 = aview(T * 2, [128, T], F32, "AR2")
        ATTB = [CQ, CQN, CQT, QNT, QRT, OT, RDEN, AR1, AR2] + PTs

        P.carry(prol_bufs, M0)
        P.carry(prol_bufs, [browL, nrowL, aslabL])

        def norm_to_HT(mi):
            memset("dve", st4.ap[:, 0:4], 0.0, [st4])
            for s in range(4):
                act(junk.ap[:], X.ap[:, s, :], AF.Square, [X], [junk, st4], accum=st4.ap[:, s:s + 1])
            rstd_from_ss(st4.ap[:, 0:4], st4.ap[:, 0:4], D, [st4], [st4])
            for s in range(4):
                if s % 2 == 0:
                    ts("dve", XN.ap[:, s, :], X.ap[:, s, :], st4.ap[:, s:s + 1], None, ALU.mult, None, [X, st4], [XN])
                else:
                    act(XN.ap[:, s, :], X.ap[:, s, :], AF.Identity, [X, st4], [XN], scale=st4.ap[:, s:s + 1])
            transpose_to_HT(XN, modA.ap[:, mi, :], modB.ap[:, mi, :], [modA, modB])

        def transpose_to_HT(src, a_ap, b_ap, extra):
            for k in range(8):
                pt = P.ps()
                pv = pt.ap[:].bitcast(BF16)
                for s in range(4):
                    trn(pt, pv[:, s * 128:(s + 1) * 128], src.ap[:, s, k * 128:(k + 1) * 128], ident_b, [src])
                if a_ap is not None:
                    if k % 2 == 0:
                        act(HT.ap[:, k, :], pv[:, 0:T], AF.Identity, [pt] + extra, [HT], scale=a_ap[:, k:k + 1], bias=b_ap[:, k:k + 1])
                    else:
                        ts("dve", HT.ap[:, k, :], pv[:, 0:T], a_ap[:, k:k + 1], b_ap[:, k:k + 1], ALU.mult, ALU.add, [pt] + extra, [HT])
                else:
                    if k % 2 == 0:
                        act(HT.ap[:, k, :], pv[:, 0:T], AF.Copy, [pt], [HT])
                    else:
                        tcopy("dve", HT.ap[:, k, :], pv[:, 0:T], [pt], [HT])

        def out_proj_residual(lhs_of, wsrc, gi, lhs_reads):
            w0 = wload(wsrc.ap[:, 0:512].rearrange("(k p) n -> p k n", p=128), 8, 512, wsrc)
            w1 = wload(wsrc.ap[:, 512:1024].rearrange("(k p) n -> p k n", p=128), 8, 512, wsrc)
            for s in range(4):
                Y = Ys[s % 2]
                for half, w in enumerate((w0, w1)):
                    pt = P.ps()
                    for k in range(8):
                        mm(pt, pt.ap[:, :], lhs_of(k, s), w.ap[:, k, :], k == 0, k == 7, lhs_reads + [w])
                    finish_half(pt, Y, s, half)
                residual_update(Y, s, gi)

        def finish_half(pt, Y, s, half):
            if half == 0:
                memset("dve", st4.ap[:, 8:10], 0.0, [st4])
            act(junk.ap[:, 0:512], pt.ap[:, :], AF.Square, [pt], [junk, st4], accum=st4.ap[:, 8 + half:9 + half])
            tcopy("dve", Y.ap[:, half * 512:(half + 1) * 512], pt.ap[:, :], [pt], [Y])

        def residual_update(Y, s, gi):
            tt("dve", st4.ap[:, 10:11], st4.ap[:, 8:9], st4.ap[:, 9:10], ALU.add, [st4], [st4])
            rstd_from_ss(st4.ap[:, 10:11], st4.ap[:, 10:11], D, [st4], [st4])
            stt("dve", Y.ap[:], Y.ap[:], st4.ap[:, 10:11], G.ap[:, gi, :], ALU.mult, ALU.mult, [Y, st4, G], [Y])
            tt("dve", X.ap[:, s, :], X.ap[:, s, :], Y.ap[:], ALU.add, [X, Y], [X])

        def mlp(l, mi, gi, hook=None):
            norm_to_HT(mi)
            if hook is not None:
                hook()
            for g8 in range(8):
                w = wload(wb_w1[l].ap[:, g8 * 512:(g8 + 1) * 512].rearrange("(k p) n -> p k n", p=128), 8, 512, wb_w1[l])
                for j in range(4):
                    m = g8 * 4 + j
                    pt = P.ps()
                    for k in range(8):
                        mm(pt, pt.ap[:, :], w.ap[:, k, j * 128:(j + 1) * 128], HT.ap[:, k, :], k == 0, k == 7, [w, HT])
                    rt = RT[m % 2]
                    act(rt.ap[:], pt.ap[:, :], AF.Relu, [pt], [rt])
                    tt("pool", HID.ap[:, m, :], rt.ap[:], rt.ap[:], ALU.mult, [rt], [HID])
            memset("dve", st4.ap[:, 4:8], 0.0, [st4])
            memset("dve", st4.ap[:, 12:16], 0.0, [st4])
            for half in range(2):
                acc, got = P.acquire(4)
                for kg in range(4):
                    w = wload(wb_w2[l].ap[kg * 1024:(kg + 1) * 1024, half * 512:(half + 1) * 512].rearrange("(k p) n -> p k n", p=128), 8, 512, wb_w2[l])
                    for kk in range(8):
                        kc = kg * 8 + kk
                        for s in range(4):
                            mm(acc[s], acc[s].ap[:, :], HID.ap[:, kc, s * 128:(s + 1) * 128], w.ap[:, kk, :], kc == 0, kc == 31, [HID, w])
                for s in range(4):
                    finish_half_mlp(acc[s], s, half)
                P.release(got)
            for s in range(4):
                residual_update_mlp(s, gi)

        state["aoff"] = 32 * T + 2 * T * 2
        Y2 = aview(4 * D * 2, [128, 4, D], F32, "Y2") if state["aoff"] + 4 * D * 2 <= ARENA_E else None
        assert Y2 is not None
        MLPB.append(Y2)

        def finish_half_mlp(pt, s, half):
            act(junk.ap[:, 0:512], pt.ap[:, :], AF.Square, [pt], [junk, st4], accum=st4.ap[:, ((4 + s) if half == 0 else (12 + s)):((5 + s) if half == 0 else (13 + s))])
            tcopy("dve", Y2.ap[:, s, half * 512:(half + 1) * 512], pt.ap[:, :], [pt], [Y2])

        def residual_update_mlp(s, gi):
            tt("dve", st4.ap[:, 10:11], st4.ap[:, 4 + s:5 + s], st4.ap[:, 12 + s:13 + s], ALU.add, [st4], [st4])
            rstd_from_ss(st4.ap[:, 10:11], st4.ap[:, 10:11], D, [st4], [st4])
            stt("dve", Y2.ap[:, s, :], Y2.ap[:, s, :], st4.ap[:, 10:11], G.ap[:, gi, :], ALU.mult, ALU.mult, [Y2, st4, G], [Y2])
            tt("dve", X.ap[:, s, :], X.ap[:, s, :], Y2.ap[:, s, :], ALU.add, [X, Y2], [X])

        def mixer0(mt, state_only=False):
            ms_eng = "dve" if state_only else "pool"
            memset(ms_eng, VE.ap[:, :, :, 128:129], 1.0, [VE])
            memset(ms_eng, KE.ap[:, :, :, 64:128], 0.0, [KE])
            memset(ms_eng, KO.ap[:, :, :, 0:64], 0.0, [KO])
            norm_to_HT(0)

            def proj_q():
                wq = wload(wb_in.ap[:, 0:512].rearrange("(k p) n -> p k n", p=128), 8, 512, wb_in)
                for m in range(4):
                    pt = P.ps()
                    for k in range(8):
                        mm(pt, pt.ap[:, :], wq.ap[:, k, m * 128:(m + 1) * 128], HT.ap[:, k, :], k == 0, k == 7, [wq, HT])
                    act(QT.ap[:, m, :], pt.ap[:, :], AF.Copy, [pt], [QT], scale=0.125)

            def proj_v(hv):
                wv = wload(wb_in.ap[:, 1024 + hv * 512:1024 + (hv + 1) * 512].rearrange("(k p) n -> p k n", p=128), 8, 512, wb_in)
                for s in range(4):
                    pt = P.ps()
                    for k in range(8):
                        mm(pt, pt.ap[:, :], HT.ap[:, k, s * 128:(s + 1) * 128], wv.ap[:, k, :], k == 0, k == 7, [wv, HT])
                    ptv = pt.ap[:, :].rearrange("p (h e) -> p h e", h=4)
                    tcopy("dve", VE.ap[:, s, hv * 4:(hv + 1) * 4, 0:128], ptv, [pt], [VE])

            def proj_o(ho):
                wo = wload(wb_in.ap[:, 2064 + ho * 512:2064 + (ho + 1) * 512].rearrange("(k p) n -> p k n", p=128), 8, 512, wb_in)
                for s in range(4):
                    pt = P.ps()
                    for k in range(8):
                        mm(pt, pt.ap[:, :], HT.ap[:, k, s * 128:(s + 1) * 128], wo.ap[:, k, :], k == 0, k == 7, [wo, HT])
                    act(OG.ap[:, s, ho * 512:(ho + 1) * 512], pt.ap[:, :], AF.Sigmoid, [pt], [OG])

            def proj_k():
                wk = wload(wb_in.ap[:, 512:1024].rearrange("(k p) n -> p k n", p=128), 8, 512, wb_in)
                if not state_only:
                    for m in range(4):
                        pt = P.ps()
                        for k in range(8):
                            mm(pt, pt.ap[:, :], wk.ap[:, k, m * 128:(m + 1) * 128], HT.ap[:, k, :], k == 0, k == 7, [wk, HT])
                        tcopy("dve", KT.ap[:, m, :], pt.ap[:, :], [pt], [KT])
                for s in range(4):
                    pt = P.ps()
                    for k in range(8):
                        mm(pt, pt.ap[:, :], HT.ap[:, k, s * 128:(s + 1) * 128], wk.ap[:, k, :], k == 0, k == 7, [wk, HT])
                    ptv = pt.ap[:, :].rearrange("p (j r d) -> p j r d", j=4, r=2)
                    tcopy("dve", KE.ap[:, s, :, 0:64], ptv[:, :, 0, :], [pt], [KE])
                    tcopy("dve", KO.ap[:, s, :, 64:128], ptv[:, :, 1, :], [pt], [KO])

            if state_only:
                sched = {1: [lambda: proj_v(0)], 2: [lambda: proj_v(1)], 3: [proj_k], 4: []}
            else:
                sched = {1: [proj_q], 2: [lambda: proj_v(0)], 3: [lambda: proj_v(1), lambda: proj_o(0)], 4: [lambda: proj_o(1), proj_k]}

            def hook(i):
                for f in sched[i]:
                    f()
            pg = P.ps()
            for s in range(4):
                for k in range(8):
                    mm(pg, pg.ap[:, s * 16:(s + 1) * 16], HT.ap[:, k, s * 128:(s + 1) * 128], WIF.ap[:, k, :], k == 0, k == 7, [HT, WIF])
            gi_v = GT.ap[:, 0:32].rearrange("p (s h) -> p s h", s=4)
            gf_v = GT.ap[:, 32:64].rearrange("p (s h) -> p s h", s=4)
            pgv = pg.ap[:, 0:64].rearrange("p (s g) -> p s g", s=4)
            tt("dve", gi_v, pgv[:, :, 0:8], gb.ap[:, 0:1, :].to_broadcast([128, 4, 8]), ALU.add, [pg, gb], [GT])
            tt("dve", gf_v, pgv[:, :, 8:16], gb.ap[:, 1:2, :].to_broadcast([128, 4, 8]), ALU.add, [pg, gb], [GT])
            act(gf_v, gf_v, AF.Exp, [GT], [GT], scale=-1.0)
            act(gf_v, gf_v, AF.Ln, [GT, ones_f], [GT], scale=1.0, bias=ones_f.ap[:, 0:1])
            ts("dve", gf_v, gf_v, -1.0, None, ALU.mult, None, [GT], [GT])
            hook(1)
            pb = P.ps()
            for s in range(4):
                mm(pb, pb.ap[:, s * 8:(s + 1) * 8], mask_f.ap[:], GT.ap[:, 32 + s * 8:32 + (s + 1) * 8], True, True, [mask_f, GT])
            bb_v = GT.ap[:, 64:96].rearrange("p (s h) -> p s h", s=4)
            a_v = GT.ap[:, 96:128].rearrange("p (s h) -> p s h", s=4)
            tcopy("dve", GT.ap[:, 64:96], pb.ap[:, 0:32], [pb], [GT])
            tt("dve", GT.ap[:, 96:128], GT.ap[:, 0:32], GT.ap[:, 64:96], ALU.subtract, [GT], [GT])
            hook(2)
            pa = P.ps()
            pbt = P.ps()
            for s in range(4):
                trn(pa, pa.ap[0:8, s * 128:(s + 1) * 128], GT.ap[:, 96 + s * 8:96 + (s + 1) * 8], ident_f, [GT])
                trn(pbt, pbt.ap[0:8, s * 128:(s + 1) * 128], GT.ap[:, 64 + s * 8:64 + (s + 1) * 8], ident_f, [GT])
            P.op("dve", lambda e: e.tensor_reduce(out=GH.ap[0:8, 0:4], in_=pa.ap[0:8, 0:512].rearrange("p (s t) -> p s t", s=4), axis=AX.X, op=ALU.max), reads=[pa], writes=[GH])
            tcopy("dve", GH.ap[0:8, 4:8], pbt.ap[0:8, 0:512].rearrange("p (s t) -> p s t", s=4)[:, :, 127], [pbt], [GH])
            for c in range(4):
                tt("dve", GH.ap[0:8, 8 + c:9 + c], GH.ap[0:8, c:c + 1], mst.ap[:], ALU.max, [GH, mst], [GH])
                tt("dve", GH.ap[0:8, 12 + c:13 + c], mst.ap[:], GH.ap[0:8, 8 + c:9 + c], ALU.subtract, [GH, mst], [GH])
                tt("dve", mst.ap[:], GH.ap[0:8, 4 + c:5 + c], GH.ap[0:8, 8 + c:9 + c], ALU.add, [GH], [mst])
            act(GH.ap[0:8, 12:16], GH.ap[0:8, 12:16], AF.Exp, [GH], [GH])
            Rv = GH.ap[0:8, 32:96].rearrange("p (c x) -> p c x", c=4)
            for c in range(4):
                ts("dve", Rv[:, c, 0:8], ident_f.ap[0:8, 0:8], GH.ap[0:8, 8 + c:9 + c], None, ALU.mult, None, [ident_f, GH], [GH])
                ts("dve", Rv[:, c, 8:12], ident_f.ap[0:8, 0:8].rearrange("p (j r) -> p j r", r=2)[:, :, 0], GH.ap[0:8, 12 + c:13 + c], None, ALU.mult, None, [ident_f, GH], [GH])
                ts("dve", Rv[:, c, 12:16], ident_f.ap[0:8, 0:8].rearrange("p (j r) -> p j r", r=2)[:, :, 1], GH.ap[0:8, 12 + c:13 + c], None, ALU.mult, None, [ident_f, GH], [GH])
            hook(3)
            pm = P.ps()
            mm(pm, pm.ap[:, 0:64], ones_f.ap[0:8, :], GH.ap[0:8, 32:96], True, True, [ones_f, GH])
            pmv = pm.ap[:, 0:64].rearrange("p (c x) -> p c x", c=4)
            w_v = GT.ap[:, 128:160].rearrange("p (s h) -> p s h", s=4)
            e_v = GT.ap[:, 160:192].rearrange("p (s h) -> p s h", s=4)
            dec_v = GT.ap[:, 192:224].rearrange("p (c x) -> p c x", c=4)
            tt("dve", w_v, a_v, pmv[:, :, 0:8], ALU.subtract, [GT, pm], [GT])
            stt("dve", e_v, bb_v, -1.0, pmv[:, :, 0:8], ALU.mult, ALU.subtract, [GT, pm], [GT])
            tcopy("dve", dec_v, pmv[:, :, 8:16], [pm], [GT])
            act(GT.ap[:, 128:192], GT.ap[:, 128:192], AF.Exp, [GT], [GT])

            if stage == 2:
                for s_ in range(4):
                    tcopy("dve", X.ap[:, s_, :], HT.ap[:, 2 * s_:2 * s_ + 2, :].rearrange("p a t -> p (a t)"), [HT], [X])
                tcopy("dve", X.ap[:, 0, 0:256], GT.ap[:], [GT], [X])
            chk(2)
            hook(4)
            if not state_only:
                for s in range(4):
                    tt("dve", OG.ap[:, s, :], OG.ap[:, s, :], headg.ap[:], ALU.mult, [OG, headg], [OG])
            for s in range(4):
                tt("dve", VW.ap[:, s], VE.ap[:, s], w_v[:, s, :].unsqueeze(2).to_broadcast([128, 8, 130]), ALU.mult, [VE, GT], [VW])

            chk(3)
            for s in range(4):
                sc = slice(s * 128, (s + 1) * 128)
                tt("dve", Cf.ap[0:64], Cf.ap[0:64], dec_v[0:64, s, 0:4].unsqueeze(2).to_broadcast([64, 4, 130]), ALU.mult, [Cf, GT], [Cf])
                tt("dve", Cf.ap[64:128], Cf.ap[64:128], dec_v[64:128, s, 4:8].unsqueeze(2).to_broadcast([64, 4, 130]), ALU.mult, [Cf, GT], [Cf])
                if not state_only:
                    act(Cb.ap[:], Cf.ap[:], AF.Copy, [Cf], [Cb])
                for r in (range(2) if not state_only else ()):
                    pS = P.ps()
                    for j in range(4):
                        mm(pS, pS.ap[:, j * 128:(j + 1) * 128], KT.ap[64 * r:64 * r + 64, j, sc], QT.ap[64 * r:64 * r + 64, j, sc], True, True, [KT, QT])
                    tt("dve", STt.ap[:, :, :].rearrange("p (j r) t -> p j r t", r=2)[:, :, r, :],
                       pS.ap[:, :].rearrange("p (j t) -> p j t", j=4), mask_b.ap[:, None, :].to_broadcast([128, 4, 128]), ALU.mult, [pS, mask_b], [STt])
                for (h0, nh) in (((0, 3), (3, 3), (6, 2)) if not state_only else ()):
                    pN = P.ps()
                    for hh in range(nh):
                        h = h0 + hh
                        j, r = h // 2, h % 2
                        mm(pN, pN.ap[:, hh * 129:(hh + 1) * 129], QT.ap[64 * r:64 * r + 64, j, sc], Cb.ap[64 * r:64 * r + 64, j, 0:129], True, False, [QT, Cb])
                        mm(pN, pN.ap[:, hh * 129:(hh + 1) * 129], STt.ap[:, h, :], VW.ap[:, s, h, 0:129], False, True, [STt, VW])
                    pNv = pN.ap[:, 0:nh * 129].rearrange("p (h e) -> p h e", h=nh)
                    dd = GT.ap[:, 224:224 + nh]
                    act(dd, pNv[:, :, 128], AF.Abs, [pN], [GT])
                    tt("dve", dd, dd, e_v[:, s, h0:h0 + nh], ALU.max, [GT], [GT])
                    P.op("dve", lambda e, dd=dd: e.reciprocal(out=dd, in_=dd), reads=[GT], writes=[GT])
                    tt("dve", HH.ap[:, h0:h0 + nh, :], pNv[:, :, 0:128], dd.unsqueeze(2).to_broadcast([128, nh, 128]), ALU.mult, [pN, GT], [HH])
                for jb in range(2):
                    pC = P.ps()
                    for jj in range(2):
                        j = jb * 2 + jj
                        mm(pC, pC.ap[:, jj * 129:(jj + 1) * 129], KE.ap[:, s, j, :], VW.ap[:, s, 2 * j, 0:129], True, False, [KE, VW])
                        mm(pC, pC.ap[:, jj * 129:(jj + 1) * 129], KO.ap[:, s, j, :], VW.ap[:, s, 2 * j + 1, 0:129], False, True, [KO, VW])
                    tt("dve", Cf.ap[:, jb * 2:jb * 2 + 2, 0:129], Cf.ap[:, jb * 2:jb * 2 + 2, 0:129], pC.ap[:, 0:258].rearrange("p (j e) -> p j e", j=2), ALU.add, [Cf, pC], [Cf])
                if state_only:
                    continue
                tt("pool", SQ.ap[:], HH.ap[:], HH.ap[:], ALU.mult, [HH], [SQ])
                P.op("dve", lambda e: e.tensor_reduce(out=GT.ap[:, 232:240], in_=SQ.ap[:], axis=AX.X, op=ALU.add), reads=[SQ], writes=[GT])
                rstd_from_ss(GT.ap[:, 232:240], GT.ap[:, 232:240], 128, [GT], [GT])
                tt("dve", HH.ap[:], HH.ap[:], GT.ap[:, 232:240].unsqueeze(2).to_broadcast([128, 8, 128]), ALU.mult, [HH, GT], [HH])
                tt("dve", XN.ap[:, s, :].rearrange("p (h e) -> p h e", h=8), HH.ap[:], OG.ap[:, s, :].rearrange("p (h e) -> p h e", h=8), ALU.mult, [HH, OG], [XN])
            if state_only:
                return
            chk(4)
            transpose_to_HT(XN, None, None, [])
            out_proj_residual(lambda k, s: HT.ap[:, k, s * 128:(s + 1) * 128], wb_out, 0, [HT])
            chk(5)

        def rope_tables(mt):
            t0 = mt * T
            ji = junk.ap[0:64, :].bitcast(I32)
            P.dma("sp", ji, pos[0:1, t0:t0 + T].partition_broadcast(64), [], junk)
            tcopy("dve", ANG.ap[:], ji, [junk], [ANG])
            ts("dve", ANG.ap[:], ANG.ap[:], inv_col.ap[:, 0:1], None, ALU.mult, None, [ANG, inv_col], [ANG])
            for (dst, shift) in ((SIN, 0.0), (COS, PI / 2)):
                ts("dve", dst.ap[:], ANG.ap[:], shift, None, ALU.add, None, [ANG], [dst])
                ts("dve", KF.ap[:], dst.ap[:], 1.0 / TWO_PI, None, ALU.mult, None, [dst], [KF])
                tcopy("dve", ji, KF.ap[:], [KF], [junk])
                tcopy("dve", KF.ap[:], ji, [junk], [KF])
                stt("dve", dst.ap[:], KF.ap[:], -6.28125, dst.ap[:], ALU.mult, ALU.add, [KF, dst], [dst])
                stt("dve", dst.ap[:], KF.ap[:], -0.0019353071795864769, dst.ap[:], ALU.mult, ALU.add, [KF, dst], [dst])
                ts("dve", KF.ap[:], dst.ap[:], PI, None, ALU.is_gt, None, [dst], [KF])
                stt("dve", dst.ap[:], KF.ap[:], -TWO_PI, dst.ap[:], ALU.mult, ALU.add, [KF, dst], [dst])
                ts("dve", KF.ap[:], dst.ap[:], -PI, None, ALU.is_lt, None, [dst], [KF])
                stt("dve", dst.ap[:], KF.ap[:], TWO_PI, dst.ap[:], ALU.mult, ALU.add, [KF, dst], [dst])
                act(dst.ap[:], dst.ap[:], AF.Sin, [dst], [dst])

        def kv_phase(mt):
            t0 = mt * T
            norm_to_HT(4)
            wa = wload(wb_a2.ap[:, :].rearrange("(k p) n -> p k n", p=128), 8, 384, wb_a2)
            memset("dve", st4.ap[:, 0:4], 0.0, [st4])
            for s in range(4):
                pt = P.ps()
                for k in range(8):
                    mm(pt, pt.ap[:, 0:256], HT.ap[:, k, s * 128:(s + 1) * 128], wa.ap[:, k, 0:256], k == 0, k == 7, [HT, wa])
                act(junk.ap[:, 0:256], pt.ap[:, 0:256], AF.Square, [pt], [junk, st4], accum=st4.ap[:, s:s + 1])
                tcopy("dve", CKV.ap[:, s, :], pt.ap[:, 0:256], [pt], [CKV])
            rstd_from_ss(st4.ap[:, 0:4], st4.ap[:, 0:4], 256, [st4], [st4])
            for s in range(4):
                stt("dve", CKN.ap[:, s, :], CKV.ap[:, s, :], st4.ap[:, s:s + 1], latg.ap[:], ALU.mult, ALU.mult, [CKV, st4, latg], [CKN])
            for c in range(2):
                pt = P.ps()
                pv = pt.ap[:].bitcast(BF16)
                for s in range(4):
                    trn(pt, pv[:, s * 128:(s + 1) * 128], CKN.ap[:, s, c * 128:(c + 1) * 128], ident_b, [CKN])
                tcopy("dve", CKT.ap[:, c, :], pv[:, 0:T], [pt], [CKT])
            pA = P.ps()
            pB = P.ps()
            for k in range(8):
                mm(pA, pA.ap[0:64, :], wa.ap[:, k, 256:320], HT.ap[:, k, :], k == 0, k == 7, [HT, wa])
            for k in range(8):
                mm(pB, pB.ap[0:64, :], wa.ap[:, k, 320:384], HT.ap[:, k, :], k == 0, k == 7, [HT, wa])
            tt("dve", R1.ap[0:64], pA.ap[0:64, :], COS.ap[0:64], ALU.mult, [pA, COS], [R1])
            tt("dve", R2.ap[0:64], pB.ap[0:64, :], SIN.ap[0:64], ALU.mult, [pB, SIN], [R2])
            tt("pool", krT.ap[0:64, t0:t0 + T], R1.ap[0:64], R2.ap[0:64], ALU.add, [R1, R2], [krT])
            ti = mt - FIRST_OWN
            P.dma("act", lat_src[ti].ap[:, 0:1024].rearrange("p (c t) -> p c t", c=2), CKT.ap[:], [CKT], lat_src[ti])
            P.dma("act", lat_src[ti].ap[0:64, 1024:1536], krT.ap[0:64, t0:t0 + T], [krT], lat_src[ti])
            P.collective(lambda e, ti=ti: e.collective_compute("AllGather", ALU.bypass, replica_groups=[[0, 1], [2, 3], [4, 5], [6, 7]],
                                                               ins=[lat_src[ti].ap[:, :]], outs=[lat_all[ti].ap[:, :]]), [lat_src[ti]], lat_all[ti])
            expand_kv(mt)

        def expand_kv(mt):
            t0 = mt * T
            wbk = wload(wb_b.ap[:, :].rearrange("(c p) n -> p c n", p=128), 2, 2048, wb_b)
            wbv = wbk.ap[:, :, :].rearrange("p c (h x) -> p c h x", h=8)
            for h in range(8):
                pt = P.ps()
                for c in range(2):
                    mm(pt, pt.ap[:, :], wbv[:, c, h, 0:128], CKT.ap[:, c, :], c == 0, c == 1, [wbk, CKT])
                if h % 2 == 0:
                    act(KEXP.ap[:, h, :], pt.ap[:, :], AF.Copy, [pt], [KEXP])
                else:
                    tcopy("dve", KEXP.ap[:, h, :], pt.ap[:, :], [pt], [KEXP])
            for s in range(4):
                for hv in range(2):
                    pt = P.ps()
                    for hh in range(4):
                        for c in range(2):
                            h = hv * 4 + hh
                            mm(pt, pt.ap[:, hh * 128:(hh + 1) * 128], CKT.ap[:, c, s * 128:(s + 1) * 128], wbv[:, c, h, 128:256], c == 0, c == 1, [wbk, CKT])
                    ptv = pt.ap[:, :].rearrange("p (h e) -> p h e", h=4)
                    if hv == 0:
                        act(VEXP.ap[:, hv * 4:(hv + 1) * 4, s, :], ptv, AF.Copy, [pt], [VEXP])
                    else:
                        tcopy("dve", VEXP.ap[:, hv * 4:(hv + 1) * 4, s, :], ptv, [pt], [VEXP])
            for h in range(8):
                P.dma("act", kcache.ap[h, :, t0:t0 + T], KEXP.ap[:, h, :], [KEXP], kcache)
                P.dma("act", vcache.ap[h, :, mt * 4:(mt + 1) * 4, :], VEXP.ap[:, h, :, :], [VEXP], vcache)

        def mixer1(mt):
            t0 = mt * T
            norm_to_HT(2)
            wqa = wload(wb_qa.ap[:, :].rearrange("(k p) n -> p k n", p=128), 8, 384, wb_qa)
            memset("dve", st4.ap[:, 0:4], 0.0, [st4])
            for s in range(4):
                pt = P.ps()
                for k in range(8):
                    mm(pt, pt.ap[:, 0:384], HT.ap[:, k, s * 128:(s + 1) * 128], wqa.ap[:, k, :], k == 0, k == 7, [HT, wqa])
                act(junk.ap[:, 0:384], pt.ap[:, 0:384], AF.Square, [pt], [junk, st4], accum=st4.ap[:, s:s + 1])
                tcopy("dve", CQ.ap[:, s, :], pt.ap[:, 0:384], [pt], [CQ])
            rstd_from_ss(st4.ap[:, 0:4], st4.ap[:, 0:4], 384, [st4], [st4])
            for s in range(4):
                stt("dve", CQN.ap[:, s, :], CQ.ap[:, s, :], st4.ap[:, s:s + 1], qlatg.ap[:], ALU.mult, ALU.mult, [CQ, st4, qlatg], [CQN])
            for c in range(3):
                pt = P.ps()
                pv = pt.ap[:].bitcast(BF16)
                for s in range(4):
                    trn(pt, pv[:, s * 128:(s + 1) * 128], CQN.ap[:, s, c * 128:(c + 1) * 128], ident_b, [CQN])
                tcopy("dve", CQT.ap[:, c, :], pv[:, 0:T], [pt], [CQT])
            memset("pool", QRT.ap[64:65, :, :], 0.0, [QRT])
            for hg in range(2):
                wqb = wload(wb_qb2.ap[:, hg * 4:(hg + 1) * 4, :].rearrange("(c p) h x -> p c (h x)", p=128), 3, 1024, wb_qb2)
                wv = wqb.ap[:, :, :].rearrange("p c (h x) -> p c h x", h=4)
                for hh in range(4):
                    h = hg * 4 + hh
                    pt = P.ps()
                    for c in range(3):
                        mm(pt, pt.ap[:, :], wv[:, c, hh, 0:128], CQT.ap[:, c, :], c == 0, c == 2, [wqb, CQT])
                    act(QNT.ap[:, h, :], pt.ap[:, :], AF.Copy, [pt], [QNT])
                    pA = P.ps()
                    pB = P.ps()
                    for c in range(3):
                        mm(pA, pA.ap[0:64, :], wv[:, c, hh, 128:192], CQT.ap[:, c, :], c == 0, c == 2, [wqb, CQT])
                    for c in range(3):
                        mm(pB, pB.ap[0:64, :], wv[:, c, hh, 192:256], CQT.ap[:, c, :], c == 0, c == 2, [wqb, CQT])
                    tt("dve", AR1.ap[0:64], pA.ap[0:64, :], COS.ap[0:64], ALU.mult, [pA, COS], [AR1])
                    tt("dve", AR2.ap[0:64], pB.ap[0:64, :], SIN.ap[0:64], ALU.mult, [pB, SIN], [AR2])
                    tt("pool", QRT.ap[0:64, h, :], AR1.ap[0:64], AR2.ap[0:64], ALU.add, [AR1, AR2], [QRT])
            nkb = 4 * (mt + 1)
            npiece = (nkb + 15) // 16
            accs, got = P.acquire(4)
            LOOK = 2
            pieces = [(h, pc) for h in range(8) for pc in range(npiece)]
            loaded = {}

            def load_piece(idx):
                if idx >= len(pieces) or idx in loaded:
                    return
                h, pc = pieces[idx]
                kb0 = pc * 16
                nb = min(16, nkb - kb0)
                sl = kslots[state["ks"] % NKS]
                state["ks"] += 1
                kv_k = sl.ap[:, 0:nb * 128]
                kv_v = sl.ap[:, 2048:2048 + nb * 128].rearrange("p (b e) -> p b e", b=nb)
                P.dma("sp", kv_k, kcache.ap[h, :, kb0 * 128:(kb0 + nb) * 128], [kcache], sl)
                P.dma("sp", kv_v, vcache.ap[h, :, kb0:kb0 + nb, :], [vcache], sl)
                loaded[idx] = (sl, kv_k, kv_v)

            blocks = []
            for idx, (h, pc) in enumerate(pieces):
                kb0 = pc * 16
                for bi in range(min(16, nkb - kb0)):
                    blocks.append((h, idx, bi, kb0 + bi))
            nblk = len(blocks)
            pend = {}

            def emit_S(i):
                h, idx, bi, kb = blocks[i]
                if bi == 0:
                    load_piece(idx)
                if bi == LOOK:
                    load_piece(idx + 1)
                sl, kv_k, kv_v = loaded[idx]
                pS = P.ps()
                mm(pS, pS.ap[:, :], kv_k[:, bi * 128:(bi + 1) * 128], QNT.ap[:, h, :], True, False, [sl, QNT])
                mm(pS, pS.ap[:, :], krT.ap[0:65, kb * 128:(kb + 1) * 128], QRT.ap[0:65, h, :], False, True, [krT, QRT])
                PT = PTs[i % 4]
                act(PT.ap[:], pS.ap[:, :], AF.Exp, [pS], [PT], scale=ATT_SCALE)
                jd = kb - 4 * mt
                if jd >= 0:
                    if jd > 0:
                        memset("pool", PT.ap[:, 0:jd * 128], 0.0, [PT])
                    tt("pool", PT.ap[:, jd * 128:(jd + 1) * 128], PT.ap[:, jd * 128:(jd + 1) * 128], mask_b.ap[:], ALU.mult, [PT, mask_b], [PT])
                pend[i] = PT

            def emit_PV(i):
                h, idx, bi, kb = blocks[i]
                sl, kv_k, kv_v = loaded[idx]
                pO, pD = accs[2 * (h % 2)], accs[2 * (h % 2) + 1]
                PT = pend.pop(i)
                first = kb == 0
                last = kb == nkb - 1
                mm(pO, pO.ap[:, :], kv_v[:, bi, :], PT.ap[:], first, last, [sl, PT])
                onesT = fones_b if kb < 4 * FIRST_OWN else ones_b
                mm(pD, pD.ap[:, :], onesT.ap[:], PT.ap[:], first, last, [onesT, PT])
                if last:
                    P.op("dve", lambda e, pD=pD: e.reciprocal(out=RDEN.ap[:], in_=pD.ap[:, :]), reads=[pD], writes=[RDEN])
                    tt("dve", OT.ap[:, h, :], pO.ap[:, :], RDEN.ap[:], ALU.mult, [pO, RDEN], [OT])

            for i in range(nblk + LOOK):
                if i < nblk:
                    emit_S(i)
                if i - LOOK >= 0:
                    emit_PV(i - LOOK)
            P.release(got)
            out_proj_residual(lambda k, s: OT.ap[:, k, s * 128:(s + 1) * 128], wb_bo, 2, [OT])

        def load_x(src_ap, reads):
            P.dma("sp", X.ap[:], src_ap.rearrange("(s p) d -> p s d", p=128), reads, X)

        state["abuf"] = {"slabs": [aslabL], "brow": browL, "nrow": nrowL}
        for mt in range(FIRST_OWN):
            load_x(xs[mt * T:(mt + 1) * T, :], [])
            if deferred:
                ada_group(*deferred.pop(0))
            mixer0(mt, state_only=True)
            if deferred:
                ada_group(*deferred.pop(0))
        while deferred:
            ada_group(*deferred.pop(0))
        ada_finish(4, 10, ((2, 4, 5), (3, 6, 7), (4, 8, 9)))
        P.carry(LATE, M0)
        ts("dve", Cf.ap[:], Cf.ap[:], flag.ap[:, 0:1], None, ALU.mult, None, [Cf, flag], [Cf])
        ts("dve", mst.ap[:], mst.ap[:], flag.ap[0:8, 0:1], None, ALU.mult, None, [mst, flag], [mst])

        prev = M0
        for mt in range(FIRST_OWN, NT):
            load_x(xs[mt * T:(mt + 1) * T, :], [])
            if prev is not M0:
                P.carry(prev, M0)
            mixer0(mt)
            P.carry(M0, MLPB)
            mlp(0, 1, 1, hook=lambda: rope_tables(mt))
            o0 = (mt - FIRST_OWN) * T
            P.dma("act", xmid.ap[o0:o0 + T, :].rearrange("(s p) d -> p s d", p=128), X.ap[:], [X], xmid)
            P.carry(MLPB, KVB)
            kv_phase(mt)
            prev = KVB

        for mt in range(FIRST_OWN):
            P.dma("sp", krT.ap[0:64, mt * T:(mt + 1) * T], lat_all[mt].ap[0:64, 1024:1536], [lat_all[mt]], krT)
            P.dma("sp", CKT.ap[:], lat_all[mt].ap[0:128, 0:1024].rearrange("p (c t) -> p c t", c=2), [lat_all[mt]], CKT)
            ts("dve", CKT.ap[:], CKT.ap[:], flag.ap[:, 0:1], None, ALU.mult, None, [CKT, flag], [CKT])
            expand_kv(mt)

        rope_tables(FIRST_OWN)
        prev = KVB
        for mt in range(FIRST_OWN, NT):
            o0 = (mt - FIRST_OWN) * T
            load_x(xmid.ap[o0:o0 + T, :], [xmid])
            P.carry(prev, ATTB)
            mixer1(mt)
            P.carry(ATTB, MLPB)
            mlp(1, 3, 3, hook=(lambda: rope_tables(mt + 1)) if mt + 1 < NT else None)
            prev = MLPB
            P.dma("act", yout[o0:o0 + T, :].rearrange("(s p) d -> p s d", p=128), X.ap[:], [X], yout_b)
    try:
        body()
    except _Stop:
        P.dma("pool", yout[0:T, :].rearrange("(s p) d -> p s d", p=128), X.ap[:], [X], yout_b)
    P.final_wait("pool", [yout_b])
    allb = [Buf("fin")]
    block = es.enter_context(nc.Block())

    @block.tensor
    def _(e):
        P.replay("pe", e)

    @block.scalar
    def _(e):
        P.replay("act", e)

    @block.vector
    def _(e):
        P.replay("dve", e)

    @block.gpsimd
    def _(e):
        P.replay("pool", e)

    @block.sync
    def _(e):
        P.replay("sp", e)


_CACHE = {}


def _prep_inputs(x, c, positions, ada_w, ada_b, norm_g, a_w_in, a_gate_b, a_head_g, a_w_out,
                 kv_ada_w, kv_ada_b, kv_norm_g, kv_w_a, kv_latent_g, kv_w_b, b_w_q_a, b_q_latent_g,
                 b_w_q_b, b_w_out, mlp_w1, mlp_w2):
    f = np.float32

    def pk(v):
        return np.ascontiguousarray(np.asarray(v, f).reshape(8, 128).T)

    ident = np.eye(128, dtype=f)
    mask01 = np.triu(np.ones((128, 128), f))
    half = 32
    inv = (10000.0 ** (-np.arange(half, dtype=f) / half)).astype(f)
    inv_col = np.concatenate([inv, inv]).reshape(64, 1).astype(f)
    ada_b = np.asarray(ada_b, f)
    norm_g = np.asarray(norm_g, f)
    ada_bA = np.stack([np.stack([pk(ada_b[l, v * 1024:(v + 1) * 1024]) for v in (0, 1, 3, 4)]) for l in range(2)])
    ada_bG = np.stack([np.stack([ada_b[l, v * 1024:(v + 1) * 1024].reshape(1, 1024) for v in (2, 5)]) for l in range(2)])
    normA = np.stack([np.stack([pk(norm_g[l, v]) for v in (0, 2)]) for l in range(2)])
    normG = np.stack([np.stack([norm_g[l, v].reshape(1, 1024) for v in (1, 3)]) for l in range(2)])
    kv_ada_b = np.asarray(kv_ada_b, f)
    shared = {
        "ident": ident, "mask01": mask01, "inv_col": inv_col,
        "ada_w": np.ascontiguousarray(ada_w, f), "ada_bA": np.ascontiguousarray(ada_bA), "ada_bG": np.ascontiguousarray(ada_bG),
        "normA": np.ascontiguousarray(normA), "normG": np.ascontiguousarray(normG),
        "kv_ada_w": np.ascontiguousarray(kv_ada_w, f),
        "kv_ada_bA": np.ascontiguousarray(np.stack([pk(kv_ada_b[0:1024]), pk(kv_ada_b[1024:2048])])),
        "kv_normA": pk(kv_norm_g),
        "a_w_in": np.ascontiguousarray(a_w_in[0], f),
        "gate_b": np.ascontiguousarray(np.asarray(a_gate_b[0], f).reshape(2, 1, 8)),
        "head_g": np.ascontiguousarray(np.asarray(a_head_g[0], f).reshape(1, 1024)),
        "a_w_out": np.ascontiguousarray(a_w_out[0], f),
        "kv_w_a": np.ascontiguousarray(kv_w_a, f),
        "kv_lat_g": np.ascontiguousarray(np.asarray(kv_latent_g, f).reshape(1, 256)),
        "kv_w_b": np.ascontiguousarray(kv_w_b, f),
        "w_q_a": np.ascontiguousarray(b_w_q_a[0], f),
        "q_lat_g": np.ascontiguousarray(np.asarray(b_q_latent_g[0], f).reshape(1, 384)),
        "w_q_b": np.ascontiguousarray(b_w_q_b[0], f),
        "b_w_out": np.ascontiguousarray(b_w_out[0], f),
        "mlp_w1": np.ascontiguousarray(mlp_w1, f),
        "mlp_w2": np.ascontiguousarray(mlp_w2, f),
    }
    x = np.asarray(x, f)
    positions = np.asarray(positions, np.int32)
    in_maps = []
    for core in range(8):
        b, hf = core // 2, core % 2
        if hf == 1:
            xs = x[b]
            ps = positions[b]
        else:
            xs = np.concatenate([np.zeros((SEQ // 2, D), f), x[b, :SEQ // 2]], axis=0)
            ps = np.concatenate([np.zeros((SEQ // 2,), np.int32), positions[b, :SEQ // 2]])
        m = dict(shared)
        m["xs"] = np.ascontiguousarray(xs)
        m["pos"] = np.ascontiguousarray(ps.reshape(1, SEQ))
        m["flag"] = np.full((128, 1), float(hf), f)
        m["cT"] = pk(np.asarray(c, f)[b])
        in_maps.append(m)
    return in_maps


def kernel(**inputs):
    in_maps = _prep_inputs(**inputs)
    if "nc" not in _CACHE:
        _CACHE["nc"] = build_program(NT)
    nc = _CACHE["nc"]
    res = run_bass_kernel_spmd(nc, in_maps, core_ids=list(range(8)))
    out = np.zeros((4, SEQ, D), np.float32)
    for core in range(8):
        b, hf = core // 2, core % 2
        out[b, hf * (SEQ // 2):(hf + 1) * (SEQ // 2)] = res.results[core]["y"]
    return out
```

```python
import numpy as np
from contextlib import ExitStack
import concourse.bass as bass
import concourse.mybir as mybir
from concourse.bass_utils import run_bass_kernel_spmd

F32 = mybir.dt.float32
BF16 = mybir.dt.bfloat16
I32 = mybir.dt.int32
AF = mybir.ActivationFunctionType
ALU = mybir.AluOpType
AX = mybir.AxisListType

D = 1024
SEQ = 8192
T = 512
NT = 16
FIRST_OWN = 8
EPS = 1e-6
H = 8
SAME_ENG_SYNC = True
TWO_PI = 6.283185307179586
PI = 3.141592653589793
ATT_SCALE = 192.0 ** -0.5

ENGS = ("pe", "act", "dve", "pool", "sp")


class Buf:
    __slots__ = ("name", "w", "r", "dsem", "dcnt")

    def __init__(self, name):
        self.name = name
        self.w = None
        self.r = {}
        self.dsem = None
        self.dcnt = 0


class TT:
    def __init__(self, ap, buf):
        self.ap = ap
        self.b = buf

    def __getitem__(self, k):
        return self.ap[k]


class Prog:
    def __init__(self, nc, es):
        self.nc = nc
        self.es = es
        self.streams = {e: [] for e in ENGS}
        self.esem = {e: es.enter_context(nc.semaphore("es_" + e)) for e in ENGS}
        self.ecnt = {e: 0 for e in ENGS}
        self.waited = {e: {} for e in ENGS}
        self.semh = {}
        for e in ENGS:
            self.semh["es_" + e] = self.esem[e]
        self.nbuf = 0
        self.psb = []
        self.rot = []
        self.rot_i = 0

    def sb(self, name, shape, dt):
        t = self.es.enter_context(self.nc.sbuf_tensor("s_" + name, list(shape), dt))
        return TT(t, Buf(name))

    def view(self, ap, name):
        return TT(ap, Buf(name))

    def dsem_of(self, buf):
        if buf.dsem is None:
            nm = "ds%d" % len(self.semh)
            buf.dsem = nm
            self.semh[nm] = self.es.enter_context(self.nc.semaphore(nm))
        return buf.dsem

    def _collect(self, reads, writes, eng=None):
        deps = {}
        own = None if eng is None else "es_" + eng

        def add(tok, raw):
            if tok is None:
                return
            s, v = tok
            if s == own and not raw:
                return
            if deps.get(s, 0) < v:
                deps[s] = v

        for b in reads:
            add(b.w, True)
        for b in writes:
            add(b.w, False)
            for s, v in b.r.items():
                add((s, v), False)
        return deps

    def _emit_waits(self, eng, deps):
        own = "es_" + eng
        for s, v in deps.items():
            if s == own and (eng in ("pe", "sp") or not SAME_ENG_SYNC):
                continue
            if self.waited[eng].get(s, 0) >= v:
                continue
            self.waited[eng][s] = v
            self.streams[eng].append(("wait", s, v))

    @staticmethod
    def _bufs(lst):
        return [x.b if isinstance(x, TT) else x for x in lst]

    def op(self, eng, fn, reads=(), writes=()):
        reads = self._bufs(reads)
        writes = self._bufs(writes)
        if eng != "pe":
            writes = writes + [b for b in reads if b.name.startswith("ps") and b not in writes]
        self._emit_waits(eng, self._collect(reads, writes, eng))
        self.ecnt[eng] += 1
        tok = ("es_" + eng, self.ecnt[eng])
        self.streams[eng].append(("ins", fn, tok[0], 1))
        for b in reads:
            if b.r.get(tok[0], 0) < tok[1]:
                b.r[tok[0]] = tok[1]
        for b in writes:
            b.w = tok
            b.r = {}

    def dma(self, q, out_ap, in_ap, reads, target):
        reads = self._bufs(reads)
        tb = target.b if isinstance(target, TT) else target
        self._emit_waits(q, self._collect(reads, [tb]))
        s = self.dsem_of(tb)
        tb.dcnt += 16
        tok = (s, tb.dcnt)
        self.streams[q].append(("ins", lambda e: e.dma_start(out=out_ap, in_=in_ap), s, 16))
        for b in reads:
            if b.r.get(s, 0) < tok[1]:
                b.r[s] = tok[1]
        tb.w = tok
        tb.r = {}

    def collective(self, fn, reads, target):
        reads = self._bufs(reads)
        tb = target.b if isinstance(target, TT) else target
        self._emit_waits("pool", self._collect(reads, [tb]))
        s = self.dsem_of(tb)
        tb.dcnt += 1
        tok = (s, tb.dcnt)
        self.streams["pool"].append(("ins", fn, s, 1))
        for b in reads:
            if b.r.get(s, 0) < tok[1]:
                b.r[s] = tok[1]
        tb.w = tok
        tb.r = {}

    def carry(self, frm, to):
        m = {}
        for b in self._bufs(frm):
            if b.w is not None and m.get(b.w[0], 0) < b.w[1]:
                m[b.w[0]] = b.w[1]
            for s, v in b.r.items():
                if m.get(s, 0) < v:
                    m[s] = v
        for b in self._bufs(to):
            for s, v in m.items():
                if b.r.get(s, 0) < v:
                    b.r[s] = v
            if b.w is not None:
                pass

    def final_wait(self, eng, bufs):
        self._emit_waits(eng, self._collect(self._bufs(bufs), []))

    def init_psum(self):
        for i in range(8):
            t = self.es.enter_context(self.nc.psum_tensor("ps%d" % i, [128, 512], F32))
            self.psb.append(TT(t, Buf("ps%d" % i)))
        self.rot = list(range(8))

    def ps(self):
        i = self.rot[self.rot_i % len(self.rot)]
        self.rot_i += 1
        return self.psb[i]

    def acquire(self, k):
        got = self.rot[-k:]
        self.rot = self.rot[:-k]
        return [self.psb[i] for i in got], got

    def release(self, got):
        self.rot = self.rot + list(got)

    def replay(self, eng, handle):
        for it in self.streams[eng]:
            if it[0] == "wait":
                handle.wait_ge(self.semh[it[1]], it[2])
            else:
                ins = it[1](handle)
                ins.then_inc(self.semh[it[2]], it[3])


class _Stop(Exception):
    pass


def build_program(ntiles=NT, stage=None):
    nc = bass.Bass("TRN2", target_bir_lowering=False)
    es = ExitStack()
    with es:
        _build(nc, es, ntiles, stage)
    return nc


def _build(nc, es, ntiles, stage=None):
    P = Prog(nc, es)

    def din(name, shape, dt=F32):
        return nc.dram_tensor(name, list(shape), dt, kind="ExternalInput").ap()

    def dscr(name, shape, dt=BF16):
        return TT(nc.dram_tensor(name, list(shape), dt, kind="Internal").ap(), Buf(name))

    xs = din("xs", [SEQ, D])
    pos = din("pos", [1, SEQ], I32)
    flag_d = din("flag", [128, 1])
    cT_d = din("cT", [128, 8])
    ident_d = din("ident", [128, 128])
    mask_d = din("mask01", [128, 128])
    inv_d = din("inv_col", [64, 1])
    ada_w = din("ada_w", [2, D, 6 * D])
    ada_bA = din("ada_bA", [2, 4, 128, 8])
    ada_bG = din("ada_bG", [2, 2, 1, D])
    normA = din("normA", [2, 2, 128, 8])
    normG = din("normG", [2, 2, 1, D])
    kv_ada_w = din("kv_ada_w", [D, 2 * D])
    kv_ada_bA = din("kv_ada_bA", [2, 128, 8])
    kv_normA = din("kv_normA", [128, 8])
    a_w_in = din("a_w_in", [D, 3088])
    gate_b = din("gate_b", [2, 1, 8])
    head_g = din("head_g", [1, D])
    a_w_out = din("a_w_out", [D, D])
    kv_w_a = din("kv_w_a", [D, 320])
    kv_lat_g = din("kv_lat_g", [1, 256])
    kv_w_b = din("kv_w_b", [256, 2048])
    w_q_a = din("w_q_a", [D, 384])
    q_lat_g = din("q_lat_g", [1, 384])
    w_q_b = din("w_q_b", [384, 1536])
    b_w_out = din("b_w_out", [D, D])
    mlp_w1 = din("mlp_w1", [2, D, 4 * D])
    mlp_w2 = din("mlp_w2", [2, 4 * D, D])
    yout = nc.dram_tensor("y", [SEQ // 2, D], F32, kind="ExternalOutput").ap()
    yout_b = Buf("yout")

    wb_in = dscr("wb_in", [D, 3088])
    wb_out = dscr("wb_out", [D, D])
    wb_w1 = [dscr("wb_w1_%d" % l, [D, 4 * D]) for l in range(2)]
    wb_w2 = [dscr("wb_w2_%d" % l, [4 * D, D]) for l in range(2)]
    wb_a2 = dscr("wb_a2", [D, 384])
    wb_b = dscr("wb_b", [256, 2048])
    wb_qa = dscr("wb_qa", [D, 384])
    wb_qb2 = dscr("wb_qb2", [384, 8, 256])
    wb_bo = dscr("wb_bo", [D, D])
    kcache = dscr("kcache", [H, 128, SEQ])
    vcache = dscr("vcache", [H, 128, SEQ // 128, 128])
    xmid = dscr("xmid", [SEQ // 2, D], F32)
    lat_src = [dscr("lat_src%d" % i, [128, 1536]) for i in range(NT - FIRST_OWN)]
    lat_all = [dscr("lat_all%d" % i, [256, 1536]) for i in range(NT - FIRST_OWN)]

    ident_f = P.sb("ident_f", [128, 128], F32)
    ident_b = P.sb("ident_b", [128, 128], BF16)
    mask_f = P.sb("mask_f", [128, 128], F32)
    mask_b = P.sb("mask_b", [128, 128], BF16)
    ones_f = P.sb("ones_f", [128, 128], F32)
    ones_b = P.sb("ones_b", [128, 128], BF16)
    fones_b = P.sb("fones_b", [128, 128], BF16)
    flag = P.sb("flag", [128, 1], F32)
    cond = P.sb("cond", [128, 8], F32)
    inv_col = P.sb("inv_col", [64, 1], F32)
    vecA = P.sb("vecA", [128, 16, 8], F32)
    biasA = P.sb("biasA", [128, 10, 8], F32)
    nrmA = P.sb("nrmA", [128, 5, 8], F32)
    modA = P.sb("modA", [128, 5, 8], F32)
    modB = P.sb("modB", [128, 5, 8], F32)
    G = P.sb("G", [128, 4, D], F32)
    headg = P.sb("headg", [128, D], F32)
    latg = P.sb("latg", [128, 256], F32)
    qlatg = P.sb("qlatg", [128, 384], F32)
    gb = P.sb("gb", [128, 2, 8], F32)
    krT = P.sb("krT", [65, SEQ], BF16)
    WIF = P.sb("WIF", [128, 8, 16], BF16)
    Cf = P.sb("Cf", [128, 4, 130], F32)
    Cb = P.sb("Cb", [128, 4, 130], BF16)
    mst = P.sb("mst", [8, 1], F32)
    eps_t = P.sb("eps_t", [128, 1], F32)

    X = P.sb("X", [128, 4, D], F32)
    XN = P.sb("XN", [128, 4, D], BF16)
    HT = P.sb("HT", [128, 8, T], BF16)
    Ys = [P.sb("Y%d" % i, [128, D], F32) for i in range(2)]
    junk = P.sb("junk", [128, D], BF16)
    st4 = P.sb("st4", [128, 16], F32)
    ARENA_E = 31 * 1024
    arena = es.enter_context(nc.sbuf_tensor("arena", [128, ARENA_E], BF16))
    NWS = 4
    wslots = [P.sb("wslot%d" % i, [128, 4096], BF16) for i in range(NWS)]
    NKS = 2
    kslots = [P.sb("kslot%d" % i, [128, 4096], BF16) for i in range(NKS)]
    ANG = P.sb("ANG", [64, T], F32)
    KF = P.sb("KF", [64, T], F32)
    COS = P.sb("COS", [64, T], F32)
    SIN = P.sb("SIN", [64, T], F32)
    P.init_psum()

    state = {"ws": 0, "ks": 0, "aoff": 0}

    def aview(nelem_bf16, shape, dt, name):
        o = state["aoff"]
        assert o + nelem_bf16 <= ARENA_E, (name, o, nelem_bf16)
        state["aoff"] = o + nelem_bf16
        ap = arena[:, o:o + nelem_bf16]
        if dt == F32:
            ap = ap.bitcast(F32)
        if len(shape) == 3:
            ap = ap.rearrange("p (a b) -> p a b", a=shape[1])
        elif len(shape) == 4:
            ap = ap.rearrange("p (a b c) -> p a b c", a=shape[1], b=shape[2])
        return TT(ap, Buf(name))

    def wload(src_ap, a, bcols, src_buf):
        sl = wslots[state["ws"] % NWS]
        state["ws"] += 1
        v = sl.ap[:, 0:a * bcols].rearrange("p (a b) -> p a b", a=a)
        P.dma("sp", v, src_ap, [src_buf], sl)
        return TT(v, sl.b)

    def mm(psT, out_ap, lhsT, rhs, start, stop, reads):
        P.op("pe", lambda e: e.matmul(out_ap, lhsT=lhsT, rhs=rhs, start=start, stop=stop), reads=reads, writes=[psT])

    def trn(psT, out_ap, in_ap, idn, reads):
        P.op("pe", lambda e: e.transpose(out_ap, in_ap, idn.ap[:]), reads=list(reads) + [idn], writes=[psT])

    def act(out_ap, in_ap, func, reads, writes, scale=1.0, bias=None, accum=None):
        kw = {}
        if bias is not None:
            kw["bias"] = bias
        if accum is not None:
            kw["accum_out"] = accum
        P.op("act", lambda e: e.activation(out=out_ap, in_=in_ap, func=func, scale=scale, **kw), reads=reads, writes=writes)

    def tcopy(eng, out_ap, in_ap, reads, writes):
        P.op(eng, lambda e: e.tensor_copy(out=out_ap, in_=in_ap), reads=reads, writes=writes)

    def tt(eng, out_ap, in0, in1, op, reads, writes):
        P.op(eng, lambda e: e.tensor_tensor(out=out_ap, in0=in0, in1=in1, op=op), reads=reads, writes=writes)

    def ts(eng, out_ap, in0, s1, s2, op0, op1, reads, writes):
        if s2 is None:
            P.op(eng, lambda e: e.tensor_scalar(out=out_ap, in0=in0, scalar1=s1, scalar2=None, op0=op0), reads=reads, writes=writes)
        else:
            P.op(eng, lambda e: e.tensor_scalar(out=out_ap, in0=in0, scalar1=s1, scalar2=s2, op0=op0, op1=op1), reads=reads, writes=writes)

    def stt(eng, out_ap, in0, scalar, in1, op0, op1, reads, writes):
        P.op(eng, lambda e: e.scalar_tensor_tensor(out=out_ap, in0=in0, scalar=scalar, in1=in1, op0=op0, op1=op1), reads=reads, writes=writes)

    def memset(eng, ap, val, writes):
        P.op(eng, lambda e: e.memset(ap, val), reads=[], writes=writes)

    def rstd_from_ss(out_ap, ss_ap, n, rd, wr):
        act(out_ap, ss_ap, AF.Sqrt, rd + [eps_t], wr, scale=1.0 / n, bias=eps_t.ap[0:ss_ap.shape[0], 0:1])
        P.op("dve", lambda e: e.reciprocal(out=out_ap, in_=out_ap), reads=wr, writes=wr)

    def chk(n):
        if stage is not None and stage == n:
            raise _Stop()

    def body():
        P.dma("sp", ident_f.ap[:], ident_d[:, :], [], ident_f)
        P.dma("sp", mask_f.ap[:], mask_d[:, :], [], mask_f)
        P.dma("sp", flag.ap[:], flag_d[:, :], [], flag)
        P.dma("sp", cond.ap[:], cT_d[:, :], [], cond)
        P.dma("sp", inv_col.ap[:], inv_d[:, :], [], inv_col)
        P.dma("sp", biasA.ap[:, 0:8, :], ada_bA.rearrange("l v p k -> p (l v) k"), [], biasA)
        P.dma("sp", biasA.ap[:, 8:10, :], kv_ada_bA.rearrange("v p k -> p v k"), [], biasA)
        P.dma("sp", nrmA.ap[:, 0:4, :], normA.rearrange("l v p k -> p (l v) k"), [], nrmA)
        P.dma("sp", nrmA.ap[:, 4, :], kv_normA[:, :], [], nrmA)
        P.dma("sp", headg.ap[:], head_g[0:1, :].partition_broadcast(128), [], headg)
        P.dma("sp", latg.ap[:], kv_lat_g[0:1, :].partition_broadcast(128), [], latg)
        P.dma("sp", qlatg.ap[:], q_lat_g[0:1, :].partition_broadcast(128), [], qlatg)
        for i in range(2):
            P.dma("sp", gb.ap[:, i, :], gate_b[i, 0:1, :].partition_broadcast(128), [], gb)
        tcopy("dve", ident_b.ap[:], ident_f.ap[:], [ident_f], [ident_b])
        tcopy("dve", mask_b.ap[:], mask_f.ap[:], [mask_f], [mask_b])
        memset("dve", ones_f.ap[:], 1.0, [ones_f])
        memset("dve", ones_b.ap[:], 1.0, [ones_b])
        memset("dve", eps_t.ap[:], EPS, [eps_t])
        ts("dve", fones_b.ap[:], ones_f.ap[:], flag.ap[:, 0:1], None, ALU.mult, None, [ones_f, flag], [fones_b])
        memset("pool", krT.ap[64:65, :], 1.0, [krT])
        memset("pool", Cf.ap[:], 0.0, [Cf])
        memset("pool", mst.ap[:], 0.0, [mst])
        act(cond.ap[:], cond.ap[:], AF.Silu, [cond], [cond])

        def cast_rows(dst, src, nrows, step=256):
            for r0 in range(0, nrows, step):
                r1 = min(nrows, r0 + step)
                P.dma("pool", dst.ap[r0:r1], src[r0:r1], [], dst)

        cast_rows(wb_in, a_w_in, D)
        P.dma("pool", wb_a2.ap[:, 0:320], kv_w_a[:, :], [], wb_a2)
        P.dma("pool", wb_qb2.ap[:, :, 0:192], w_q_b.rearrange("r (h c) -> r h c", h=8), [], wb_qb2)
        cast_rows(wb_out, a_w_out, D)
        cast_rows(wb_w1[0], mlp_w1[0], D)
        cast_rows(wb_w2[0], mlp_w2[0], 4 * D)
        cast_rows(wb_b, kv_w_b, 256)
        cast_rows(wb_qa, w_q_a, D)
        cast_rows(wb_bo, b_w_out, D)
        cast_rows(wb_w1[1], mlp_w1[1], D)
        cast_rows(wb_w2[1], mlp_w2[1], 4 * D)

        state["aoff"] = 0
        cond_rep = aview(8 * 128 * 2, [128, 8, 128], F32, "cond_rep")
        aslab = [aview(8 * 512 * 2, [128, 8, 512], F32, "aslab%d" % i) for i in range(2)]
        brow = aview(512 * 2, [128, 512], F32, "brow")
        nrow = aview(512 * 2, [128, 512], F32, "nrow")
        rtmp = aview(8 * 64 * 2, [128, 8, 64], F32, "rtmp")
        rtmpb = aview(8 * 64, [128, 8, 64], BF16, "rtmpb")
        rq = aview(3 * 512 * 2, [128, 3, 8, 64], F32, "rq")
        rqb = aview(3 * 512, [128, 3, 8, 64], BF16, "rqb")
        prol_bufs = [cond_rep, brow, nrow, rtmp, rtmpb, rq, rqb] + aslab

        for k in range(8):
            tcopy("dve", cond_rep.ap[:, k, :], cond.ap[:, k:k + 1].to_broadcast([128, 128]), [cond], [cond_rep])

        P.dma("sp", rtmp.ap[:], kv_w_a[:, 256:320].rearrange("(k p) c -> p k c", p=128), [], rtmp)
        ts("dve", rtmpb.ap[:, :, 0:32], rtmp.ap[:, :, 32:64], -1.0, None, ALU.mult, None, [rtmp], [rtmpb])
        tcopy("dve", rtmpb.ap[:, :, 32:64], rtmp.ap[:, :, 0:32], [rtmp], [rtmpb])
        P.dma("act", wb_a2.ap[:, 320:384].rearrange("(k p) c -> p k c", p=128), rtmpb.ap[:], [rtmpb], wb_a2)
        for c3 in range(3):
            P.dma("sp", rq.ap[:, c3], w_q_b[c3 * 128:(c3 + 1) * 128, :].rearrange("p (h c) -> p h c", h=8)[:, :, 128:192], [], rq)
        ts("dve", rqb.ap[:, :, :, 0:32], rq.ap[:, :, :, 32:64], -1.0, None, ALU.mult, None, [rq], [rqb])
        tcopy("dve", rqb.ap[:, :, :, 32:64], rq.ap[:, :, :, 0:32], [rq], [rqb])
        for c3 in range(3):
            P.dma("act", wb_qb2.ap[c3 * 128:(c3 + 1) * 128, :, 192:256], rqb.ap[:, c3], [rqb], wb_qb2)

        chk(0)
        def ada_group(wsrc, col0, kind, idx, li, sidx):
            ab = state["abuf"]
            sl = ab["slabs"][state["as"] % len(ab["slabs"])]
            brow, nrow = ab["brow"], ab["nrow"]
            state["as"] += 1
            P.dma("sp", sl.ap[:], wsrc[:, col0:col0 + 512].rearrange("(k p) n -> p k n", p=128), [], sl)
            half = (col0 % 1024) // 512
            pt = P.ps()
            if kind == "A":
                for j in range(4):
                    for k in range(8):
                        mm(pt, pt.ap[:, j:j + 1], sl.ap[:, k, j * 128:(j + 1) * 128], cond.ap[:, k:k + 1], k == 0, k == 7, [sl, cond])
                tcopy("dve", vecA.ap[:, idx, half * 4:half * 4 + 4], pt.ap[:, 0:4], [pt], [vecA])
            else:
                for k in range(8):
                    mm(pt, pt.ap[:, :], cond_rep.ap[:, k, :], sl.ap[:, k, :], k == 0, k == 7, [sl, cond_rep])
                P.dma("sp", brow.ap[:], ada_bG[li, sidx, 0:1, half * 512:(half + 1) * 512].partition_broadcast(128), [], brow)
                P.dma("sp", nrow.ap[:], normG[li, sidx, 0:1, half * 512:(half + 1) * 512].partition_broadcast(128), [], nrow)
                tt("dve", brow.ap[:], pt.ap[:, :], brow.ap[:], ALU.add, [pt, brow], [brow])
                tt("dve", G.ap[:, idx, half * 512:(half + 1) * 512], brow.ap[:], nrow.ap[:], ALU.mult, [brow, nrow], [G])

        state["as"] = 0
        state["abuf"] = {"slabs": aslab, "brow": brow, "nrow": nrow}

        def ada_layer_groups(l):
            gl = []
            for v in range(6):
                for half in range(2):
                    col0 = v * 1024 + half * 512
                    if v in (2, 5):
                        gl.append((ada_w[l], col0, "G", l * 2 + (0 if v == 2 else 1), l, 0 if v == 2 else 1))
                    else:
                        gl.append((ada_w[l], col0, "A", l * 4 + {0: 0, 1: 1, 3: 2, 4: 3}[v], l, 0))
            return gl

        def ada_finish(i0, i1, mods):
            tt("dve", vecA.ap[:, i0:i1, :], vecA.ap[:, i0:i1, :], biasA.ap[:, i0:i1, :], ALU.add, [vecA, biasA], [vecA])
            for (mi, shi, sci) in mods:
                stt("dve", modA.ap[:, mi, :], vecA.ap[:, sci, :], 1.0, nrmA.ap[:, mi, :], ALU.add, ALU.mult, [vecA, nrmA], [modA])
                tcopy("dve", modB.ap[:, mi, :], vecA.ap[:, shi, :], [vecA], [modB])

        for g in ada_layer_groups(0):
            ada_group(*g)
        ada_finish(0, 4, ((0, 0, 1), (1, 2, 3)))
        deferred = [(kv_ada_w, v * 1024 + half * 512, "A", 8 + v, 0, 0) for v in range(2) for half in range(2)] + ada_layer_groups(1)
        if stage == 1:
            tcopy("dve", X.ap[:, :, :], G.ap[:, :, :], [G], [X])
            tcopy("dve", X.ap[:, 0, 0:40], modA.ap[:].rearrange("p a k -> p (a k)"), [modA], [X])
            tcopy("dve", X.ap[:, 0, 40:80], modB.ap[:].rearrange("p a k -> p (a k)"), [modB], [X])
        chk(1)
        P.dma("sp", WIF.ap[:], wb_in.ap[:, 2048:2064].rearrange("(k p) n -> p k n", p=128), [wb_in], WIF)

        state["aoff"] = 0
        QT = aview(4 * T, [128, 4, T], BF16, "QT")
        KT = aview(4 * T, [128, 4, T], BF16, "KT")
        KE = aview(4 * 4 * 128, [128, 4, 4, 128], BF16, "KE")
        KO = aview(4 * 4 * 128, [128, 4, 4, 128], BF16, "KO")
        VE = aview(4 * 8 * 130, [128, 4, 8, 130], BF16, "VE")
        VW = aview(4 * 8 * 130, [128, 4, 8, 130], BF16, "VW")
        OG = aview(4 * D, [128, 4, D], BF16, "OG")
        STt = aview(8 * 128, [128, 8, 128], BF16, "ST")
        HH = aview(2 * D, [128, 8, 128], F32, "HH")
        SQ = aview(2 * D, [128, 8, 128], F32, "SQ")
        GT = aview(2 * 256, [128, 256], F32, "GT")
        GH = aview(2 * 768, [128, 768], F32, "GH")
        M0 = [QT, KT, KE, KO, VE, VW, OG, STt, HH, SQ, GT, GH]
        _save = state["aoff"]
        state["aoff"] = 2048
        browL = aview(1024, [128, 512], F32, "browL")
        nrowL = aview(1024, [128, 512], F32, "nrowL")
        state["aoff"] = 16512
        aslabL = aview(8 * 512 * 2, [128, 8, 512], F32, "aslabL")
        state["aoff"] = _save
        LATE = [browL, nrowL, aslabL, cond_rep]
        m0_end = state["aoff"]
        state["aoff"] = 0
        HID = aview(32 * T, [128, 32, T], BF16, "HID")
        RT = [aview(T * 2, [128, T], F32, "RT%d" % i) for i in range(2)]
        MLPB = [HID] + RT
        state["aoff"] = 0
        CKV = aview(4 * 256 * 2, [128, 4, 256], F32, "CKV")
        CKN = aview(4 * 256, [128, 4, 256], BF16, "CKN")
        CKT = aview(2 * T, [128, 2, T], BF16, "CKT")
        KEXP = aview(8 * T, [128, 8, T], BF16, "KEXP")
        VEXP = aview(8 * T, [128, 8, 4, 128], BF16, "VEXP")
        R1 = aview(T * 2, [128, T], F32, "R1")
        R2 = aview(T * 2, [128, T], F32, "R2")
        KVB = [CKV, CKN, CKT, KEXP, VEXP, R1, R2]
        kv_end = state["aoff"]
        state["aoff"] = 0
        CQ = aview(4 * 384 * 2, [128, 4, 384], F32, "CQ")
        CQN = aview(4 * 384, [128, 4, 384], BF16, "CQN")
        CQT = aview(3 * T, [128, 3, T], BF16, "CQT")
        QNT = aview(8 * T, [128, 8, T], BF16, "QNT")
        QRT = aview(8 * T, [128, 8, T], BF16, "QRT")
        OT = aview(8 * T, [128, 8, T], BF16, "OT")
        PTs = [aview(T, [128, T], BF16, "PT%d" % i) for i in range(4)]
        RDEN = aview(T * 2, [128, T], F32, "RDEN")
        AR1 = aview(T * 2, [128, T], F32, "AR1")
        AR2 = aview(T * 2, [128, T], F32, "AR2")
        ATTB = [CQ, CQN, CQT, QNT, QRT, OT, RDEN, AR1, AR2] + PTs

        P.carry(prol_bufs, M0)
        P.carry(prol_bufs, [browL, nrowL, aslabL])

        def norm_to_HT(mi):
            memset("dve", st4.ap[:, 0:4], 0.0, [st4])
            for s in range(4):
                act(junk.ap[:], X.ap[:, s, :], AF.Square, [X], [junk, st4], accum=st4.ap[:, s:s + 1])
            rstd_from_ss(st4.ap[:, 0:4], st4.ap[:, 0:4], D, [st4], [st4])
            for s in range(4):
                if s % 2 == 0:
                    ts("dve", XN.ap[:, s, :], X.ap[:, s, :], st4.ap[:, s:s + 1], None, ALU.mult, None, [X, st4], [XN])
                else:
                    act(XN.ap[:, s, :], X.ap[:, s, :], AF.Identity, [X, st4], [XN], scale=st4.ap[:, s:s + 1])
            transpose_to_HT(XN, modA.ap[:, mi, :], modB.ap[:, mi, :], [modA, modB])

        def transpose_to_HT(src, a_ap, b_ap, extra):
            for k in range(8):
                pt = P.ps()
                pv = pt.ap[:].bitcast(BF16)
                for s in range(4):
                    trn(pt, pv[:, s * 128:(s + 1) * 128], src.ap[:, s, k * 128:(k + 1) * 128], ident_b, [src])
                if a_ap is not None:
                    if k % 2 == 0:
                        act(HT.ap[:, k, :], pv[:, 0:T], AF.Identity, [pt] + extra, [HT], scale=a_ap[:, k:k + 1], bias=b_ap[:, k:k + 1])
                    else:
                        ts("dve", HT.ap[:, k, :], pv[:, 0:T], a_ap[:, k:k + 1], b_ap[:, k:k + 1], ALU.mult, ALU.add, [pt] + extra, [HT])
                else:
                    if k % 2 == 0:
                        act(HT.ap[:, k, :], pv[:, 0:T], AF.Copy, [pt], [HT])
                    else:
                        tcopy("dve", HT.ap[:, k, :], pv[:, 0:T], [pt], [HT])

        def out_proj_residual(lhs_of, wsrc, gi, lhs_reads):
            w0 = wload(wsrc.ap[:, 0:512].rearrange("(k p) n -> p k n", p=128), 8, 512, wsrc)
            w1 = wload(wsrc.ap[:, 512:1024].rearrange("(k p) n -> p k n", p=128), 8, 512, wsrc)
            for s in range(4):
                Y = Ys[s % 2]
                for half, w in enumerate((w0, w1)):
                    pt = P.ps()
                    for k in range(8):
                        mm(pt, pt.ap[:, :], lhs_of(k, s), w.ap[:, k, :], k == 0, k == 7, lhs_reads + [w])
                    finish_half(pt, Y, s, half)
                residual_update(Y, s, gi)

        def finish_half(pt, Y, s, half):
            if half == 0:
                memset("dve", st4.ap[:, 8:10], 0.0, [st4])
            act(junk.ap[:, 0:512], pt.ap[:, :], AF.Square, [pt], [junk, st4], accum=st4.ap[:, 8 + half:9 + half])
            tcopy("dve", Y.ap[:, half * 512:(half + 1) * 512], pt.ap[:, :], [pt], [Y])

        def residual_update(Y, s, gi):
            tt("dve", st4.ap[:, 10:11], st4.ap[:, 8:9], st4.ap[:, 9:10], ALU.add, [st4], [st4])
            rstd_from_ss(st4.ap[:, 10:11], st4.ap[:, 10:11], D, [st4], [st4])
            stt("dve", Y.ap[:], Y.ap[:], st4.ap[:, 10:11], G.ap[:, gi, :], ALU.mult, ALU.mult, [Y, st4, G], [Y])
            tt("dve", X.ap[:, s, :], X.ap[:, s, :], Y.ap[:], ALU.add, [X, Y], [X])

        def mlp(l, mi, gi, hook=None):
            norm_to_HT(mi)
            if hook is not None:
                hook()
            for g8 in range(8):
                w = wload(wb_w1[l].ap[:, g8 * 512:(g8 + 1) * 512].rearrange("(k p) n -> p k n", p=128), 8, 512, wb_w1[l])
                for j in range(4):
                    m = g8 * 4 + j
                    pt = P.ps()
                    for k in range(8):
                        mm(pt, pt.ap[:, :], w.ap[:, k, j * 128:(j + 1) * 128], HT.ap[:, k, :], k == 0, k == 7, [w, HT])
                    rt = RT[m % 2]
                    act(rt.ap[:], pt.ap[:, :], AF.Relu, [pt], [rt])
                    tt("pool", HID.ap[:, m, :], rt.ap[:], rt.ap[:], ALU.mult, [rt], [HID])
            memset("dve", st4.ap[:, 4:8], 0.0, [st4])
            memset("dve", st4.ap[:, 12:16], 0.0, [st4])
            for half in range(2):
                acc, got = P.acquire(4)
                for kg in range(4):
                    w = wload(wb_w2[l].ap[kg * 1024:(kg + 1) * 1024, half * 512:(half + 1) * 512].rearrange("(k p) n -> p k n", p=128), 8, 512, wb_w2[l])
                    for kk in range(8):
                        kc = kg * 8 + kk
                        for s in range(4):
                            mm(acc[s], acc[s].ap[:, :], HID.ap[:, kc, s * 128:(s + 1) * 128], w.ap[:, kk, :], kc == 0, kc == 31, [HID, w])
                for s in range(4):
                    finish_half_mlp(acc[s], s, half)
                P.release(got)
            for s in range(4):
                residual_update_mlp(s, gi)

        state["aoff"] = 32 * T + 2 * T * 2
        Y2 = aview(4 * D * 2, [128, 4, D], F32, "Y2") if state["aoff"] + 4 * D * 2 <= ARENA_E else None
        assert Y2 is not None
        MLPB.append(Y2)

        def finish_half_mlp(pt, s, half):
            act(junk.ap[:, 0:512], pt.ap[:, :], AF.Square, [pt], [junk, st4], accum=st4.ap[:, ((4 + s) if half == 0 else (12 + s)):((5 + s) if half == 0 else (13 + s))])
            tcopy("dve", Y2.ap[:, s, half * 512:(half + 1) * 512], pt.ap[:, :], [pt], [Y2])

        def residual_update_mlp(s, gi):
            tt("dve", st4.ap[:, 10:11], st4.ap[:, 4 + s:5 + s], st4.ap[:, 12 + s:13 + s], ALU.add, [st4], [st4])
            rstd_from_ss(st4.ap[:, 10:11], st4.ap[:, 10:11], D, [st4], [st4])
            stt("dve", Y2.ap[:, s, :], Y2.ap[:, s, :], st4.ap[:, 10:11], G.ap[:, gi, :], ALU.mult, ALU.mult, [Y2, st4, G], [Y2])
            tt("dve", X.ap[:, s, :], X.ap[:, s, :], Y2.ap[:, s, :], ALU.add, [X, Y2], [X])

        def mixer0(mt, state_only=False):
            ms_eng = "dve" if state_only else "pool"
            memset(ms_eng, VE.ap[:, :, :, 128:129], 1.0, [VE])
            memset(ms_eng, KE.ap[:, :, :, 64:128], 0.0, [KE])
            memset(ms_eng, KO.ap[:, :, :, 0:64], 0.0, [KO])
            norm_to_HT(0)

            def proj_q():
                wq = wload(wb_in.ap[:, 0:512].rearrange("(k p) n -> p k n", p=128), 8, 512, wb_in)
                for m in range(4):
                    pt = P.ps()
                    for k in range(8):
                        mm(pt, pt.ap[:, :], wq.ap[:, k, m * 128:(m + 1) * 128], HT.ap[:, k, :], k == 0, k == 7, [wq, HT])
                    act(QT.ap[:, m, :], pt.ap[:, :], AF.Copy, [pt], [QT], scale=0.125)

            def proj_v(hv):
                wv = wload(wb_in.ap[:, 1024 + hv * 512:1024 + (hv + 1) * 512].rearrange("(k p) n -> p k n", p=128), 8, 512, wb_in)
                for s in range(4):
                    pt = P.ps()
                    for k in range(8):
                        mm(pt, pt.ap[:, :], HT.ap[:, k, s * 128:(s + 1) * 128], wv.ap[:, k, :], k == 0, k == 7, [wv, HT])
                    ptv = pt.ap[:, :].rearrange("p (h e) -> p h e", h=4)
                    tcopy("dve", VE.ap[:, s, hv * 4:(hv + 1) * 4, 0:128], ptv, [pt], [VE])

            def proj_o(ho):
                wo = wload(wb_in.ap[:, 2064 + ho * 512:2064 + (ho + 1) * 512].rearrange("(k p) n -> p k n", p=128), 8, 512, wb_in)
                for s in range(4):
                    pt = P.ps()
                    for k in range(8):
                        mm(pt, pt.ap[:, :], HT.ap[:, k, s * 128:(s + 1) * 128], wo.ap[:, k, :], k == 0, k == 7, [wo, HT])
                    act(OG.ap[:, s, ho * 512:(ho + 1) * 512], pt.ap[:, :], AF.Sigmoid, [pt], [OG])

            def proj_k():
                wk = wload(wb_in.ap[:, 512:1024].rearrange("(k p) n -> p k n", p=128), 8, 512, wb_in)
                if not state_only:
                    for m in range(4):
                        pt = P.ps()
                        for k in range(8):
                            mm(pt, pt.ap[:, :], wk.ap[:, k, m * 128:(m + 1) * 128], HT.ap[:, k, :], k == 0, k == 7, [wk, HT])
                        tcopy("dve", KT.ap[:, m, :], pt.ap[:, :], [pt], [KT])
                for s in range(4):
                    pt = P.ps()
                    for k in range(8):
                        mm(pt, pt.ap[:, :], HT.ap[:, k, s * 128:(s + 1) * 128], wk.ap[:, k, :], k == 0, k == 7, [wk, HT])
                    ptv = pt.ap[:, :].rearrange("p (j r d) -> p j r d", j=4, r=2)
                    tcopy("dve", KE.ap[:, s, :, 0:64], ptv[:, :, 0, :], [pt], [KE])
                    tcopy("dve", KO.ap[:, s, :, 64:128], ptv[:, :, 1, :], [pt], [KO])

            if state_only:
                sched = {1: [lambda: proj_v(0)], 2: [lambda: proj_v(1)], 3: [proj_k], 4: []}
            else:
                sched = {1: [proj_q], 2: [lambda: proj_v(0)], 3: [lambda: proj_v(1), lambda: proj_o(0)], 4: [lambda: proj_o(1), proj_k]}

            def hook(i):
                for f in sched[i]:
                    f()
            pg = P.ps()
            for s in range(4):
                for k in range(8):
                    mm(pg, pg.ap[:, s * 16:(s + 1) * 16], HT.ap[:, k, s * 128:(s + 1) * 128], WIF.ap[:, k, :], k == 0, k == 7, [HT, WIF])
            gi_v = GT.ap[:, 0:32].rearrange("p (s h) -> p s h", s=4)
            gf_v = GT.ap[:, 32:64].rearrange("p (s h) -> p s h", s=4)
            pgv = pg.ap[:, 0:64].rearrange("p (s g) -> p s g", s=4)
            tt("dve", gi_v, pgv[:, :, 0:8], gb.ap[:, 0:1, :].to_broadcast([128, 4, 8]), ALU.add, [pg, gb], [GT])
            tt("dve", gf_v, pgv[:, :, 8:16], gb.ap[:, 1:2, :].to_broadcast([128, 4, 8]), ALU.add, [pg, gb], [GT])
            act(gf_v, gf_v, AF.Exp, [GT], [GT], scale=-1.0)
            act(gf_v, gf_v, AF.Ln, [GT, ones_f], [GT], scale=1.0, bias=ones_f.ap[:, 0:1])
            ts("dve", gf_v, gf_v, -1.0, None, ALU.mult, None, [GT], [GT])
            hook(1)
            pb = P.ps()
            for s in range(4):
                mm(pb, pb.ap[:, s * 8:(s + 1) * 8], mask_f.ap[:], GT.ap[:, 32 + s * 8:32 + (s + 1) * 8], True, True, [mask_f, GT])
            bb_v = GT.ap[:, 64:96].rearrange("p (s h) -> p s h", s=4)
            a_v = GT.ap[:, 96:128].rearrange("p (s h) -> p s h", s=4)
            tcopy("dve", GT.ap[:, 64:96], pb.ap[:, 0:32], [pb], [GT])
            tt("dve", GT.ap[:, 96:128], GT.ap[:, 0:32], GT.ap[:, 64:96], ALU.subtract, [GT], [GT])
            hook(2)
            pa = P.ps()
            pbt = P.ps()
            for s in range(4):
                trn(pa, pa.ap[0:8, s * 128:(s + 1) * 128], GT.ap[:, 96 + s * 8:96 + (s + 1) * 8], ident_f, [GT])
                trn(pbt, pbt.ap[0:8, s * 128:(s + 1) * 128], GT.ap[:, 64 + s * 8:64 + (s + 1) * 8], ident_f, [GT])
            P.op("dve", lambda e: e.tensor_reduce(out=GH.ap[0:8, 0:4], in_=pa.ap[0:8, 0:512].rearrange("p (s t) -> p s t", s=4), axis=AX.X, op=ALU.max), reads=[pa], writes=[GH])
            tcopy("dve", GH.ap[0:8, 4:8], pbt.ap[0:8, 0:512].rearrange("p (s t) -> p s t", s=4)[:, :, 127], [pbt], [GH])
            for c in range(4):
                tt("dve", GH.ap[0:8, 8 + c:9 + c], GH.ap[0:8, c:c + 1], mst.ap[:], ALU.max, [GH, mst], [GH])
                tt("dve", GH.ap[0:8, 12 + c:13 + c], mst.ap[:], GH.ap[0:8, 8 + c:9 + c], ALU.subtract, [GH, mst], [GH])
                tt("dve", mst.ap[:], GH.ap[0:8, 4 + c:5 + c], GH.ap[0:8, 8 + c:9 + c], ALU.add, [GH], [mst])
            act(GH.ap[0:8, 12:16], GH.ap[0:8, 12:16], AF.Exp, [GH], [GH])
            Rv = GH.ap[0:8, 32:96].rearrange("p (c x) -> p c x", c=4)
            for c in range(4):
                ts("dve", Rv[:, c, 0:8], ident_f.ap[0:8, 0:8], GH.ap[0:8, 8 + c:9 + c], None, ALU.mult, None, [ident_f, GH], [GH])
                ts("dve", Rv[:, c, 8:12], ident_f.ap[0:8, 0:8].rearrange("p (j r) -> p j r", r=2)[:, :, 0], GH.ap[0:8, 12 + c:13 + c], None, ALU.mult, None, [ident_f, GH], [GH])
                ts("dve", Rv[:, c, 12:16], ident_f.ap[0:8, 0:8].rearrange("p (j r) -> p j r", r=2)[:, :, 1], GH.ap[0:8, 12 + c:13 + c], None, ALU.mult, None, [ident_f, GH], [GH])
            hook(3)
            pm = P.ps()
            mm(pm, pm.ap[:, 0:64], ones_f.ap[0:8, :], GH.ap[0:8, 32:96], True, True, [ones_f, GH])
            pmv = pm.ap[:, 0:64].rearrange("p (c x) -> p c x", c=4)
            w_v = GT.ap[:, 128:160].rearrange("p (s h) -> p s h", s=4)
            e_v = GT.ap[:, 160:192].rearrange("p (s h) -> p s h", s=4)
            dec_v = GT.ap[:, 192:224].rearrange("p (c x) -> p c x", c=4)
            tt("dve", w_v, a_v, pmv[:, :, 0:8], ALU.subtract, [GT, pm], [GT])
            stt("dve", e_v, bb_v, -1.0, pmv[:, :, 0:8], ALU.mult, ALU.subtract, [GT, pm], [GT])
            tcopy("dve", dec_v, pmv[:, :, 8:16], [pm], [GT])
            act(GT.ap[:, 128:192], GT.ap[:, 128:192], AF.Exp, [GT], [GT])

            if stage == 2:
                for s_ in range(4):
                    tcopy("dve", X.ap[:, s_, :], HT.ap[:, 2 * s_:2 * s_ + 2, :].rearrange("p a t -> p (a t)"), [HT], [X])
                tcopy("dve", X.ap[:, 0, 0:256], GT.ap[:], [GT], [X])
            chk(2)
            hook(4)
            if not state_only:
                for s in range(4):
                    tt("dve", OG.ap[:, s, :], OG.ap[:, s, :], headg.ap[:], ALU.mult, [OG, headg], [OG])
            for s in range(4):
                tt("dve", VW.ap[:, s], VE.ap[:, s], w_v[:, s, :].unsqueeze(2).to_broadcast([128, 8, 130]), ALU.mult, [VE, GT], [VW])

            chk(3)
            for s in range(4):
                sc = slice(s * 128, (s + 1) * 128)
                tt("dve", Cf.ap[0:64], Cf.ap[0:64], dec_v[0:64, s, 0:4].unsqueeze(2).to_broadcast([64, 4, 130]), ALU.mult, [Cf, GT], [Cf])
                tt("dve", Cf.ap[64:128], Cf.ap[64:128], dec_v[64:128, s, 4:8].unsqueeze(2).to_broadcast([64, 4, 130]), ALU.mult, [Cf, GT], [Cf])
                if not state_only:
                    act(Cb.ap[:], Cf.ap[:], AF.Copy, [Cf], [Cb])
                for r in (range(2) if not state_only else ()):
                    pS = P.ps()
                    for j in range(4):
                        mm(pS, pS.ap[:, j * 128:(j + 1) * 128], KT.ap[64 * r:64 * r + 64, j, sc], QT.ap[64 * r:64 * r + 64, j, sc], True, True, [KT, QT])
                    tt("dve", STt.ap[:, :, :].rearrange("p (j r) t -> p j r t", r=2)[:, :, r, :],
                       pS.ap[:, :].rearrange("p (j t) -> p j t", j=4), mask_b.ap[:, None, :].to_broadcast([128, 4, 128]), ALU.mult, [pS, mask_b], [STt])
                for (h0, nh) in (((0, 3), (3, 3), (6, 2)) if not state_only else ()):
                    pN = P.ps()
                    for hh in range(nh):
                        h = h0 + hh
                        j, r = h // 2, h % 2
                        mm(pN, pN.ap[:, hh * 129:(hh + 1) * 129], QT.ap[64 * r:64 * r + 64, j, sc], Cb.ap[64 * r:64 * r + 64, j, 0:129], True, False, [QT, Cb])
                        mm(pN, pN.ap[:, hh * 129:(hh + 1) * 129], STt.ap[:, h, :], VW.ap[:, s, h, 0:129], False, True, [STt, VW])
                    pNv = pN.ap[:, 0:nh * 129].rearrange("p (h e) -> p h e", h=nh)
                    dd = GT.ap[:, 224:224 + nh]
                    act(dd, pNv[:, :, 128], AF.Abs, [pN], [GT])
                    tt("dve", dd, dd, e_v[:, s, h0:h0 + nh], ALU.max, [GT], [GT])
                    P.op("dve", lambda e, dd=dd: e.reciprocal(out=dd, in_=dd), reads=[GT], writes=[GT])
                    tt("dve", HH.ap[:, h0:h0 + nh, :], pNv[:, :, 0:128], dd.unsqueeze(2).to_broadcast([128, nh, 128]), ALU.mult, [pN, GT], [HH])
                for jb in range(2):
                    pC = P.ps()
                    for jj in range(2):
                        j = jb * 2 + jj
                        mm(pC, pC.ap[:, jj * 129:(jj + 1) * 129], KE.ap[:, s, j, :], VW.ap[:, s, 2 * j, 0:129], True, False, [KE, VW])
                        mm(pC, pC.ap[:, jj * 129:(jj + 1) * 129], KO.ap[:, s, j, :], VW.ap[:, s, 2 * j + 1, 0:129], False, True, [KO, VW])
                    tt("dve", Cf.ap[:, jb * 2:jb * 2 + 2, 0:129], Cf.ap[:, jb * 2:jb * 2 + 2, 0:129], pC.ap[:, 0:258].rearrange("p (j e) -> p j e", j=2), ALU.add, [Cf, pC], [Cf])
                if state_only:
                    continue
                memset("pool", GT.ap[:, 232:240], 0.0, [GT])
                for h in range(8):
                    act(SQ.ap[:, h, :], HH.ap[:, h, :], AF.Square, [HH], [SQ, GT], accum=GT.ap[:, 232 + h:233 + h])
                rstd_from_ss(GT.ap[:, 232:240], GT.ap[:, 232:240], 128, [GT], [GT])
                for h in range(8):
                    act(SQ.ap[:, h, :], HH.ap[:, h, :], AF.Identity, [HH, GT], [SQ], scale=GT.ap[:, 232 + h:233 + h])
                tt("pool", XN.ap[:, s, :].rearrange("p (h e) -> p h e", h=8), SQ.ap[:], OG.ap[:, s, :].rearrange("p (h e) -> p h e", h=8), ALU.mult, [SQ, OG], [XN])
            if state_only:
                return
            chk(4)
            transpose_to_HT(XN, None, None, [])
            out_proj_residual(lambda k, s: HT.ap[:, k, s * 128:(s + 1) * 128], wb_out, 0, [HT])
            chk(5)

        def rope_tables(mt):
            t0 = mt * T
            ji = junk.ap[0:64, :].bitcast(I32)
            P.dma("sp", ji, pos[0:1, t0:t0 + T].partition_broadcast(64), [], junk)
            tcopy("dve", ANG.ap[:], ji, [junk], [ANG])
            ts("dve", ANG.ap[:], ANG.ap[:], inv_col.ap[:, 0:1], None, ALU.mult, None, [ANG, inv_col], [ANG])
            for (dst, shift) in ((SIN, 0.0), (COS, PI / 2)):
                ts("dve", dst.ap[:], ANG.ap[:], shift, None, ALU.add, None, [ANG], [dst])
                ts("dve", KF.ap[:], dst.ap[:], 1.0 / TWO_PI, None, ALU.mult, None, [dst], [KF])
                tcopy("dve", ji, KF.ap[:], [KF], [junk])
                tcopy("dve", KF.ap[:], ji, [junk], [KF])
                stt("dve", dst.ap[:], KF.ap[:], -6.28125, dst.ap[:], ALU.mult, ALU.add, [KF, dst], [dst])
                stt("dve", dst.ap[:], KF.ap[:], -0.0019353071795864769, dst.ap[:], ALU.mult, ALU.add, [KF, dst], [dst])
                ts("dve", KF.ap[:], dst.ap[:], PI, None, ALU.is_gt, None, [dst], [KF])
                stt("dve", dst.ap[:], KF.ap[:], -TWO_PI, dst.ap[:], ALU.mult, ALU.add, [KF, dst], [dst])
                ts("dve", KF.ap[:], dst.ap[:], -PI, None, ALU.is_lt, None, [dst], [KF])
                stt("dve", dst.ap[:], KF.ap[:], TWO_PI, dst.ap[:], ALU.mult, ALU.add, [KF, dst], [dst])
                act(dst.ap[:], dst.ap[:], AF.Sin, [dst], [dst])

        def kv_phase(mt):
            t0 = mt * T
            norm_to_HT(4)
            wa = wload(wb_a2.ap[:, :].rearrange("(k p) n -> p k n", p=128), 8, 384, wb_a2)
            memset("dve", st4.ap[:, 0:4], 0.0, [st4])
            for s in range(4):
                pt = P.ps()
                for k in range(8):
                    mm(pt, pt.ap[:, 0:256], HT.ap[:, k, s * 128:(s + 1) * 128], wa.ap[:, k, 0:256], k == 0, k == 7, [HT, wa])
                act(junk.ap[:, 0:256], pt.ap[:, 0:256], AF.Square, [pt], [junk, st4], accum=st4.ap[:, s:s + 1])
                tcopy("dve", CKV.ap[:, s, :], pt.ap[:, 0:256], [pt], [CKV])
            rstd_from_ss(st4.ap[:, 0:4], st4.ap[:, 0:4], 256, [st4], [st4])
            for s in range(4):
                stt("dve", CKN.ap[:, s, :], CKV.ap[:, s, :], st4.ap[:, s:s + 1], latg.ap[:], ALU.mult, ALU.mult, [CKV, st4, latg], [CKN])
            for c in range(2):
                pt = P.ps()
                pv = pt.ap[:].bitcast(BF16)
                for s in range(4):
                    trn(pt, pv[:, s * 128:(s + 1) * 128], CKN.ap[:, s, c * 128:(c + 1) * 128], ident_b, [CKN])
                tcopy("dve", CKT.ap[:, c, :], pv[:, 0:T], [pt], [CKT])
            pA = P.ps()
            pB = P.ps()
            for k in range(8):
                mm(pA, pA.ap[0:64, :], wa.ap[:, k, 256:320], HT.ap[:, k, :], k == 0, k == 7, [HT, wa])
            for k in range(8):
                mm(pB, pB.ap[0:64, :], wa.ap[:, k, 320:384], HT.ap[:, k, :], k == 0, k == 7, [HT, wa])
            tt("dve", R1.ap[0:64], pA.ap[0:64, :], COS.ap[0:64], ALU.mult, [pA, COS], [R1])
            tt("dve", R2.ap[0:64], pB.ap[0:64, :], SIN.ap[0:64], ALU.mult, [pB, SIN], [R2])
            tt("pool", krT.ap[0:64, t0:t0 + T], R1.ap[0:64], R2.ap[0:64], ALU.add, [R1, R2], [krT])
            ti = mt - FIRST_OWN
            P.dma("pool", lat_src[ti].ap[:, 0:1024].rearrange("p (c t) -> p c t", c=2), CKT.ap[:], [CKT], lat_src[ti])
            P.dma("pool", lat_src[ti].ap[0:64, 1024:1536], krT.ap[0:64, t0:t0 + T], [krT], lat_src[ti])
            P.collective(lambda e, ti=ti: e.collective_compute("AllGather", ALU.bypass, replica_groups=[[0, 1], [2, 3], [4, 5], [6, 7]],
                                                               ins=[lat_src[ti].ap[:, :]], outs=[lat_all[ti].ap[:, :]]), [lat_src[ti]], lat_all[ti])
            expand_kv(mt)

        def expand_kv(mt):
            t0 = mt * T
            wbk = wload(wb_b.ap[:, :].rearrange("(c p) n -> p c n", p=128), 2, 2048, wb_b)
            wbv = wbk.ap[:, :, :].rearrange("p c (h x) -> p c h x", h=8)
            for h in range(8):
                pt = P.ps()
                for c in range(2):
                    mm(pt, pt.ap[:, :], wbv[:, c, h, 0:128], CKT.ap[:, c, :], c == 0, c == 1, [wbk, CKT])
                if h % 2 == 0:
                    act(KEXP.ap[:, h, :], pt.ap[:, :], AF.Copy, [pt], [KEXP])
                else:
                    tcopy("dve", KEXP.ap[:, h, :], pt.ap[:, :], [pt], [KEXP])
            for s in range(4):
                for hv in range(2):
                    pt = P.ps()
                    for hh in range(4):
                        for c in range(2):
                            h = hv * 4 + hh
                            mm(pt, pt.ap[:, hh * 128:(hh + 1) * 128], CKT.ap[:, c, s * 128:(s + 1) * 128], wbv[:, c, h, 128:256], c == 0, c == 1, [wbk, CKT])
                    ptv = pt.ap[:, :].rearrange("p (h e) -> p h e", h=4)
                    if hv == 0:
                        act(VEXP.ap[:, hv * 4:(hv + 1) * 4, s, :], ptv, AF.Copy, [pt], [VEXP])
                    else:
                        tcopy("dve", VEXP.ap[:, hv * 4:(hv + 1) * 4, s, :], ptv, [pt], [VEXP])
            for h in range(8):
                P.dma("pool", kcache.ap[h, :, t0:t0 + T], KEXP.ap[:, h, :], [KEXP], kcache)
                P.dma("pool", vcache.ap[h, :, mt * 4:(mt + 1) * 4, :], VEXP.ap[:, h, :, :], [VEXP], vcache)

        def mixer1(mt):
            t0 = mt * T
            norm_to_HT(2)
            wqa = wload(wb_qa.ap[:, :].rearrange("(k p) n -> p k n", p=128), 8, 384, wb_qa)
            memset("dve", st4.ap[:, 0:4], 0.0, [st4])
            for s in range(4):
                pt = P.ps()
                for k in range(8):
                    mm(pt, pt.ap[:, 0:384], HT.ap[:, k, s * 128:(s + 1) * 128], wqa.ap[:, k, :], k == 0, k == 7, [HT, wqa])
                act(junk.ap[:, 0:384], pt.ap[:, 0:384], AF.Square, [pt], [junk, st4], accum=st4.ap[:, s:s + 1])
                tcopy("dve", CQ.ap[:, s, :], pt.ap[:, 0:384], [pt], [CQ])
            rstd_from_ss(st4.ap[:, 0:4], st4.ap[:, 0:4], 384, [st4], [st4])
            for s in range(4):
                stt("dve", CQN.ap[:, s, :], CQ.ap[:, s, :], st4.ap[:, s:s + 1], qlatg.ap[:], ALU.mult, ALU.mult, [CQ, st4, qlatg], [CQN])
            for c in range(3):
                pt = P.ps()
                pv = pt.ap[:].bitcast(BF16)
                for s in range(4):
                    trn(pt, pv[:, s * 128:(s + 1) * 128], CQN.ap[:, s, c * 128:(c + 1) * 128], ident_b, [CQN])
                tcopy("dve", CQT.ap[:, c, :], pv[:, 0:T], [pt], [CQT])
            memset("pool", QRT.ap[64:65, :, :], 0.0, [QRT])
            for hg in range(2):
                wqb = wload(wb_qb2.ap[:, hg * 4:(hg + 1) * 4, :].rearrange("(c p) h x -> p c (h x)", p=128), 3, 1024, wb_qb2)
                wv = wqb.ap[:, :, :].rearrange("p c (h x) -> p c h x", h=4)
                for hh in range(4):
                    h = hg * 4 + hh
                    pt = P.ps()
                    for c in range(3):
                        mm(pt, pt.ap[:, :], wv[:, c, hh, 0:128], CQT.ap[:, c, :], c == 0, c == 2, [wqb, CQT])
                    act(QNT.ap[:, h, :], pt.ap[:, :], AF.Copy, [pt], [QNT])
                    pA = P.ps()
                    pB = P.ps()
                    for c in range(3):
                        mm(pA, pA.ap[0:64, :], wv[:, c, hh, 128:192], CQT.ap[:, c, :], c == 0, c == 2, [wqb, CQT])
                    for c in range(3):
                        mm(pB, pB.ap[0:64, :], wv[:, c, hh, 192:256], CQT.ap[:, c, :], c == 0, c == 2, [wqb, CQT])
                    tt("dve", AR1.ap[0:64], pA.ap[0:64, :], COS.ap[0:64], ALU.mult, [pA, COS], [AR1])
                    tt("dve", AR2.ap[0:64], pB.ap[0:64, :], SIN.ap[0:64], ALU.mult, [pB, SIN], [AR2])
                    tt("pool", QRT.ap[0:64, h, :], AR1.ap[0:64], AR2.ap[0:64], ALU.add, [AR1, AR2], [QRT])
            nkb = 4 * (mt + 1)
            npiece = (nkb + 15) // 16
            accs, got = P.acquire(4)
            LOOK = 2
            pieces = [(h, pc) for h in range(8) for pc in range(npiece)]
            loaded = {}

            def load_piece(idx):
                if idx >= len(pieces) or idx in loaded:
                    return
                h, pc = pieces[idx]
                kb0 = pc * 16
                nb = min(16, nkb - kb0)
                sl = kslots[state["ks"] % NKS]
                state["ks"] += 1
                kv_k = sl.ap[:, 0:nb * 128]
                kv_v = sl.ap[:, 2048:2048 + nb * 128].rearrange("p (b e) -> p b e", b=nb)
                P.dma("sp", kv_k, kcache.ap[h, :, kb0 * 128:(kb0 + nb) * 128], [kcache], sl)
                P.dma("sp", kv_v, vcache.ap[h, :, kb0:kb0 + nb, :], [vcache], sl)
                loaded[idx] = (sl, kv_k, kv_v)

            blocks = []
            for idx, (h, pc) in enumerate(pieces):
                kb0 = pc * 16
                for bi in range(min(16, nkb - kb0)):
                    blocks.append((h, idx, bi, kb0 + bi))
            nblk = len(blocks)
            pend = {}

            def emit_S(i):
                h, idx, bi, kb = blocks[i]
                if bi == 0:
                    load_piece(idx)
                if bi == LOOK:
                    load_piece(idx + 1)
                sl, kv_k, kv_v = loaded[idx]
                pS = P.ps()
                mm(pS, pS.ap[:, :], kv_k[:, bi * 128:(bi + 1) * 128], QNT.ap[:, h, :], True, False, [sl, QNT])
                mm(pS, pS.ap[:, :], krT.ap[0:65, kb * 128:(kb + 1) * 128], QRT.ap[0:65, h, :], False, True, [krT, QRT])
                PT = PTs[i % 4]
                act(PT.ap[:], pS.ap[:, :], AF.Exp, [pS], [PT], scale=ATT_SCALE)
                jd = kb - 4 * mt
                if jd >= 0:
                    if jd > 0:
                        memset("pool", PT.ap[:, 0:jd * 128], 0.0, [PT])
                    tt("pool", PT.ap[:, jd * 128:(jd + 1) * 128], PT.ap[:, jd * 128:(jd + 1) * 128], mask_b.ap[:], ALU.mult, [PT, mask_b], [PT])
                pend[i] = PT

            def emit_PV(i):
                h, idx, bi, kb = blocks[i]
                sl, kv_k, kv_v = loaded[idx]
                pO, pD = accs[2 * (h % 2)], accs[2 * (h % 2) + 1]
                PT = pend.pop(i)
                first = kb == 0
                last = kb == nkb - 1
                mm(pO, pO.ap[:, :], kv_v[:, bi, :], PT.ap[:], first, last, [sl, PT])
                onesT = fones_b if kb < 4 * FIRST_OWN else ones_b
                mm(pD, pD.ap[:, :], onesT.ap[:], PT.ap[:], first, last, [onesT, PT])
                if last:
                    P.op("dve", lambda e, pD=pD: e.reciprocal(out=RDEN.ap[:], in_=pD.ap[:, :]), reads=[pD], writes=[RDEN])
                    tt("dve", OT.ap[:, h, :], pO.ap[:, :], RDEN.ap[:], ALU.mult, [pO, RDEN], [OT])

            for i in range(nblk + LOOK):
                if i < nblk:
                    emit_S(i)
                if i - LOOK >= 0:
                    emit_PV(i - LOOK)
            P.release(got)
            out_proj_residual(lambda k, s: OT.ap[:, k, s * 128:(s + 1) * 128], wb_bo, 2, [OT])

        def load_x(src_ap, reads):
            P.dma("sp", X.ap[:], src_ap.rearrange("(s p) d -> p s d", p=128), reads, X)

        state["abuf"] = {"slabs": [aslabL], "brow": browL, "nrow": nrowL}
        for mt in range(FIRST_OWN):
            load_x(xs[mt * T:(mt + 1) * T, :], [])
            if deferred:
                ada_group(*deferred.pop(0))
            mixer0(mt, state_only=True)
            if deferred:
                ada_group(*deferred.pop(0))
        while deferred:
            ada_group(*deferred.pop(0))
        ada_finish(4, 10, ((2, 4, 5), (3, 6, 7), (4, 8, 9)))
        P.carry(LATE, M0)
        ts("dve", Cf.ap[:], Cf.ap[:], flag.ap[:, 0:1], None, ALU.mult, None, [Cf, flag], [Cf])
        ts("dve", mst.ap[:], mst.ap[:], flag.ap[0:8, 0:1], None, ALU.mult, None, [mst, flag], [mst])

        prev = M0
        for mt in range(FIRST_OWN, NT):
            load_x(xs[mt * T:(mt + 1) * T, :], [])
            if prev is not M0:
                P.carry(prev, M0)
            mixer0(mt)
            P.carry(M0, MLPB)
            mlp(0, 1, 1, hook=lambda: rope_tables(mt))
            o0 = (mt - FIRST_OWN) * T
            P.dma("act", xmid.ap[o0:o0 + T, :].rearrange("(s p) d -> p s d", p=128), X.ap[:], [X], xmid)
            P.carry(MLPB, KVB)
            kv_phase(mt)
            prev = KVB

        for mt in range(FIRST_OWN):
            P.dma("sp", krT.ap[0:64, mt * T:(mt + 1) * T], lat_all[mt].ap[0:64, 1024:1536], [lat_all[mt]], krT)
            P.dma("sp", CKT.ap[:], lat_all[mt].ap[0:128, 0:1024].rearrange("p (c t) -> p c t", c=2), [lat_all[mt]], CKT)
            ts("dve", CKT.ap[:], CKT.ap[:], flag.ap[:, 0:1], None, ALU.mult, None, [CKT, flag], [CKT])
            expand_kv(mt)

        rope_tables(FIRST_OWN)
        prev = KVB
        for mt in range(FIRST_OWN, NT):
            o0 = (mt - FIRST_OWN) * T
            load_x(xmid.ap[o0:o0 + T, :], [xmid])
            P.carry(prev, ATTB)
            mixer1(mt)
            P.carry(ATTB, MLPB)
            mlp(1, 3, 3, hook=(lambda: rope_tables(mt + 1)) if mt + 1 < NT else None)
            prev = MLPB
            P.dma("act", yout[o0:o0 + T, :].rearrange("(s p) d -> p s d", p=128), X.ap[:], [X], yout_b)
    try:
        body()
    except _Stop:
        P.dma("pool", yout[0:T, :].rearrange("(s p) d -> p s d", p=128), X.ap[:], [X], yout_b)
    P.final_wait("pool", [yout_b])
    allb = [Buf("fin")]
    block = es.enter_context(nc.Block())

    @block.tensor
    def _(e):
        P.replay("pe", e)

    @block.scalar
    def _(e):
        P.replay("act", e)

    @block.vector
    def _(e):
        P.replay("dve", e)

    @block.gpsimd
    def _(e):
        P.replay("pool", e)

    @block.sync
    def _(e):
        P.replay("sp", e)


_CACHE = {}


def _prep_inputs(x, c, positions, ada_w, ada_b, norm_g, a_w_in, a_gate_b, a_head_g, a_w_out,
                 kv_ada_w, kv_ada_b, kv_norm_g, kv_w_a, kv_latent_g, kv_w_b, b_w_q_a, b_q_latent_g,
                 b_w_q_b, b_w_out, mlp_w1, mlp_w2):
    f = np.float32

    def pk(v):
        return np.ascontiguousarray(np.asarray(v, f).reshape(8, 128).T)

    ident = np.eye(128, dtype=f)
    mask01 = np.triu(np.ones((128, 128), f))
    half = 32
    inv = (10000.0 ** (-np.arange(half, dtype=f) / half)).astype(f)
    inv_col = np.concatenate([inv, inv]).reshape(64, 1).astype(f)
    ada_b = np.asarray(ada_b, f)
    norm_g = np.asarray(norm_g, f)
    ada_bA = np.stack([np.stack([pk(ada_b[l, v * 1024:(v + 1) * 1024]) for v in (0, 1, 3, 4)]) for l in range(2)])
    ada_bG = np.stack([np.stack([ada_b[l, v * 1024:(v + 1) * 1024].reshape(1, 1024) for v in (2, 5)]) for l in range(2)])
    normA = np.stack([np.stack([pk(norm_g[l, v]) for v in (0, 2)]) for l in range(2)])
    normG = np.stack([np.stack([norm_g[l, v].reshape(1, 1024) for v in (1, 3)]) for l in range(2)])
    kv_ada_b = np.asarray(kv_ada_b, f)
    shared = {
        "ident": ident, "mask01": mask01, "inv_col": inv_col,
        "ada_w": np.ascontiguousarray(ada_w, f), "ada_bA": np.ascontiguousarray(ada_bA), "ada_bG": np.ascontiguousarray(ada_bG),
        "normA": np.ascontiguousarray(normA), "normG": np.ascontiguousarray(normG),
        "kv_ada_w": np.ascontiguousarray(kv_ada_w, f),
        "kv_ada_bA": np.ascontiguousarray(np.stack([pk(kv_ada_b[0:1024]), pk(kv_ada_b[1024:2048])])),
        "kv_normA": pk(kv_norm_g),
        "a_w_in": np.ascontiguousarray(a_w_in[0], f),
        "gate_b": np.ascontiguousarray(np.asarray(a_gate_b[0], f).reshape(2, 1, 8)),
        "head_g": np.ascontiguousarray(np.asarray(a_head_g[0], f).reshape(1, 1024)),
        "a_w_out": np.ascontiguousarray(a_w_out[0], f),
        "kv_w_a": np.ascontiguousarray(kv_w_a, f),
        "kv_lat_g": np.ascontiguousarray(np.asarray(kv_latent_g, f).reshape(1, 256)),
        "kv_w_b": np.ascontiguousarray(kv_w_b, f),
        "w_q_a": np.ascontiguousarray(b_w_q_a[0], f),
        "q_lat_g": np.ascontiguousarray(np.asarray(b_q_latent_g[0], f).reshape(1, 384)),
        "w_q_b": np.ascontiguousarray(b_w_q_b[0], f),
        "b_w_out": np.ascontiguousarray(b_w_out[0], f),
        "mlp_w1": np.ascontiguousarray(mlp_w1, f),
        "mlp_w2": np.ascontiguousarray(mlp_w2, f),
    }
    x = np.asarray(x, f)
    positions = np.asarray(positions, np.int32)
    in_maps = []
    for core in range(8):
        b, hf = core // 2, core % 2
        if hf == 1:
            xs = x[b]
            ps = positions[b]
        else:
            xs = np.concatenate([np.zeros((SEQ // 2, D), f), x[b, :SEQ // 2]], axis=0)
            ps = np.concatenate([np.zeros((SEQ // 2,), np.int32), positions[b, :SEQ // 2]])
        m = dict(shared)
        m["xs"] = np.ascontiguousarray(xs)
        m["pos"] = np.ascontiguousarray(ps.reshape(1, SEQ))
        m["flag"] = np.full((128, 1), float(hf), f)
        m["cT"] = pk(np.asarray(c, f)[b])
        in_maps.append(m)
    return in_maps


def kernel(**inputs):
    in_maps = _prep_inputs(**inputs)
    if "nc" not in _CACHE:
        _CACHE["nc"] = build_program(NT)
    nc = _CACHE["nc"]
    res = run_bass_kernel_spmd(nc, in_maps, core_ids=list(range(8)))
    out = np.zeros((4, SEQ, D), np.float32)
    for core in range(8):
        b, hf = core // 2, core % 2
        out[b, hf * (SEQ // 2):(hf + 1) * (SEQ // 2)] = res.results[core]["y"]
    return out
```
